# Optimizing a Trainium2 kernel written in Bass

```python
import math
import jax
import jax.numpy as jnp
from jax import lax
import numpy as np

D_MODEL = 2048
BATCH = 16
SEQ = 256
DEPTH = 2
DEC_BATCH = 8
DEC_SEQ = 4096
PAST_LEN = 256

GRID_W = 64
N_BRANCH = 3
BRANCH_W = D_MODEL // 4
DA_HEAD_DIM = 128
DA_HALF = DA_HEAD_DIM // 2
DA_HEADS = BRANCH_W // DA_HEAD_DIM
S5_GROUP = 16
S5_GROUPS = BRANCH_W // S5_GROUP
S5_STATE = 64
RW_HEAD = 64
RW_HEADS = BRANCH_W // RW_HEAD
RW_DECAY_LORA = 64
RW_A_LORA = 64
RW_GATE_LORA = 128
RW_COLS = 3 * BRANCH_W + 2 * RW_DECAY_LORA + 2 * RW_A_LORA + RW_GATE_LORA
OFF_DA = 0
OFF_S5 = 3 * BRANCH_W
OFF_RW = 4 * BRANCH_W
OFF_GATE = OFF_RW + RW_COLS
N_IN = OFF_GATE + N_BRANCH * D_MODEL
D_FF = -(-(8 * D_MODEL) // (3 * 256)) * 256
ROPE_THETA = 10000.0
Q_BLOCK = 128
RMS_EPS = 1e-6
GN_EPS = 64e-5
L2_EPS = 1e-12

kernel_name = 'hybrid_diffattn_s5_rwkv7_flow_step'


def rmsnorm(x, w):
    xf = x.astype(jnp.float32)
    y = xf * lax.rsqrt(jnp.mean(xf * xf, axis=-1, keepdims=True) + RMS_EPS)
    return (y * w.astype(jnp.float32)).astype(x.dtype)


def axial_rope(x):
    T = x.shape[1]
    rows = T // GRID_W
    row = jnp.repeat(jnp.arange(rows, dtype=jnp.float32), GRID_W)
    col = jnp.tile(jnp.arange(GRID_W, dtype=jnp.float32), rows)
    n_freq = DA_HALF // 4
    inv_freq = ROPE_THETA ** (-jnp.arange(n_freq, dtype=jnp.float32) / n_freq)

    def rotate(xa, pos):
        ang = pos[:, None] * inv_freq[None, :]
        cos = jnp.cos(ang)[None, :, None, None, :]
        sin = jnp.sin(ang)[None, :, None, None, :]
        x1, x2 = xa[..., :n_freq], xa[..., n_freq:]
        return jnp.concatenate([x1 * cos - x2 * sin, x1 * sin + x2 * cos], axis=-1)

    xf = x.astype(jnp.float32)
    half = DA_HALF // 2
    out = jnp.concatenate([rotate(xf[..., :half], row), rotate(xf[..., half:], col)], axis=-1)
    return out.astype(x.dtype)


def diff_attention(q, k, v, lam):
    B, Tq = q.shape[0], q.shape[1]
    nb = Tq // Q_BLOCK
    qb = q.reshape(B, nb, Q_BLOCK, DA_HEADS, 2, DA_HALF).transpose(1, 0, 2, 3, 4, 5)
    scale = DA_HALF ** -0.5

    def block(q_blk):
        s = jnp.einsum('bqhcd,bkhcd->bhcqk', q_blk, k, preferred_element_type=jnp.float32) * scale
        p = jax.nn.softmax(s, axis=-1)
        p = p[:, :, 0] - lam * p[:, :, 1]
        return jnp.einsum('bhqk,bkhe->bqhe', p.astype(v.dtype), v, preferred_element_type=jnp.float32)

    o = lax.map(block, qb)
    return o.transpose(1, 0, 2, 3, 4).reshape(B, Tq, DA_HEADS, DA_HEAD_DIM).astype(q.dtype)


def centred_shift(z):
    zp = jnp.pad(z, ((0, 0), (1, 1), (0, 0)))
    return 0.5 * (zp[:, :-2] + zp[:, 2:])


def complex_affine_combine(e1, e2):
    a1r, a1i, b1r, b1i = e1
    a2r, a2i, b2r, b2i = e2
    return (a2r * a1r - a2i * a1i,
            a2r * a1i + a2i * a1r,
            a2r * b1r - a2i * b1i + b2r,
            a2r * b1i + a2i * b1r + b2i)


def s5_scan(bu_r, bu_i, lb_r, lb_i, h0_r, h0_i, reverse):
    T = bu_r.shape[1]
    first = T - 1 if reverse else 0
    bu_r = bu_r.at[:, first].add(lb_r * h0_r - lb_i * h0_i)
    bu_i = bu_i.at[:, first].add(lb_r * h0_i + lb_i * h0_r)
    a_r = jnp.broadcast_to(lb_r, (1, T) + lb_r.shape)
    a_i = jnp.broadcast_to(lb_i, (1, T) + lb_i.shape)
    _, _, h_r, h_i = lax.associative_scan(complex_affine_combine, (a_r, a_i, bu_r, bu_i),
                                          reverse=reverse, axis=1)
    return h_r, h_i


def s5_branch(u, p, h0):
    f32 = jnp.float32
    B, T = u.shape[0], u.shape[1]
    uf = u.astype(f32)
    ug = uf.reshape(B, T, S5_GROUPS, S5_GROUP)
    y = uf * p['s5_d'].astype(f32)
    finals = []
    for d in range(2):
        lr = p['s5_lam_re'][d].astype(f32)
        li = p['s5_lam_im'][d].astype(f32)
        dt = jnp.exp(p['s5_log_step'][d].astype(f32))[:, None]
        mag = jnp.exp(lr * dt)
        lb_r = mag * jnp.cos(li * dt)
        lb_i = mag * jnp.sin(li * dt)
        den = lr * lr + li * li
        coef_r = ((lb_r - 1.0) * lr + lb_i * li) / den
        coef_i = (lb_i * lr - (lb_r - 1.0) * li) / den
        b_r = p['s5_b_re'][d].astype(f32)
        b_i = p['s5_b_im'][d].astype(f32)
        bb_r = coef_r[..., None] * b_r - coef_i[..., None] * b_i
        bb_i = coef_r[..., None] * b_i + coef_i[..., None] * b_r
        bu_r = jnp.einsum('btgc,gnc->btgn', ug, bb_r)
        bu_i = jnp.einsum('btgc,gnc->btgn', ug, bb_i)
        h_r, h_i = s5_scan(bu_r, bu_i, lb_r, lb_i, h0[:, d, 0].astype(f32), h0[:, d, 1].astype(f32),
                           reverse=(d == 1))
        y_d = (jnp.einsum('btgn,gcn->btgc', h_r, p['s5_c_re'][d].astype(f32))
               - jnp.einsum('btgn,gcn->btgc', h_i, p['s5_c_im'][d].astype(f32)))
        y = y + y_d.reshape(B, T, BRANCH_W)
        idx = T - 1 if d == 0 else 0
        finals.append(jnp.stack([h_r[:, idx], h_i[:, idx]], axis=1))
    y = jax.nn.gelu(y)
    y = y * jax.nn.sigmoid(y @ p['s5_w_glu'].astype(f32))
    return y.astype(u.dtype), jnp.stack(finals, axis=1)


def rwkv_scan(s0, r, w, k, v, a, b, reverse):
    xs = tuple(t.transpose(1, 0, 2, 3) for t in (r, w, k, v, a, b))

    def step(S, inp):
        r_t, w_t, k_t, v_t, a_t, b_t = inp
        sa = jnp.einsum('bhvk,bhk->bhv', S, a_t)
        S = S * w_t[:, :, None, :] + sa[..., None] * b_t[:, :, None, :] + v_t[..., None] * k_t[:, :, None, :]
        return S, jnp.einsum('bhvk,bhk->bhv', S, r_t)

    s_fin, ys = lax.scan(step, s0, xs, reverse=reverse)
    return s_fin, ys.transpose(1, 0, 2, 3)


def rwkv_branch(z, p, s0):
    f32 = jnp.float32
    B, T = z.shape[0], z.shape[1]
    zf = z.astype(f32)
    zf = zf + p['rw_mu'].astype(f32) * (centred_shift(zf) - zf)
    r = zf[..., :BRANCH_W]
    k = zf[..., BRANCH_W:2 * BRANCH_W]
    v = zf[..., 2 * BRANCH_W:3 * BRANCH_W]
    o = 3 * BRANCH_W
    hw = zf[..., o:o + 2 * RW_DECAY_LORA].reshape(B, T, 2, RW_DECAY_LORA)
    o += 2 * RW_DECAY_LORA
    ha = zf[..., o:o + 2 * RW_A_LORA].reshape(B, T, 2, RW_A_LORA)
    o += 2 * RW_A_LORA
    hg = zf[..., o:o + RW_GATE_LORA]
    g = jax.nn.sigmoid(hg) @ p['rw_g2'].astype(f32)

    def heads(t):
        return t.reshape(B, T, RW_HEADS, RW_HEAD)

    kk = heads(k * p['rw_k_k'].astype(f32))
    kk = kk * lax.rsqrt(jnp.sum(kk * kk, axis=-1, keepdims=True) + L2_EPS)
    y = jnp.zeros((B, T, RW_HEADS, RW_HEAD), f32)
    finals = []
    for d in range(2):
        w_raw = p['rw_w0'][d].astype(f32) + jnp.tanh(hw[:, :, d]) @ p['rw_w2'][d].astype(f32)
        decay = jnp.exp(-jnp.exp(-jax.nn.softplus(-w_raw) - 0.5))
        a = jax.nn.sigmoid(p['rw_a0'][d].astype(f32) + ha[:, :, d] @ p['rw_a2'][d].astype(f32))
        k_d = k * (1.0 + (a - 1.0) * p['rw_k_a'].astype(f32))
        s_fin, y_d = rwkv_scan(s0[:, d].astype(f32), heads(r), heads(decay), heads(k_d), heads(v),
                               -kk, kk * heads(a), reverse=(d == 1))
        y = y + y_d
        finals.append(s_fin)
    mean = jnp.mean(y, axis=-1, keepdims=True)
    var = jnp.mean(jnp.square(y - mean), axis=-1, keepdims=True)
    y = ((y - mean) * lax.rsqrt(var + GN_EPS)).reshape(B, T, BRANCH_W)
    y = y * p['rw_ln_w'].astype(f32) + p['rw_ln_b'].astype(f32)
    bonus = jnp.sum(heads(r) * heads(k) * p['rw_r_k'].astype(f32), axis=-1, keepdims=True) * heads(v)
    y = (y + bonus.reshape(B, T, BRANCH_W)) * g
    return y.astype(z.dtype), jnp.stack(finals, axis=1)


def trunk_layer(x, mod, p, layer_idx, ctx_kv, s5_h0, rw_s0):
    B, T = x.shape[0], x.shape[1]
    sh1, sc1, gt1, sh2, sc2, gt2 = jnp.split(mod[:, None, :], 6, axis=-1)
    h = rmsnorm(x, p['norm_mix']) * (1.0 + sc1) + sh1
    proj = h @ p['w_in']

    q = proj[..., OFF_DA:OFF_DA + BRANCH_W].reshape(B, T, DA_HEADS, 2, DA_HALF)
    k = proj[..., OFF_DA + BRANCH_W:OFF_DA + 2 * BRANCH_W].reshape(B, T, DA_HEADS, 2, DA_HALF)
    v = proj[..., OFF_DA + 2 * BRANCH_W:OFF_DA + 3 * BRANCH_W].reshape(B, T, DA_HEADS, DA_HEAD_DIM)
    q = rmsnorm(q, p['da_q_norm'])
    k = rmsnorm(k, p['da_k_norm'])
    if ctx_kv is None:
        k_all, v_all = k, v
        new_kv = jnp.stack([k.reshape(B, T, DA_HEADS, DA_HEAD_DIM), v], axis=2)
    else:
        t_ctx = ctx_kv.shape[1]
        ck = ctx_kv[:, :, 0].reshape(B, t_ctx, DA_HEADS, 2, DA_HALF).astype(k.dtype)
        cv = ctx_kv[:, :, 1].astype(v.dtype)
        q = axial_rope(q)
        k_all = jnp.concatenate([ck, axial_rope(k)], axis=1)
        v_all = jnp.concatenate([cv, v], axis=1)
        new_kv = None
    lam_p = p['da_lambda'].astype(jnp.float32)
    lam_init = 0.8 - 0.6 * math.exp(-0.3 * layer_idx)
    lam = jnp.exp(jnp.sum(lam_p[0] * lam_p[1])) - jnp.exp(jnp.sum(lam_p[2] * lam_p[3])) + lam_init
    o_da = diff_attention(q, k_all, v_all, lam)
    o_da = (rmsnorm(o_da, p['da_out_norm']) * (1.0 - lam_init)).reshape(B, T, BRANCH_W)

    o_s5, s5_state = s5_branch(proj[..., OFF_S5:OFF_S5 + BRANCH_W], p, s5_h0)

    o_rw, rw_state = rwkv_branch(proj[..., OFF_RW:OFF_RW + RW_COLS], p, rw_s0)

    merged = jnp.zeros_like(x)
    for n, o_n in enumerate((o_da, o_s5, o_rw)):
        gate_n = jax.nn.sigmoid(proj[..., OFF_GATE + n * D_MODEL:OFF_GATE + (n + 1) * D_MODEL])
        merged = merged + gate_n * (o_n @ p['w_up'][n])
    x = x + gt1 * (merged @ p['w_out'])

    h2 = rmsnorm(x, p['norm_ffn']) * (1.0 + sc2) + sh2
    gu = h2 @ p['w_ffn_in']
    x = x + gt2 * ((jax.nn.silu(gu[..., :D_FF]) * gu[..., D_FF:]) @ p['w_ffn_out'])
    return x, new_kv, s5_state, rw_state


def setup_inputs(seed: int = 0) -> dict:
    key = jax.random.key(seed)
    ks = jax.random.split(key, 48)
    f32 = jnp.float32

    def nrm(i, shape, s=1.0):
        return jax.random.normal(ks[i], shape, f32) * s

    def near_one(i, shape, s=0.02):
        return 1.0 + nrm(i, shape, s)

    lam_im = jnp.pi * jnp.arange(S5_STATE, dtype=f32) + nrm(20, (DEPTH, 2, S5_GROUPS, S5_STATE), 0.01)
    return {
        'x_prompt': nrm(0, (BATCH, SEQ, D_MODEL)),
        'x_sample': nrm(1, (DEC_BATCH, DEC_SEQ, D_MODEL)),
        'cache_attn_kv': nrm(2, (DEC_BATCH, DEPTH, PAST_LEN, 2, DA_HEADS, DA_HEAD_DIM)),
        'state_s5': nrm(3, (DEC_BATCH, DEPTH, 2, 2, S5_GROUPS, S5_STATE), 0.1),
        'state_rwkv': nrm(4, (DEC_BATCH, DEPTH, 2, RW_HEADS, RW_HEAD, RW_HEAD), 0.5),
        'c': nrm(5, (DEC_BATCH, D_MODEL)),
        'c_ctx': nrm(6, (D_MODEL,)),
        'w_mod': nrm(7, (DEPTH, D_MODEL, 6 * D_MODEL), 0.5 * D_MODEL ** -0.5),
        'b_mod': nrm(8, (DEPTH, 6 * D_MODEL), 0.02),
        'norm_mix': near_one(9, (DEPTH, D_MODEL)),
        'norm_ffn': near_one(10, (DEPTH, D_MODEL)),
        'w_in': nrm(11, (DEPTH, D_MODEL, N_IN), D_MODEL ** -0.5),
        'da_q_norm': near_one(12, (DEPTH, DA_HALF)),
        'da_k_norm': near_one(13, (DEPTH, DA_HALF)),
        'da_lambda': nrm(14, (DEPTH, 4, DA_HALF), 0.1),
        'da_out_norm': near_one(15, (DEPTH, DA_HEAD_DIM)),
        's5_lam_re': -0.5 + nrm(16, (DEPTH, 2, S5_GROUPS, S5_STATE), 0.01),
        's5_lam_im': lam_im,
        's5_log_step': jax.random.uniform(ks[17], (DEPTH, 2, S5_GROUPS), f32,
                                          minval=math.log(1e-3), maxval=math.log(1e-1)),
        's5_b_re': nrm(18, (DEPTH, 2, S5_GROUPS, S5_STATE, S5_GROUP), (2 * S5_GROUP) ** -0.5),
        's5_b_im': nrm(19, (DEPTH, 2, S5_GROUPS, S5_STATE, S5_GROUP), (2 * S5_GROUP) ** -0.5),
        's5_c_re': nrm(21, (DEPTH, 2, S5_GROUPS, S5_GROUP, S5_STATE), (2 * S5_STATE) ** -0.5),
        's5_c_im': nrm(22, (DEPTH, 2, S5_GROUPS, S5_GROUP, S5_STATE), (2 * S5_STATE) ** -0.5),
        's5_d': nrm(23, (DEPTH, BRANCH_W)),
        's5_w_glu': nrm(24, (DEPTH, BRANCH_W, BRANCH_W), BRANCH_W ** -0.5),
        'rw_mu': jax.random.uniform(ks[25], (DEPTH, RW_COLS), f32),
        'rw_w0': jax.random.uniform(ks[26], (DEPTH, 2, BRANCH_W), f32, minval=-6.0, maxval=1.0),
        'rw_w2': nrm(27, (DEPTH, 2, RW_DECAY_LORA, BRANCH_W), 0.1),
        'rw_a0': nrm(28, (DEPTH, 2, BRANCH_W), 0.1),
        'rw_a2': nrm(29, (DEPTH, 2, RW_A_LORA, BRANCH_W), 0.1),
        'rw_g2': nrm(30, (DEPTH, RW_GATE_LORA, BRANCH_W), RW_GATE_LORA ** -0.5),
        'rw_k_k': 0.85 + nrm(31, (DEPTH, BRANCH_W), 0.02),
        'rw_k_a': near_one(32, (DEPTH, BRANCH_W)),
        'rw_r_k': nrm(33, (DEPTH, RW_HEADS, RW_HEAD), 0.1),
        'rw_ln_w': near_one(34, (DEPTH, BRANCH_W)),
        'rw_ln_b': nrm(35, (DEPTH, BRANCH_W), 0.02),
        'w_up': nrm(36, (DEPTH, N_BRANCH, BRANCH_W, D_MODEL), BRANCH_W ** -0.5),
        'w_out': nrm(37, (DEPTH, D_MODEL, D_MODEL), D_MODEL ** -0.5),
        'w_ffn_in': nrm(38, (DEPTH, D_MODEL, 2 * D_FF), D_MODEL ** -0.5),
        'w_ffn_out': nrm(39, (DEPTH, D_FF, D_MODEL), D_FF ** -0.5),
    }


def reference(x_prompt, x_sample, cache_attn_kv, state_s5, state_rwkv, c, c_ctx,
              w_mod, b_mod, norm_mix, norm_ffn, w_in,
              da_q_norm, da_k_norm, da_lambda, da_out_norm,
              s5_lam_re, s5_lam_im, s5_log_step, s5_b_re, s5_b_im, s5_c_re, s5_c_im, s5_d, s5_w_glu,
              rw_mu, rw_w0, rw_w2, rw_a0, rw_a2, rw_g2, rw_k_k, rw_k_a, rw_r_k, rw_ln_w, rw_ln_b,
              w_up, w_out, w_ffn_in, w_ffn_out):
    params = []
    for l in range(DEPTH):
        params.append({
            'norm_mix': norm_mix[l], 'norm_ffn': norm_ffn[l], 'w_in': w_in[l],
            'da_q_norm': da_q_norm[l], 'da_k_norm': da_k_norm[l], 'da_lambda': da_lambda[l],
            'da_out_norm': da_out_norm[l],
            's5_lam_re': s5_lam_re[l], 's5_lam_im': s5_lam_im[l], 's5_log_step': s5_log_step[l],
            's5_b_re': s5_b_re[l], 's5_b_im': s5_b_im[l], 's5_c_re': s5_c_re[l], 's5_c_im': s5_c_im[l],
            's5_d': s5_d[l], 's5_w_glu': s5_w_glu[l],
            'rw_mu': rw_mu[l], 'rw_w0': rw_w0[l], 'rw_w2': rw_w2[l], 'rw_a0': rw_a0[l], 'rw_a2': rw_a2[l],
            'rw_g2': rw_g2[l], 'rw_k_k': rw_k_k[l], 'rw_k_a': rw_k_a[l], 'rw_r_k': rw_r_k[l],
            'rw_ln_w': rw_ln_w[l], 'rw_ln_b': rw_ln_b[l],
            'w_up': w_up[l], 'w_out': w_out[l], 'w_ffn_in': w_ffn_in[l], 'w_ffn_out': w_ffn_out[l],
        })

    b_p = x_prompt.shape[0]
    zeros_s5 = jnp.zeros((b_p, 2, 2, S5_GROUPS, S5_STATE), jnp.float32)
    zeros_rw = jnp.zeros((b_p, 2, RW_HEADS, RW_HEAD, RW_HEAD), jnp.float32)
    silu_ctx = jax.nn.silu(c_ctx)[None, :]
    y_prompt = x_prompt
    kv_list, s5_list, rw_list = [], [], []
    for l in range(DEPTH):
        mod = silu_ctx @ w_mod[l] + b_mod[l]
        y_prompt, kv_l, s5_l, rw_l = trunk_layer(y_prompt, mod, params[l], l, None, zeros_s5, zeros_rw)
        kv_list.append(kv_l)
        s5_list.append(s5_l)
        rw_list.append(rw_l)
    new_attn_kv = jnp.stack(kv_list, axis=1)
    new_s5 = jnp.stack(s5_list, axis=1)
    new_rwkv = jnp.stack(rw_list, axis=1)

    silu_c = jax.nn.silu(c)
    y_sample = x_sample
    for l in range(DEPTH):
        mod = silu_c @ w_mod[l] + b_mod[l]
        y_sample, _, _, _ = trunk_layer(y_sample, mod, params[l], l,
                                        cache_attn_kv[:, l], state_s5[:, l], state_rwkv[:, l])

    return (y_prompt, y_sample, new_attn_kv, new_s5, new_rwkv)
```

```python
import math
from contextlib import ExitStack
from functools import partial

import numpy as np
import concourse.bass as bass
import concourse.mybir as mybir
from concourse.bass_utils import run_bass_kernel_spmd

F32 = mybir.dt.float32
BF16 = mybir.dt.bfloat16
AF = mybir.ActivationFunctionType
ALU = mybir.AluOpType
AX = mybir.AxisListType

D_MODEL = 2048
DEPTH = 2
BW = 512
N_IN = 10112
D_FF = 5632
OFF_S5 = 1536
OFF_RW = 2048
OFF_GATE = 3968
RW_COLS = 1920
RMS_EPS = 1e-6
GN_EPS = 64e-5
L2_EPS = 1e-12
NJ = D_MODEL // 128
SAME_ENGINE_SYNC = True


class Buf:
    __slots__ = ("name", "last_w", "readers", "excl")

    def __init__(self, name, excl=False):
        self.name = name
        self.last_w = None
        self.readers = []
        self.excl = excl


class Op:
    __slots__ = ("eng", "fn", "deps", "need_inc", "semkey", "val", "dma", "idx", "barrier")


class Prog:
    ENGS = ("pe", "act", "dve", "pool", "sp")

    def __init__(self, nc, n_dma_sems=8, same_engine_sync=True):
        self.nc = nc
        self.ops = []
        self.same = same_engine_sync
        self.n_dma_sems = n_dma_sems
        self.eng_obj = {"pe": nc.tensor, "act": nc.scalar, "dve": nc.vector,
                        "pool": nc.gpsimd, "sp": nc.sync}
        self.last_op = {}

    def add(self, eng, fn, reads=(), writes=(), dma=False):
        op = Op()
        op.eng = eng
        op.fn = fn
        op.dma = dma
        op.need_inc = False
        op.val = None
        op.semkey = None
        op.barrier = False
        op.idx = len(self.ops)
        ex = [b for b in reads if b.excl]
        if ex:
            reads = [b for b in reads if not b.excl]
            writes = list(writes) + ex
        deps = {}
        for b in reads:
            if b.last_w is not None:
                deps[b.last_w.idx] = b.last_w
        for b in writes:
            if b.last_w is not None:
                deps[b.last_w.idx] = b.last_w
            for r in b.readers:
                deps[r.idx] = r
        dl = []
        for d in deps.values():
            if d.eng == eng and not d.dma:
                if eng == "pe" or not self.same:
                    continue
            d.need_inc = True
            dl.append(d)
        op.deps = dl
        for b in reads:
            b.readers.append(op)
        for b in writes:
            b.last_w = op
            b.readers = []
        self.ops.append(op)
        if not dma:
            self.last_op[eng] = op
        return op

    def barrier(self):
        for e, o in self.last_op.items():
            o.need_inc = True
        op = Op()
        op.barrier = True
        op.idx = len(self.ops)
        op.deps = []
        op.dma = False
        op.eng = None
        op.need_inc = False
        op.val = None
        op.semkey = None
        self.ops.append(op)

    def emit(self):
        nc = self.nc
        sems = {}

        def getsem(key):
            if key not in sems:
                c = nc.semaphore("s_" + "_".join(str(k) for k in key))
                sems[key] = c.__enter__()
            return sems[key]

        cnt = {}
        seen = {e: {} for e in self.ENGS}
        dma_n = {e: 0 for e in self.ENGS}
        dma_cnt = {}
        for op in self.ops:
            if op.barrier:
                tgt = {}
                for k, v in cnt.items():
                    tgt[k] = v
                for k, n in dma_cnt.items():
                    tgt[k] = 16 * n
                for e in self.ENGS:
                    eo = self.eng_obj[e]
                    for k, v in tgt.items():
                        if k == ("e", e):
                            continue
                        if seen[e].get(k, 0) >= v:
                            continue
                        eo.wait_ge(getsem(k), v)
                        seen[e][k] = v
                continue
            e = op.eng
            eo = self.eng_obj[e]
            need = {}
            for d in op.deps:
                assert d.val is not None, "dep not emitted"
                if need.get(d.semkey, 0) < d.val:
                    need[d.semkey] = d.val
            if op.dma:
                slot = dma_n[e] % self.n_dma_sems
                dma_n[e] += 1
                dkey = ("d", e, slot)
                prev = dma_cnt.get(dkey, 0)
                if prev > 0 and need.get(dkey, 0) < 16 * prev:
                    need[dkey] = 16 * prev
            for key, v in need.items():
                if seen[e].get(key, 0) >= v:
                    continue
                eo.wait_ge(getsem(key), v)
                seen[e][key] = v
            ins = op.fn()
            if op.dma:
                dma_cnt[dkey] = prev + 1
                op.semkey = dkey
                op.val = 16 * (prev + 1)
                ins.then_inc(getsem(dkey), 16)
            elif op.need_inc:
                key = ("e", e)
                cnt[key] = cnt.get(key, 0) + 1
                op.semkey = key
                op.val = cnt[key]
                ins.then_inc(getsem(key), 1)
        eo = self.eng_obj["sp"]
        for key, n in dma_cnt.items():
            eo.wait_ge(getsem(key), 16 * n)
        return len(self.ops)


class Tl:
    __slots__ = ("t", "b")

    def __init__(self, t, name, excl=False):
        self.t = t
        self.b = Buf(name, excl)


class Builder:
    def __init__(self, cfg):
        self.cfg = cfg
        self.nc = bass.Bass("TRN2", target_bir_lowering=False)
        self.P = Prog(self.nc, same_engine_sync=cfg.get("same", SAME_ENGINE_SYNC))
        self.es = ExitStack()
        self.pes = None
        self.uid = 0
        self.dbufs = {}
        self.outs = []
        nc = self.nc
        self.seqs = cfg["seqs"]
        self.NT = sum(s[2] for s in self.seqs)
        assert self.NT % 512 == 0
        self.NCH = self.NT // 512
        self.chunk_row = []
        self.chunk_rope = []
        for ci in range(self.NCH):
            t0 = ci * 512
            for (kind, off, T) in self.seqs:
                if off <= t0 < off + T:
                    self.chunk_row.append(0 if kind == "S" else 1)
                    self.chunk_rope.append((t0 - off) if kind == "S" else None)
        self.L = cfg["layers"]
        self.ps = [Tl(self.es.enter_context(nc.psum_tensor("ps%d" % i, [128, 512], F32)), "ps%d" % i, True)
                   for i in range(8)]

    def dram_in(self, name, shape, dt=F32):
        t = self.nc.dram_tensor(name, list(shape), dt, kind="ExternalInput")
        self.dbufs[name] = Buf(name)
        return t.ap()

    def dram_out(self, name, shape, dt=F32):
        t = self.nc.dram_tensor(name, list(shape), dt, kind="ExternalOutput")
        self.dbufs[name] = Buf(name)
        self.outs.append(name)
        return t.ap()

    def dram_tmp(self, name, shape, dt=F32):
        t = self.nc.dram_tensor(name, list(shape), dt, kind="Internal")
        return t.ap()

    def sb(self, name, shape, dt=F32, persistent=False):
        self.uid += 1
        nm = "%s_%d" % (name, self.uid)
        st = self.es if persistent else self.pes
        t = st.enter_context(self.nc.sbuf_tensor(nm, list(shape), dt))
        return Tl(t, nm)

    def phase_begin(self):
        self.pes = ExitStack()

    def phase_end(self):
        self.P.barrier()
        self.pes.close()
        self.pes = None

    def dma(self, out_ap, in_ap, reads=(), writes=(), q="sp", slow=False):
        nc = self.nc
        eng = {"sp": nc.sync, "pool": nc.gpsimd, "act": nc.scalar}[q]
        if slow:
            self.P.add(q, lambda: eng.dma_start(out=out_ap, in_=in_ap, allow_slow_non_contiguous=True),
                       reads=reads, writes=writes, dma=True)
        else:
            self.P.add(q, lambda: eng.dma_start(out=out_ap, in_=in_ap), reads=reads, writes=writes, dma=True)

    def mm(self, out_ap, lhsT, rhs, start, stop, reads, writes):
        nc = self.nc
        self.P.add("pe", lambda: nc.tensor.matmul(out_ap, lhsT=lhsT, rhs=rhs, start=start, stop=stop),
                   reads=reads, writes=writes)

    def tr(self, out_ap, in_ap, ident, reads, writes):
        nc = self.nc
        self.P.add("pe", lambda: nc.tensor.transpose(out=out_ap, in_=in_ap, identity=ident),
                   reads=reads, writes=writes)

    def act(self, out_ap, in_ap, func, reads, writes, bias=None, scale=None):
        nc = self.nc
        kw = {}
        if bias is not None:
            kw["bias"] = bias
        if scale is not None:
            kw["scale"] = scale
        self.P.add("act", lambda: nc.scalar.activation(out=out_ap, in_=in_ap, func=func, **kw),
                   reads=reads, writes=writes)

    def _veng(self, e):
        return self.nc.vector if e == "dve" else self.nc.gpsimd

    def tt(self, e, out_ap, in0, in1, op, reads, writes):
        en = self._veng(e)
        self.P.add(e, lambda: en.tensor_tensor(out=out_ap, in0=in0, in1=in1, op=op), reads=reads, writes=writes)

    def ts(self, e, out_ap, in0, s1, s2, op0, op1, reads, writes):
        en = self._veng(e)
        if op1 is None:
            self.P.add(e, lambda: en.tensor_scalar(out=out_ap, in0=in0, scalar1=s1, scalar2=None, op0=op0),
                       reads=reads, writes=writes)
        else:
            self.P.add(e, lambda: en.tensor_scalar(out=out_ap, in0=in0, scalar1=s1, scalar2=s2, op0=op0, op1=op1),
                       reads=reads, writes=writes)

    def stt(self, out_ap, in0, scalar, in1, op0, op1, reads, writes):
        nc = self.nc
        self.P.add("dve", lambda: nc.vector.scalar_tensor_tensor(out=out_ap, in0=in0, scalar=scalar, in1=in1,
                                                                 op0=op0, op1=op1), reads=reads, writes=writes)

    def cp(self, e, out_ap, in_ap, reads, writes):
        nc = self.nc
        if e == "act":
            self.P.add("act", lambda: nc.scalar.activation(out=out_ap, in_=in_ap, func=AF.Identity), reads=reads, writes=writes)
        else:
            en = self._veng(e)
            self.P.add(e, lambda: en.tensor_copy(out=out_ap, in_=in_ap), reads=reads, writes=writes)

    def recip(self, out_ap, in_ap, reads, writes):
        nc = self.nc
        self.P.add("dve", lambda: nc.vector.reciprocal(out=out_ap, in_=in_ap), reads=reads, writes=writes)

    def memset(self, e, ap, val, writes):
        en = self._veng(e)
        self.P.add(e, lambda: en.memset(ap, val), writes=writes)

    def declare_io(self):
        NT = self.NT
        L = DEPTH
        d = self.dram_in
        self.xL = d("xL", [self.NCH, 128, NJ, 512])
        self.cT = d("cT", [128, NJ, 2])
        self.w_mod = d("w_mod", [L, 24, 128, NJ, 512])
        self.bmodT = d("bmodT", [L, 128, 96])
        self.nmixT = d("nmixT", [L, 128, NJ])
        self.nffnT = d("nffnT", [L, 128, NJ])
        self.w_in = d("w_in", [L, 40, 128, NJ, 256])
        self.qkw = d("qkw", [L, 128, 2])
        self.consts = d("consts", [128, 4 * 128])
        self.ropeC = d("ropeC", [128, 4096])
        self.ropeS = d("ropeS", [128, 4096])
        self.n_prompt = sum(1 for s in self.seqs if s[0] == "P")
        npt = max(256, sum(s_[2] for s_ in self.seqs if s_[0] == "P"))
        self.o_yL = self.dram_out("yL", [self.NCH, 128, NJ, 512])
        self.o_kT = self.dram_out("kT", [L, BW, npt])
        self.o_v = self.dram_out("vtok", [L, npt, BW])
        t = self.dram_tmp
        NCH = self.NCH
        self.Qs = t("Qs", [4, NCH, 128, 512], BF16)
        self.Ks = t("Ks", [4, NCH, 128, 512], BF16)
        self.Vs = t("Vs", [NT, BW], BF16)
        self.Us = t("Us", [4, NCH, 128, 512], F32)
        self.RWs = t("RWs", [15, NCH, 128, 512], F32)
        self.Gs = t("Gs", [48, NCH, 128, 512], BF16)
        for n in ("Qs", "Ks", "Vs", "Us", "RWs", "Gs"):
            self.dbufs[n] = Buf(n)
        self.OUTs = [t("ODA", [4, NCH, 128, 512], BF16), t("OS5", [4, NCH, 128, 512], BF16),
                     t("ORW", [4, NCH, 128, 512], BF16)]
        self.X1s = t("X1s", [NCH, 128, NJ, 512], F32)
        self.H2s = t("H2s", [NCH, 128, NJ, 512], BF16)
        self.Xs = [t("Xs%d" % i, [NCH, 128, NJ, 512], F32) for i in range(max(1, self.L - 1))]
        self.w_up = d("w_up", [L, 16, 128, 12, 128])
        self.w_out = d("w_out", [L, 16, 128, 16, 128])
        self.w_ffn_in = d("w_ffn_in", [L, 44, 128, 32, 128])
        self.w_ffn_out = d("w_ffn_out", [L, 16, 128, 44, 128])
        self.da_lam = d("da_lam", [L, 1, 256])
        self.da_on = d("da_on", [L, 128, 1])
        self.s5p = d("s5p", [L, 2, 128, 3, 16])
        self.s5b = d("s5b", [L, 2, 128, 2, 16, 16])
        self.s5c = d("s5c", [L, 2, 128, 2, 16, 16])
        self.s5d = d("s5d", [L, 128, 4])
        self.s5glu = d("s5glu", [L, 128, 4, 512])
        self.o_s5f = self.dram_out("s5f", [L, 128, max(1, self.n_prompt), 2, 2, 16])
        self.o_rwf = self.dram_out("rwf", [L, max(1, self.n_prompt), 2, 8, 64, 64])
        self.rw_vec = d("rw_vec", [L, 128, 4, 10])
        self.rw_wa = d("rw_wa", [L, 128, 4, 2, 2])
        self.rw_mu3 = d("rw_mu3", [L, 128, 3])
        self.rw_w2 = d("rw_w2", [L, 128, 512])
        self.rw_a2 = d("rw_a2", [L, 128, 512])
        self.rw_g2 = d("rw_g2", [L, 128, 512])
        self.rw_masks = d("rw_masks", [128, 4, 128])
        self.rw_c2 = d("rw_c2", [128, 2, 128])
        self.has_sample = any(s_[0] == "S" for s_ in self.seqs)
        if self.has_sample:
            self.cacheKT = d("cacheKT", [L, 4, 128, 256])
            self.cacheV = d("cacheV", [L, 2, 128, 512])
            self.s5h0 = d("s5h0", [L, 128, 2, 2, 16])
            self.rwS0 = d("rwS0", [L, 2, 8, 64, 64])

    def load_consts(self):
        cst = self.sb("consts", [128, 512], F32, persistent=True)
        self.dma(cst.t[:], self.consts[:, :], writes=[cst.b])
        self.cst = cst
        self.ident = cst.t[:, 0:128]
        self.ropePT = cst.t[:, 384:512]
        cb = self.sb("constsb", [128, 256], BF16, persistent=True)
        self.cp("dve", cb.t[:], cst.t[:, 128:384], [cst.b], [cb.b])
        self.cstb = cb
        self.bones64 = cb.t[:, 0:128]
        self.ones128 = cb.t[:, 128:256]
        self.onesf = self.sb("onesf", [128, 512], F32, persistent=True)
        self.memset("pool", self.onesf.t[:], 1.0, [self.onesf.b])
        self.onesL = self.sb("onesL", [128, 1024], F32, persistent=True)
        self.memset("pool", self.onesL.t[:], 1.0, [self.onesL.b])

    def phase_mod(self, l):
        self.phase_begin()
        nc = self.nc
        cs = self.sb("cs", [128, NJ, 2])
        self.dma(cs.t[:], self.cT, writes=[cs.b])
        sc = self.sb("silu_c", [128, NJ, 2])
        self.act(sc.t[:], cs.t[:], AF.Silu, [cs.b], [sc.b])
        slabs = [self.sb("wmod_slab%d" % i, [128, NJ, 512]) for i in range(2)]
        mp = self.ps[0]
        for s in range(24):
            sl = slabs[s % 2]
            self.dma(sl.t[:], self.w_mod[l, s], writes=[sl.b])
            for cb in range(4):
                jc = s * 4 + cb
                for j in range(NJ):
                    self.mm(mp.t[:, 2 * jc:2 * jc + 2], sl.t[:, j, cb * 128:(cb + 1) * 128], sc.t[:, j, :],
                            j == 0, j == NJ - 1, [sl.b, sc.b], [mp.b])
        bm = self.sb("bmod", [128, 96])
        self.dma(bm.t[:], self.bmodT[l], writes=[bm.b])
        nm = self.sb("nmix", [128, NJ])
        self.dma(nm.t[:], self.nmixT[l], writes=[nm.b])
        nf = self.sb("nffn", [128, NJ])
        self.dma(nf.t[:], self.nffnT[l], writes=[nf.b])
        modT = self.modT_l[l]
        self.tt("dve", modT.t[:], mp.t[:, 0:192].rearrange("p (c r) -> p c r", r=2),
                bm.t[:].unsqueeze(2).to_broadcast([128, 96, 2]), ALU.add, [mp.b, bm.b], [modT.b])
        A = self.modA_l[l]
        self.stt(A.t[:, 0], modT.t[:, 16:32, :], 1.0, nm.t[:].unsqueeze(2).to_broadcast([128, NJ, 2]),
                 ALU.add, ALU.mult, [modT.b, nm.b], [A.b])
        self.stt(A.t[:, 1], modT.t[:, 64:80, :], 1.0, nf.t[:].unsqueeze(2).to_broadcast([128, NJ, 2]),
                 ALU.add, ALU.mult, [modT.b, nf.b], [A.b])
        self.modT = modT
        self.modA = A
        self.phase_end()

    def norm_chunk(self, src_ap, xin, hT, hslot, row, Aidx, sh_base, sq, rstd, pbank):
        self.dma(xin.t[:], src_ap, writes=[xin.b])
        self.norm_core(xin, hT, hslot, row, Aidx, sh_base, sq, rstd, pbank)

    def norm_core(self, xin, hT, hslot, row, Aidx, sh_base, sq, rstd, pbank):
        self.act(sq.t[:], xin.t[:], AF.Square, [xin.b], [sq.b])
        for j in range(NJ):
            self.mm(pbank.t[:], self.ones128, sq.t[:, j, :], j == 0, j == NJ - 1, [sq.b, self.cstb.b], [pbank.b])
        self.act(rstd.t[:], pbank.t[:], AF.Sqrt, [pbank.b], [rstd.b], bias=self.eps_t.t[:, 0:1], scale=1.0 / 16.0)
        self.recip(rstd.t[:], rstd.t[:], [rstd.b], [rstd.b])
        for j in range(NJ):
            e = "dve" if j % 2 == 0 else "pool"
            self.tt(e, xin.t[:, j, :], xin.t[:, j, :], rstd.t[:], ALU.mult, [xin.b, rstd.b], [xin.b])
            self.act(hT.t[:, j, hslot * 512:(hslot + 1) * 512], xin.t[:, j, :], AF.Identity, [xin.b, self.modA.b, self.modT.b], [hT.b],
                     bias=self.modT.t[:, sh_base + j, row:row + 1], scale=self.modA.t[:, Aidx, j, row:row + 1])

    def phase_A(self, l, src_ap):
        self.phase_begin()
        nc = self.nc
        SC = 3
        hT = self.sb("hT", [128, NJ, SC * 512], BF16)
        xins = [self.sb("xin%d" % i, [128, NJ, 512]) for i in range(1)]
        sq = self.sb("sq", [128, NJ, 512], BF16)
        rstd = self.sb("rstd", [128, 512])
        wst = [self.sb("wst%d" % i, [128, NJ, 256]) for i in range(2)]
        wbf = [self.sb("wbf%d" % i, [128, NJ, 256], BF16) for i in range(2)]
        wv = self.sb("wv", [128, NJ, 512], BF16)
        qkw = self.sb("qkw", [128, 2])
        self.dma(qkw.t[:], self.qkw[l], writes=[qkw.b])
        stg_f = [self.sb("stgf%d" % i, [128, 512]) for i in range(4)]
        stg_b = [self.sb("stgb%d" % i, [128, 512], BF16) for i in range(4)]
        qf = self.sb("qf", [128, 512])
        qn = self.sb("qn", [128, 512])
        qsq = self.sb("qsq", [128, 512], BF16)
        qrs = self.sb("qrs", [128, 512])
        qt1 = self.sb("qt1", [128, 512])
        qt2 = self.sb("qt2", [128, 512])
        rc = self.sb("ropec", [128, 512])
        rs = self.sb("ropes", [128, 512])
        cnt = {"f": 0, "b": 0, "ps": 0}
        mainps = [self.ps[i] for i in (0, 1, 2, 3)]
        auxps = [self.ps[i] for i in (4, 5)]
        normps = self.ps[6]
        vps = [self.ps[4], self.ps[5]]
        nsc = (self.NCH + SC - 1) // SC
        Qb, Kb, Vb, Ub, RWb, Gb = (self.dbufs[n] for n in ("Qs", "Ks", "Vs", "Us", "RWs", "Gs"))
        pr_tok0 = min([s[1] for s in self.seqs if s[0] == "P"] + [1 << 30])
        for sci in range(nsc):
            chunks = list(range(sci * SC, min(self.NCH, (sci + 1) * SC)))
            for hs, ci in enumerate(chunks):
                self.norm_chunk(src_ap[ci], xins[0], hT, hs, self.chunk_row[ci], 0, 0, sq, rstd, normps)
            if self.cfg.get("A_stop") == "norm":
                o = self.dram_out("dbg_h", [128, NJ * 512], BF16)
                self.dma(o[:, :].rearrange("p (j t) -> p j t", j=NJ), hT.t[:, :, 0:512], reads=[hT.b], writes=[self.dbufs["dbg_h"]])
                break
            for half in range(2):
                w = wst[half]
                self.dma(w.t[:], self.w_in[l, 4 + half], writes=[w.b])
                self.cp("dve", wv.t[:, :, half * 256:(half + 1) * 256], w.t[:], [w.b], [wv.b])
            if self.cfg.get("A_stop") == "v0":
                o = self.dram_out("dbg_wv", [128, NJ * 512], BF16)
                self.dma(o[:, :], wv.t[:].rearrange("p j c -> p (j c)"), reads=[wv.b], writes=[self.dbufs["dbg_wv"]])
                break
            for hs, ci in enumerate(chunks):
                for tt_ in range(4):
                    pb = vps[tt_ % 2]
                    for j in range(NJ):
                        self.mm(pb.t[:], hT.t[:, j, hs * 512 + tt_ * 128:hs * 512 + (tt_ + 1) * 128], wv.t[:, j, :],
                                j == 0, j == NJ - 1, [hT.b, wv.b], [pb.b])
                    tok0 = ci * 512 + tt_ * 128
                    sb_ = stg_b[cnt["b"] % 4]; cnt["b"] += 1
                    if self.chunk_row[ci] == 1:
                        sf = stg_f[cnt["f"] % 4]; cnt["f"] += 1
                        self.cp("act", sf.t[:], pb.t[:], [pb.b], [sf.b])
                        self.cp("pool", sb_.t[:], sf.t[:], [sf.b], [sb_.b])
                        p0 = tok0 - pr_tok0
                        self.dma(self.o_v[l, p0:p0 + 128, :], sf.t[:], reads=[sf.b])
                    else:
                        self.cp("act", sb_.t[:], pb.t[:], [pb.b], [sb_.b])
                    self.dma(self.Vs[tok0:tok0 + 128, :], sb_.t[:], reads=[sb_.b])
            if self.cfg.get("A_stop") in ("v", "v1"):
                break
            slabs = [[2 * i, 2 * i + 1] for i in range(39) if i not in (4, 5)] + [[78]]
            def slab_load(si_):
                self.dma(wst[si_ % 2].t[:], self.w_in[l, slabs[si_][0] // 2], writes=[wst[si_ % 2].b])

            def slab_conv(si_):
                nb_ = len(slabs[si_])
                self.cp("act" if si_ % 2 == 0 else "dve", wbf[si_ % 2].t[:, :, 0:nb_ * 128], wst[si_ % 2].t[:, :, 0:nb_ * 128],
                        [wst[si_ % 2].b], [wbf[si_ % 2].b])

            slab_load(0)
            slab_conv(0)
            for si, slab in enumerate(slabs):
                w = wst[si % 2]
                wb = wbf[si % 2]
                nb = len(slab)
                c0 = slab[0] * 128
                if si + 1 < len(slabs):
                    slab_load(si + 1)
                for hs, ci in enumerate(chunks):
                    tsl = slice(ci * 512, (ci + 1) * 512)
                    for bi, blk in enumerate(slab):
                        pb = mainps[cnt["ps"] % 4]; cnt["ps"] += 1
                        for j in range(NJ):
                            self.mm(pb.t[:], wb.t[:, j, bi * 128:(bi + 1) * 128], hT.t[:, j, hs * 512:(hs + 1) * 512],
                                    j == 0, j == NJ - 1, [wb.b, hT.b], [pb.b])
                        col = blk * 128
                        if col < 1024:
                            isk = col >= 512
                            hb = (col % 512) // 128
                            ap_ = auxps[0]
                            self.cp("act", qf.t[:], pb.t[:], [pb.b], [qf.b])
                            self.act(qsq.t[:], pb.t[:], AF.Square, [pb.b], [qsq.b])
                            self.mm(ap_.t[:], self.bones64, qsq.t[:], True, True, [qsq.b, self.cstb.b], [ap_.b])
                            self.act(qrs.t[:], ap_.t[:], AF.Sqrt, [ap_.b], [qrs.b], bias=self.eps_t.t[:, 0:1], scale=1.0)
                            self.recip(qrs.t[:], qrs.t[:], [qrs.b], [qrs.b])
                            self.stt(qn.t[:], qf.t[:], qkw.t[:, (1 if isk else 0):(2 if isk else 1)], qrs.t[:],
                                     ALU.mult, ALU.mult, [qf.b, qkw.b, qrs.b], [qn.b])
                            sb_ = stg_b[cnt["b"] % 4]; cnt["b"] += 1
                            ro = self.chunk_rope[ci]
                            if ro is not None:
                                if bi == 0:
                                    self.dma(rc.t[:], self.ropeC[:, ro:ro + 512], writes=[rc.b])
                                    self.dma(rs.t[:], self.ropeS[:, ro:ro + 512], writes=[rs.b])
                                ap2 = auxps[1]
                                self.mm(ap2.t[:], self.ropePT, qn.t[:], True, True, [qn.b, self.cst.b], [ap2.b])
                                self.tt("dve", qt1.t[:], ap2.t[:], rs.t[:], ALU.mult, [ap2.b, rs.b], [qt1.b])
                                self.tt("pool", qt2.t[:], qn.t[:], rc.t[:], ALU.mult, [qn.b, rc.b], [qt2.b])
                                self.tt("pool", sb_.t[:], qt1.t[:], qt2.t[:], ALU.add, [qt1.b, qt2.b], [sb_.b])
                            else:
                                self.cp("pool", sb_.t[:], qn.t[:], [qn.b], [sb_.b])
                                if isk:
                                    p0 = ci * 512 - pr_tok0
                                    self.dma(self.o_kT[l, hb * 128:(hb + 1) * 128, p0:p0 + 512], qn.t[:], reads=[qn.b])
                            dst = self.Ks if isk else self.Qs
                            self.dma(dst[hb, ci], sb_.t[:], reads=[sb_.b])
                        elif col < OFF_GATE:
                            sf = stg_f[cnt["f"] % 4]; cnt["f"] += 1
                            self.cp("act" if cnt["f"] % 2 else "dve", sf.t[:], pb.t[:], [pb.b], [sf.b])
                            if col < OFF_RW:
                                self.dma(self.Us[(col - OFF_S5) // 128, ci], sf.t[:], reads=[sf.b])
                            else:
                                self.dma(self.RWs[(col - OFF_RW) // 128, ci], sf.t[:], reads=[sf.b])
                        else:
                            sb_ = stg_b[cnt["b"] % 4]; cnt["b"] += 1
                            self.act(sb_.t[:], pb.t[:], AF.Sigmoid, [pb.b], [sb_.b])
                            self.dma(self.Gs[(col - OFF_GATE) // 128, ci], sb_.t[:], reads=[sb_.b])
                if si + 1 < len(slabs):
                    slab_conv(si + 1)
        self.phase_end()

    def phase_zero_branch(self, which):
        self.phase_begin()
        z = self.sb("zb", [128, 512], BF16)
        self.memset("pool", z.t[:], 0.0, [z.b])
        for cb in range(4):
            for ci in range(self.NCH):
                self.dma(self.OUTs[which][cb, ci], z.t[:], reads=[z.b])
        self.phase_end()

    def phase_s5(self, l):
        self.phase_begin()
        nc = self.nc
        PI = math.pi
        Tmax = max(T for _, _, T in self.seqs)
        prm = self.sb("s5prm", [128, 2, 3, 16])
        self.dma(prm.t[:], self.s5p[l].rearrange("d p k s -> p d k s"), writes=[prm.b])
        bt = self.sb("s5bt", [128, 2, 2, 16, 16])
        self.dma(bt.t[:], self.s5b[l].rearrange("d p r s c -> p d r s c"), writes=[bt.b])
        ct = self.sb("s5ct", [128, 2, 2, 16, 16])
        self.dma(ct.t[:], self.s5c[l].rearrange("d p r s c -> p d r s c"), writes=[ct.b])
        dsk = self.sb("s5dsk", [128, 4])
        self.dma(dsk.t[:], self.s5d[l], writes=[dsk.b])
        io_i = self.sb("io_i", [128, 512], mybir.dt.int32)
        self.P.add("pool", lambda: nc.gpsimd.iota(io_i.t[:], pattern=[[1, 512]], base=1, channel_multiplier=0), writes=[io_i.b])
        io_f = self.sb("io_f", [128, 512])
        self.cp("dve", io_f.t[:], io_i.t[:], [io_i.b], [io_f.b])
        w = self.sb("s5w", [128, 2, 12, 16])
        W = lambda k: w.t[:, :, k, :]
        lr = prm.t[:, :, 0, :]; li = prm.t[:, :, 1, :]
        self.act(W(0), prm.t[:, :, 2, :], AF.Exp, [prm.b], [w.b])
        self.tt("dve", W(1), lr, W(0), ALU.mult, [prm.b, w.b], [w.b])
        self.act(W(1), W(1), AF.Exp, [w.b], [w.b])
        self.tt("dve", W(2), li, W(0), ALU.mult, [prm.b, w.b], [w.b])
        qi = self.sb("s5qi", [128, 2, 16], mybir.dt.int32)
        qf = self.sb("s5qf", [128, 2, 16])

        def sincos(dst_sin, dst_cos, ang_ap, shape_tile_i, shape_tile_f, tmp_ap, rd, wr):
            for dst, shift in ((dst_sin, 0.0), (dst_cos, PI / 2)):
                self.ts("dve", tmp_ap, ang_ap, shift, None, ALU.add, None, rd, wr)
                self.ts("dve", shape_tile_i, tmp_ap, 1.0 / (2 * PI), None, ALU.mult, None, wr, wr)
                self.cp("dve", shape_tile_f, shape_tile_i, wr, wr)
                self.stt(tmp_ap, shape_tile_f, -2 * PI, tmp_ap, ALU.mult, ALU.add, wr, wr)
                self.act(dst, tmp_ap, AF.Sin, wr, wr)

        sincos(W(3), W(4), W(2), qi.t[:], qf.t[:], W(5), [w.b], [w.b, qi.b, qf.b])
        self.tt("dve", W(6), W(1), W(4), ALU.mult, [w.b], [w.b])
        self.tt("dve", W(7), W(1), W(3), ALU.mult, [w.b], [w.b])
        self.tt("dve", W(8), lr, lr, ALU.mult, [prm.b], [w.b])
        self.tt("dve", W(9), li, li, ALU.mult, [prm.b], [w.b])
        self.tt("dve", W(8), W(8), W(9), ALU.add, [w.b], [w.b])
        self.recip(W(8), W(8), [w.b], [w.b])
        self.ts("dve", W(9), W(6), -1.0, None, ALU.add, None, [w.b], [w.b])
        self.tt("dve", W(10), W(9), lr, ALU.mult, [w.b, prm.b], [w.b])
        self.tt("dve", W(11), W(7), li, ALU.mult, [w.b, prm.b], [w.b])
        self.tt("dve", W(10), W(10), W(11), ALU.add, [w.b], [w.b])
        self.tt("dve", W(10), W(10), W(8), ALU.mult, [w.b], [w.b])
        self.tt("dve", W(11), W(7), lr, ALU.mult, [w.b, prm.b], [w.b])
        self.tt("dve", W(9), W(9), li, ALU.mult, [w.b, prm.b], [w.b])
        self.tt("dve", W(11), W(11), W(9), ALU.subtract, [w.b], [w.b])
        self.tt("dve", W(11), W(11), W(8), ALU.mult, [w.b], [w.b])
        Z = self.sb("s5Z", [128, 2, 16, 32])
        t1 = self.sb("s5t1", [128, 16, 16])
        t2 = self.sb("s5t2", [128, 16, 16])
        BB = self.sb("s5BB", [32, 2, 16, 128])
        CC = self.sb("s5CC", [128, 2, 16, 128])
        self.memset("pool", Z.t[:], 0.0, [Z.b])
        self.memset("pool", CC.t[:], 0.0, [CC.b])
        kk_ = [0]

        def build_dir(d_):
            cr = W(10)[:, d_, :].unsqueeze(2).to_broadcast([128, 16, 16])
            cim = W(11)[:, d_, :].unsqueeze(2).to_broadcast([128, 16, 16])
            br = bt.t[:, d_, 0]; bi = bt.t[:, d_, 1]
            for (g2, lo) in ((0, 0), (1, 64)):
                ps_ = slice(lo, lo + 64)
                cs = slice(g2 * 16, g2 * 16 + 16)
                self.tt("dve", t1.t[ps_], br[ps_], cr[ps_], ALU.mult, [bt.b, w.b], [t1.b])
                self.tt("dve", t2.t[ps_], bi[ps_], cim[ps_], ALU.mult, [bt.b, w.b], [t2.b])
                self.tt("dve", Z.t[ps_, 0, :, cs], t1.t[ps_], t2.t[ps_], ALU.subtract, [t1.b, t2.b], [Z.b])
                self.tt("dve", t1.t[ps_], bi[ps_], cr[ps_], ALU.mult, [bt.b, w.b], [t1.b])
                self.tt("dve", t2.t[ps_], br[ps_], cim[ps_], ALU.mult, [bt.b, w.b], [t2.b])
                self.tt("dve", Z.t[ps_, 1, :, cs], t1.t[ps_], t2.t[ps_], ALU.add, [t1.b, t2.b], [Z.b])
            for ri in range(2):
                for st in range(16):
                    pb = self.ps[kk_[0] % 2]; kk_[0] += 1
                    self.tr(pb.t[0:32, 0:128], Z.t[:, ri, st, :], self.ident, [Z.b, self.cst.b], [pb.b])
                    self.cp("act" if kk_[0] % 2 else "dve", BB.t[:, ri, st, :], pb.t[0:32, 0:128], [pb.b], [BB.b])
            for st in range(16):
                for (g2, lo) in ((0, 0), (1, 64)):
                    c0 = (st % 4) * 32 + g2 * 16
                    self.cp("pool", CC.t[lo:lo + 64, 0, st, c0:c0 + 16], ct.t[lo:lo + 64, d_, 0, st, :], [ct.b], [CC.b])
                    self.ts("dve", CC.t[lo:lo + 64, 1, st, c0:c0 + 16], ct.t[lo:lo + 64, d_, 1, st, :], -1.0, None, ALU.mult, None,
                            [ct.b], [CC.b])
        ytile = self.sb("s5y", [128, 4, Tmax], BF16)
        hst = self.sb("s5hst", [128, 4, 2])
        fin = self.sb("s5fin", [128, max(1, self.n_prompt), 2, 2, 16])
        h0t = self.sb("s5h0", [128, 2, 2, 16])
        if self.has_sample:
            self.dma(h0t.t[:], self.s5h0[l], writes=[h0t.b])
        cosT = [self.sb("s5cos%d" % i, [128, 512]) for i in range(4)]
        sinT = [self.sb("s5sin%d" % i, [128, 512]) for i in range(4)]
        magT = [self.sb("s5mag%d" % i, [128, 512]) for i in range(4)]
        angi = self.sb("s5angi", [128, 512], mybir.dt.int32)
        angf = self.sb("s5angf", [128, 512])
        angt = self.sb("s5angt", [128, 512])
        ang0 = self.sb("s5ang0", [128, 512])
        Ut = [self.sb("s5U%d" % i, [32, 512]) for i in range(4)]
        D2 = lambda nm: [self.sb("%s%d" % (nm, i), [128, 512]) for i in range(2)]
        burs = D2("s5bur"); buis = D2("s5bui"); xrs = D2("s5xr"); xis = D2("s5xi")
        grs = D2("s5gr"); gis = D2("s5gi"); hrs = D2("s5hr"); his = D2("s5hi")
        xa = self.sb("s5xa", [128, 512]); xb = self.sb("s5xb", [128, 512])
        xa2 = self.sb("s5xa2", [128, 512]); xb2 = self.sb("s5xb2", [128, 512])
        ubuf = self.sb("s5u", [128, 512])
        ygf = self.sb("s5ygf", [128, 4, 512])
        self.dma(ygf.t[:], self.s5glu[l], writes=[ygf.b])
        glub = self.sb("glub", [128, 4, 512], BF16)
        self.cp("dve", glub.t[:], ygf.t[:], [ygf.b], [glub.b])
        ygb = self.sb("s5ygb", [128, 4, 512], BF16)
        zs = self.sb("s5zs", [128, 512])
        osb = [self.sb("s5o%d" % i, [128, 512], BF16) for i in range(2)]
        bups = [self.ps[2], self.ps[3], self.ps[4], self.ps[5]]
        yps = [self.ps[6], self.ps[7]]
        gps = [self.ps[0], self.ps[1]]
        pi_ = -1
        ku = 0
        ky = 0
        for (kind, off, T) in self.seqs:
            if kind == "P":
                pi_ += 1
            CS = min(512, T)
            nch = T // CS
            sl = slice(0, CS)
            for d_ in range(2):
                rev = (d_ == 1)
                order = list(range(nch))[::-1] if rev else list(range(nch))
                build_dir(d_)
                for cb in range(4):
                    for sti in range(4):
                        st = cb * 4 + sti
                        cT = cosT[sti]; sT = sinT[sti]; mT = magT[sti]
                        self.ts("dve", ang0.t[:], io_f.t[:], W(2)[:, d_, st:st + 1], None, ALU.mult, None, [io_f.b, w.b], [ang0.b])
                        sincos(sT.t[:], cT.t[:], ang0.t[:], angi.t[:], angf.t[:], angt.t[:], [ang0.b], [angt.b, angi.b, angf.b, sT.b, cT.b])
                        self.ts("pool", mT.t[:], self.onesf.t[:], W(1)[:, d_, st:st + 1], None, ALU.mult, None, [self.onesf.b, w.b], [mT.b])
                        if kind == "S":
                            self.cp("pool", hst.t[:, sti, 0:1], h0t.t[:, d_, 0, st:st + 1], [h0t.b], [hst.b])
                            self.cp("pool", hst.t[:, sti, 1:2], h0t.t[:, d_, 1, st:st + 1], [h0t.b], [hst.b])
                    if kind != "S":
                        self.memset("pool", hst.t[:], 0.0, [hst.b])
                    for ch in order:
                        c0 = ch * CS
                        ci = (off + c0) // 512; o_in = (off + c0) % 512
                        yp = yps[ky % 2]; ky += 1
                        for sti in range(4):
                            st = cb * 4 + sti
                            cT = cosT[sti]; sT = sinT[sti]; mT = magT[sti]
                            c_, s_ = cT.t[:, sl], sT.t[:, sl]
                            k2 = ku % 2
                            U = Ut[ku % 4]; ku += 1
                            pr = bups[2 * k2]; pim = bups[2 * k2 + 1]
                            bur = burs[k2]; bui = buis[k2]; xr = xrs[k2]; xi = xis[k2]
                            gr = grs[k2]; gi = gis[k2]; HR = hrs[k2]; HI = his[k2]
                            self.dma(U.t[:, 0:CS], self.Us[cb, ci, sti * 32:(sti + 1) * 32, o_in:o_in + CS], writes=[U.b])
                            self.mm(pr.t[:, 0:CS], BB.t[:, 0, st, :], U.t[:, 0:CS], True, True, [BB.b, U.b], [pr.b])
                            self.mm(pim.t[:, 0:CS], BB.t[:, 1, st, :], U.t[:, 0:CS], True, True, [BB.b, U.b], [pim.b])
                            self.cp("act", bur.t[:, 0:CS], pr.t[:, 0:CS], [pr.b], [bur.b])
                            self.cp("act", bui.t[:, 0:CS], pim.t[:, 0:CS], [pim.b], [bui.b])
                            BR = self.revap(bur, CS, rev); BI = self.revap(bui, CS, rev)
                            self.tt("dve", xa.t[:, sl], BR, c_, ALU.mult, [bur.b, cT.b], [xa.b])
                            self.tt("dve", xb.t[:, sl], BI, s_, ALU.mult, [bui.b, sT.b], [xb.b])
                            self.tt("dve", xr.t[:, sl], xa.t[:, sl], xb.t[:, sl], ALU.add, [xa.b, xb.b], [xr.b])
                            self.tt("dve", xa2.t[:, sl], BI, c_, ALU.mult, [bui.b, cT.b], [xa2.b])
                            self.tt("dve", xb2.t[:, sl], BR, s_, ALU.mult, [bur.b, sT.b], [xb2.b])
                            self.tt("dve", xi.t[:, sl], xa2.t[:, sl], xb2.t[:, sl], ALU.subtract, [xa2.b, xb2.b], [xi.b])
                            self.P.add("dve", (lambda a=gr.t[:, sl], b=mT.t[:, sl], c=xr.t[:, sl], i=hst.t[:, sti, 0:1]:
                                               nc.vector.tensor_tensor_scan(out=a, data0=b, data1=c, initial=i, op0=ALU.mult, op1=ALU.add)),
                                       reads=[mT.b, xr.b, hst.b], writes=[gr.b])
                            self.P.add("dve", (lambda a=gi.t[:, sl], b=mT.t[:, sl], c=xi.t[:, sl], i=hst.t[:, sti, 1:2]:
                                               nc.vector.tensor_tensor_scan(out=a, data0=b, data1=c, initial=i, op0=ALU.mult, op1=ALU.add)),
                                       reads=[mT.b, xi.b, hst.b], writes=[gi.b])
                            self.tt("dve", xa.t[:, sl], gr.t[:, sl], c_, ALU.mult, [gr.b, cT.b], [xa.b])
                            self.tt("dve", xb.t[:, sl], gi.t[:, sl], s_, ALU.mult, [gi.b, sT.b], [xb.b])
                            self.tt("dve", HR.t[:, sl], xa.t[:, sl], xb.t[:, sl], ALU.subtract, [xa.b, xb.b], [HR.b])
                            self.tt("dve", xa2.t[:, sl], gr.t[:, sl], s_, ALU.mult, [gr.b, sT.b], [xa2.b])
                            self.tt("dve", xb2.t[:, sl], gi.t[:, sl], c_, ALU.mult, [gi.b, cT.b], [xb2.b])
                            self.tt("dve", HI.t[:, sl], xa2.t[:, sl], xb2.t[:, sl], ALU.add, [xa2.b, xb2.b], [HI.b])
                            self.cp("dve", hst.t[:, sti, 0:1], HR.t[:, CS - 1:CS], [HR.b], [hst.b])
                            self.cp("dve", hst.t[:, sti, 1:2], HI.t[:, CS - 1:CS], [HI.b], [hst.b])
                            self.mm(yp.t[:, 0:CS], CC.t[:, 0, st, :], HR.t[:, sl], sti == 0, False, [CC.b, HR.b], [yp.b])
                            self.mm(yp.t[:, 0:CS], CC.t[:, 1, st, :], HI.t[:, sl], False, sti == 3, [CC.b, HI.b], [yp.b])
                        yv = self.revap2(ytile, cb, c0, CS, rev)
                        if d_ == 0:
                            self.cp("act", yv, yp.t[:, 0:CS], [yp.b], [ytile.b])
                        else:
                            self.tt("dve", yv, yp.t[:, 0:CS], yv, ALU.add, [yp.b, ytile.b], [ytile.b])
                    if kind == "P":
                        for sti in range(4):
                            self.cp("pool", fin.t[:, pi_, d_, :, cb * 4 + sti], hst.t[:, sti, :], [hst.b], [fin.b])
            for ch in range(nch):
                c0 = ch * CS
                ci = (off + c0) // 512; o_in = (off + c0) % 512
                for cb in range(4):
                    self.dma(ubuf.t[:, 0:CS], self.Us[cb, ci, :, o_in:o_in + CS], writes=[ubuf.b])
                    self.stt(ygf.t[:, cb, 0:CS], ubuf.t[:, 0:CS], dsk.t[:, cb:cb + 1], ytile.t[:, cb, c0:c0 + CS], ALU.mult, ALU.add,
                             [ubuf.b, dsk.b, ytile.b], [ygf.b])
                    yv_ = ygf.t[:, cb, 0:CS]
                    self.tt("pool", zs.t[:, 0:CS], yv_, yv_, ALU.mult, [ygf.b], [zs.b])
                    self.ts("pool", zs.t[:, 0:CS], zs.t[:, 0:CS], 0.044715, 1.0, ALU.mult, ALU.add, [zs.b], [zs.b])
                    self.tt("pool", zs.t[:, 0:CS], zs.t[:, 0:CS], yv_, ALU.mult, [zs.b, ygf.b], [zs.b])
                    self.act(zs.t[:, 0:CS], zs.t[:, 0:CS], AF.Tanh, [zs.b], [zs.b], scale=math.sqrt(2.0 / math.pi))
                    self.stt(yv_, zs.t[:, 0:CS], 1.0, yv_, ALU.add, ALU.mult, [zs.b, ygf.b], [ygf.b])
                    self.ts("pool", yv_, yv_, 0.5, None, ALU.mult, None, [ygf.b], [ygf.b])
                    self.cp("pool", ygb.t[:, cb, 0:CS], ygf.t[:, cb, 0:CS], [ygf.b], [ygb.b])
                for c2 in range(4):
                    gp = gps[c2 % 2]
                    for c1 in range(4):
                        self.mm(gp.t[:, 0:CS], glub.t[:, c1, c2 * 128:(c2 + 1) * 128], ygb.t[:, c1, 0:CS], c1 == 0, c1 == 3,
                                [glub.b, ygb.b], [gp.b])
                    self.act(zs.t[:, 0:CS], gp.t[:, 0:CS], AF.Sigmoid, [gp.b], [zs.b])
                    o_ = osb[c2 % 2]
                    self.tt("dve", o_.t[:, 0:CS], zs.t[:, 0:CS], ygf.t[:, c2, 0:CS], ALU.mult, [zs.b, ygf.b], [o_.b])
                    self.dma(self.OUTs[1][c2, ci, :, o_in:o_in + CS], o_.t[:, 0:CS], reads=[o_.b])
        if self.n_prompt:
            self.dma(self.o_s5f[l], fin.t[:], reads=[fin.b])
        self.phase_end()

    def revap(self, tl, CS, rev):
        return tl.t[:, CS - 1::-1] if rev else tl.t[:, 0:CS]

    def revap2(self, tl, cb, c0, CS, rev):
        if not rev:
            return tl.t[:, cb, c0:c0 + CS]
        if c0 == 0:
            return tl.t[:, cb, CS - 1::-1]
        return tl.t[:, cb, c0 + CS - 1:c0 - 1:-1]

    def phase_rwkv(self, l):
        self.phase_begin()
        nc = self.nc
        SL = 1024 if any(T >= 1024 for _, _, T in self.seqs) else 256
        NCK = SL // 64
        Tmax = max(T for _, _, T in self.seqs)
        vec = self.sb("rwvec", [128, 4, 10]); self.dma(vec.t[:], self.rw_vec[l], writes=[vec.b])
        wa = self.sb("rwwa", [128, 4, 2, 2]); self.dma(wa.t[:], self.rw_wa[l], writes=[wa.b])
        mu3 = self.sb("rwmu3", [128, 3]); self.dma(mu3.t[:], self.rw_mu3[l], writes=[mu3.b])
        w2 = self.sb("rww2", [128, 512]); self.dma(w2.t[:], self.rw_w2[l], writes=[w2.b])
        a2 = self.sb("rwa2", [128, 512]); self.dma(a2.t[:], self.rw_a2[l], writes=[a2.b])
        g2 = self.sb("rwg2", [128, 512]); self.dma(g2.t[:], self.rw_g2[l], writes=[g2.b])
        msk = self.sb("rwmsk", [128, 4, 128]); self.dma(msk.t[:], self.rw_masks, writes=[msk.b])
        c2 = self.sb("rwc2", [128, 2, 128]); self.dma(c2.t[:], self.rw_c2, writes=[c2.b])
        eps2 = self.sb("rweps", [128, 1]); self.memset("pool", eps2.t[:], L2_EPS, [eps2.b])
        def tile(nm, w=SL, dt=F32):
            return self.sb(nm, [128, w], dt)
        zb = [self.sb("rwzb%d" % i, [128, SL + 2]) for i in range(2)]
        sh = tile("rwsh")
        r_m = tile("rw_r"); k_m = tile("rw_k"); v_m = tile("rw_v"); hw_m = tile("rw_hw"); ha_m = tile("rw_ha")
        kk = tile("rw_kk"); lw = tile("rw_lw"); Lc = tile("rw_Lc"); av = tile("rw_a"); lg = tile("rw_lg"); ex = tile("rw_ex")
        t1 = tile("rw_t1"); t2 = tile("rw_t2")
        AR = self.sb("rw_AR", [128, SL // 128, 2, 128])
        BH = tile("rw_BH"); KH = tile("rw_KH")
        BC = self.sb("rw_BC", [128, SL // 128, 128]); KC = self.sb("rw_KC", [128, SL // 128, 128])
        VT = self.sb("rw_VT", [128, SL // 128, 128])
        VP = [self.sb("rw_VP%d" % i, [128, SL // 128, 128]) for i in range(2)]
        Lst = self.sb("rw_Lst", [128, NCK]); Gc = self.sb("rw_Gc", [128, NCK])
        ST = self.sb("rw_ST", [128, 128])
        ytile = self.sb("rw_y", [128, Tmax])
        UT = self.sb("rw_UT", [128, 128]); UP = [self.sb("rw_UP%d" % i, [128, 128]) for i in range(2)]
        Xs = [self.sb("rw_Xs%d" % i, [128, 64]) for i in range(2)]
        NU = 8
        NM1 = [self.sb("rw_NM1_%d" % i, [128, 256]) for i in range(NU)]
        NM2 = [self.sb("rw_NM2_%d" % i, [128, 256]) for i in range(NU)]
        NT_ = [self.sb("rw_NT_%d" % i, [128, 128]) for i in range(NU)]
        XA = [[self.sb("rw_XA%d_%d" % (h, i), [128, 128]) for i in range(2)] for h in range(NU)]
        XB = [[self.sb("rw_XB%d_%d" % (h, i), [128, 128]) for i in range(2)] for h in range(NU)]
        PT = [[self.sb("rw_P%d_%d" % (h, i), [128, 128]) for i in range(2)] for h in range(NU)]
        ysq = tile("rw_ysq", 512); yc = tile("rw_yc", 512); yrs = tile("rw_yrs", 512); gsg = tile("rw_gsg", 512)
        bon = tile("rw_bon", 512); osb = [self.sb("rw_o%d" % i, [128, 512], BF16) for i in range(2)]
        for t_ in (UP[0], UP[1], UT, Xs[0], Xs[1], VP[0], VP[1]):
            self.memset("pool", t_.t[:], 0.0, [t_.b])
        s0f = self.sb("rw_s0", [128, 64])
        pk = {"n": 0}

        def PS():
            pk["n"] += 1
            return self.ps[pk["n"] % 8]

        def load_mixed(dst, blk, mu_ap, seq, s0, n):
            kind, off, T = seq
            z = zb[pk["n"] % 2]; pk["n"] += 1
            t = 0
            while t < n:
                g = off + s0 + t
                ci = g // 512; o_in = g % 512; m = min(512 - o_in, n - t)
                self.dma(z.t[:, 1 + t:1 + t + m], self.RWs[blk, ci, :, o_in:o_in + m], writes=[z.b])
                t += m
            for (pos, col) in ((s0 - 1, 0), (s0 + n, n + 1)):
                if pos < 0 or pos >= T:
                    self.memset("pool", z.t[:, col:col + 1], 0.0, [z.b])
                else:
                    g = off + pos
                    self.dma(z.t[:, col:col + 1], self.RWs[blk, g // 512, :, (g % 512):(g % 512) + 1], writes=[z.b], slow=True)
            self.tt("dve", sh.t[:, 0:n], z.t[:, 0:n], z.t[:, 2:n + 2], ALU.add, [z.b], [sh.b])
            self.stt(sh.t[:, 0:n], sh.t[:, 0:n], 0.5, z.t[:, 1:n + 1], ALU.mult, ALU.subtract, [sh.b, z.b], [sh.b])
            self.stt(dst.t[:, 0:n], sh.t[:, 0:n], mu_ap, z.t[:, 1:n + 1], ALU.mult, ALU.add, [sh.b, z.b, vec.b, mu3.b], [dst.b])

        pi_ = -1
        for seq in self.seqs:
            kind, off, T = seq
            if kind == "P":
                pi_ += 1
            sl_ = min(SL, T)
            nseg = T // sl_
            nck = sl_ // 64
            for hp in range(4):
                V = lambda k_: vec.t[:, hp, k_:k_ + 1]
                for d_ in range(2):
                    rev = d_ == 1
                    self.memset("pool", ST.t[:], 0.0, [ST.b])
                    if kind == "S":
                        for hh in range(2):
                            self.dma(s0f.t[64 * hh:64 * hh + 64, :], self.rwS0[l, d_, 2 * hp + hh], writes=[s0f.b])
                        for hh in range(2):
                            self.cp("pool", ST.t[64 * hh:64 * hh + 64, 64 * hh:64 * hh + 64], s0f.t[64 * hh:64 * hh + 64, :], [s0f.b], [ST.b])
                    mS, mI = (2, 3) if rev else (0, 1)
                    mST = 0 if rev else 2
                    segs = list(range(nseg))[::-1] if rev else list(range(nseg))
                    for sg in segs:
                        s0 = sg * sl_
                        n = sl_
                        load_mixed(r_m, hp, V(0), seq, s0, n)
                        load_mixed(k_m, 4 + hp, V(1), seq, s0, n)
                        load_mixed(v_m, 8 + hp, V(2), seq, s0, n)
                        load_mixed(hw_m, 12, mu3.t[:, 0:1], seq, s0, n)
                        load_mixed(ha_m, 13, mu3.t[:, 1:2], seq, s0, n)
                        dr = slice(64 * d_, 64 * d_ + 64)
                        self.act(hw_m.t[dr, 0:n], hw_m.t[dr, 0:n], AF.Tanh, [hw_m.b], [hw_m.b])
                        for c0 in range(0, n, 512):
                            m = min(512, n - c0)
                            pb = PS()
                            self.mm(pb.t[:, 0:m], w2.t[dr, hp * 128:(hp + 1) * 128], hw_m.t[dr, c0:c0 + m], True, True, [w2.b, hw_m.b], [pb.b])
                            self.act(lw.t[:, c0:c0 + m], pb.t[:, 0:m], AF.Sigmoid, [pb.b, wa.b], [lw.b], bias=wa.t[:, hp, d_, 0:1], scale=1.0)
                            pb = PS()
                            self.mm(pb.t[:, 0:m], a2.t[dr, hp * 128:(hp + 1) * 128], ha_m.t[dr, c0:c0 + m], True, True, [a2.b, ha_m.b], [pb.b])
                            self.act(av.t[:, c0:c0 + m], pb.t[:, 0:m], AF.Sigmoid, [pb.b, wa.b], [av.b], bias=wa.t[:, hp, d_, 1:2], scale=1.0)
                        self.ts("dve", lw.t[:, 0:n], lw.t[:, 0:n], -math.exp(-0.5), None, ALU.mult, None, [lw.b], [lw.b])
                        self.ts("dve", t1.t[:, 0:n], k_m.t[:, 0:n], V(3), None, ALU.mult, None, [k_m.b, vec.b], [t1.b])
                        self.tt("dve", t2.t[:, 0:n], t1.t[:, 0:n], t1.t[:, 0:n], ALU.mult, [t1.b], [t2.b])
                        for c0 in range(0, n, 512):
                            m = min(512, n - c0)
                            pb = PS()
                            self.mm(pb.t[:, 0:m], c2.t[:, 0, :], t2.t[:, c0:c0 + m], True, True, [c2.b, t2.b], [pb.b])
                            self.act(kk.t[:, c0:c0 + m], pb.t[:, 0:m], AF.Sqrt, [pb.b, eps2.b], [kk.b], bias=eps2.t[:, 0:1], scale=1.0)
                        self.recip(kk.t[:, 0:n], kk.t[:, 0:n], [kk.b], [kk.b])
                        self.tt("dve", kk.t[:, 0:n], kk.t[:, 0:n], t1.t[:, 0:n], ALU.mult, [kk.b, t1.b], [kk.b])
                        self.cumsum(Lc, lw, n, t2)
                        L3 = Lc.t[:, 0:n].rearrange("p (c j) -> p c j", j=64)
                        self.memset("pool", Lst.t[:, 0:1], 0.0, [Lst.b])
                        if nck > 1:
                            self.cp("pool", Lst.t[:, 1:nck], L3[:, 0:nck - 1, 63], [Lc.b], [Lst.b])
                        lg3 = lg.t[:, 0:n].rearrange("p (c j) -> p c j", j=64)
                        if not rev:
                            self.tt("dve", lg3, L3, Lst.t[:, 0:nck].unsqueeze(2).to_broadcast([128, nck, 64]), ALU.subtract, [Lc.b, Lst.b], [lg.b])
                        else:
                            self.tt("dve", t2.t[:, 0:n], Lc.t[:, 0:n], lw.t[:, 0:n], ALU.subtract, [Lc.b, lw.b], [t2.b])
                            self.tt("dve", lg3, L3[:, :, 63:64].to_broadcast([128, nck, 64]),
                                    t2.t[:, 0:n].rearrange("p (c j) -> p c j", j=64), ALU.subtract, [Lc.b, t2.b], [lg.b])
                        self.tt("dve", Gc.t[:, 0:nck], L3[:, :, 63], Lst.t[:, 0:nck], ALU.subtract, [Lc.b, Lst.b], [Gc.b])
                        self.act(Gc.t[:, 0:nck], Gc.t[:, 0:nck], AF.Exp, [Gc.b], [Gc.b])
                        AR4 = AR.t[:, 0:n // 128]
                        self.act(ex.t[:, 0:n], lg.t[:, 0:n], AF.Exp, [lg.b], [ex.b])
                        self.tt("dve", AR4[:, :, 1, :], r_m.t[:, 0:n].rearrange("p (c j) -> p c j", j=128),
                                ex.t[:, 0:n].rearrange("p (c j) -> p c j", j=128), ALU.mult, [r_m.b, ex.b], [AR.b])
                        self.act(ex.t[:, 0:n], lg.t[:, 0:n], AF.Exp, [lg.b], [ex.b], scale=-1.0)
                        self.tt("dve", t1.t[:, 0:n], kk.t[:, 0:n], av.t[:, 0:n], ALU.mult, [kk.b, av.b], [t1.b])
                        self.tt("dve", BH.t[:, 0:n], t1.t[:, 0:n], ex.t[:, 0:n], ALU.mult, [t1.b, ex.b], [BH.b])
                        self.ts("pool", t1.t[:, 0:n], av.t[:, 0:n], -1.0, V(4), ALU.add, ALU.mult, [av.b, vec.b], [t1.b])
                        self.ts("dve", t1.t[:, 0:n], t1.t[:, 0:n], 1.0, None, ALU.add, None, [t1.b], [t1.b])
                        self.tt("dve", t1.t[:, 0:n], t1.t[:, 0:n], k_m.t[:, 0:n], ALU.mult, [t1.b, k_m.b], [t1.b])
                        self.tt("dve", KH.t[:, 0:n], t1.t[:, 0:n], ex.t[:, 0:n], ALU.mult, [t1.b, ex.b], [KH.b])
                        self.tt("pool", t1.t[:, 0:n], lg.t[:, 0:n], lw.t[:, 0:n], ALU.subtract, [lg.b, lw.b], [t1.b])
                        self.act(ex.t[:, 0:n], t1.t[:, 0:n], AF.Exp, [t1.b], [ex.b])
                        self.stt(AR4[:, :, 0, :], kk.t[:, 0:n].rearrange("p (c j) -> p c j", j=128), -1.0,
                                 ex.t[:, 0:n].rearrange("p (c j) -> p c j", j=128), ALU.mult, ALU.mult, [kk.b, ex.b], [AR.b])
                        Gb = Gc.t[:, 0:nck].unsqueeze(2).to_broadcast([128, nck, 64])
                        self.tt("dve", t1.t[:, 0:n].rearrange("p (c j) -> p c j", j=64), BH.t[:, 0:n].rearrange("p (c j) -> p c j", j=64), Gb,
                                ALU.mult, [BH.b, Gc.b], [t1.b])
                        self.tt("dve", t2.t[:, 0:n].rearrange("p (c j) -> p c j", j=64), KH.t[:, 0:n].rearrange("p (c j) -> p c j", j=64), Gb,
                                ALU.mult, [KH.b, Gc.b], [t2.b])
                        for tl in range(n // 128):
                            for (src, dst, eng) in ((t1, BC, "act"), (t2, KC, "dve"), (v_m, VT, "act")):
                                pb = PS()
                                self.tr(pb.t[:, 0:128], src.t[:, tl * 128:(tl + 1) * 128], self.ident, [src.b, self.cst.b], [pb.b])
                                self.cp(eng, dst.t[:, tl, :], pb.t[:, 0:128], [pb.b], [dst.b])
                            for hh in range(2):
                                self.cp("pool", VP[hh].t[:, tl, 64 * hh:64 * hh + 64], VT.t[:, tl, 64 * hh:64 * hh + 64], [VT.b], [VP[hh].b])
                        pairs = list(range(n // 128))[::-1] if rev else list(range(n // 128))
                        GB = 2
                        NUB = 2 * GB
                        batches = [pairs[i:i + GB] for i in range(0, len(pairs), GB)]

                        def gen_inv(bi, batch, res):
                            base = (bi % 2) * NUB
                            units = [(pr, hh) for pr in batch for hh in range(2)]
                            for u, (pr, hh) in enumerate(units):
                                ub = base + u
                                tk = slice(pr * 128, (pr + 1) * 128)
                                ARp = AR.t[:, pr].rearrange("p a j -> p (a j)")
                                hr_ = slice(64 * hh, 64 * hh + 64)
                                n1 = NM1[ub]; n2 = NM2[ub]; nt = NT_[ub]
                                pb = PS()
                                self.mm(pb.t[:, 0:256], BH.t[hr_, tk], ARp[hr_, :], True, True, [BH.b, AR.b], [pb.b])
                                self.tt("dve", n1.t[:].rearrange("p (a j) -> p a j", j=128), pb.t[:, 0:256].rearrange("p (a j) -> p a j", j=128),
                                        msk.t[:, mS:mS + 2, :], ALU.mult, [pb.b, msk.b], [n1.b])
                                yield
                                pb = PS()
                                self.mm(pb.t[:, 0:256], KH.t[hr_, tk], ARp[hr_, :], True, True, [KH.b, AR.b], [pb.b])
                                self.tt("dve", n2.t[:].rearrange("p (a j) -> p a j", j=128), pb.t[:, 0:256].rearrange("p (a j) -> p a j", j=128),
                                        msk.t[:, mS:mS + 2, :], ALU.mult, [pb.b, msk.b], [n2.b])
                                yield
                                pb = PS()
                                self.mm(pb.t[:, 0:128], AR.t[hr_, pr, 0, :], BH.t[hr_, tk], True, True, [BH.b, AR.b], [pb.b])
                                self.tt("dve", nt.t[:], pb.t[:, 0:128], msk.t[:, mST, :], ALU.mult, [pb.b, msk.b], [nt.b])
                                self.tt("pool", PT[ub][0].t[:], self.ident, n1.t[:, 0:128], ALU.add, [self.cst.b, n1.b], [PT[ub][0].b])
                                yield
                            Xc = [(NM1[base + u].t[:, 0:128], NM1[base + u].b, NT_[base + u].t[:], NT_[base + u].b) for u in range(len(units))]
                            Pc = [PT[base + u][0] for u in range(len(units))]
                            for j in range(1, 6):
                                banks = []
                                for u in range(len(units)):
                                    X, Xb, XTt, XTb = Xc[u]
                                    pb = PS()
                                    self.mm(pb.t[:, 0:128], X, XTt, True, True, [Xb, XTb], [pb.b])
                                    if j < 5:
                                        self.mm(pb.t[:, 128:256], XTt, X, True, True, [Xb, XTb], [pb.b])
                                    banks.append(pb)
                                    yield
                                for u in range(len(units)):
                                    pb = banks[u]
                                    xa = XA[base + u][j % 2]; xb_ = XB[base + u][j % 2]
                                    self.cp("act", xb_.t[:], pb.t[:, 0:128], [pb.b], [xb_.b])
                                    if j < 5:
                                        self.cp("act", xa.t[:], pb.t[:, 128:256], [pb.b], [xa.b])
                                    Xc[u] = (xa.t[:], xa.b, xb_.t[:], xb_.b)
                                yield
                                banks = []
                                for u in range(len(units)):
                                    pb = PS()
                                    self.mm(pb.t[:, 0:128], Xc[u][2], Pc[u].t[:], True, True, [Xc[u][3], Pc[u].b], [pb.b])
                                    banks.append(pb)
                                    yield
                                for u in range(len(units)):
                                    Pn = PT[base + u][j % 2]
                                    self.tt("dve", Pn.t[:], banks[u].t[:, 0:128], Pc[u].t[:], ALU.add, [banks[u].b, Pc[u].b], [Pn.b])
                                    Pc[u] = Pn
                                yield
                            res["units"] = units
                            res["Pc"] = Pc
                            res["base"] = base

                        def gen_chain(batch, res):
                            units = res["units"]; Pc = res["Pc"]; base = res["base"]
                            for pr in batch:
                                u0 = units.index((pr, 0))
                                Tinv = [Pc[u0], Pc[u0 + 1]]
                                N1 = [NM1[base + u0], NM1[base + u0 + 1]]
                                N2 = [NM2[base + u0], NM2[base + u0 + 1]]
                                chs = (1, 0) if rev else (0, 1)
                                for cc in chs:
                                    ck = (pr * 2 + cc)
                                    cr_ = slice(64 * cc, 64 * cc + 64)
                                    pbx = []
                                    for hh in range(2):
                                        hr_ = slice(64 * hh, 64 * hh + 64)
                                        pb = PS()
                                        self.mm(pb.t[:, 0:64], AR.t[hr_, pr, 0, :], ST.t[hr_, hr_], True, False, [AR.b, ST.b], [pb.b])
                                        self.mm(pb.t[:, 0:64], N2[hh].t[:, 0:128], VT.t[:, pr, hr_], False, True, [N2[hh].b, VT.b], [pb.b])
                                        pbx.append(pb)
                                    for hh in range(2):
                                        self.cp("act" if hh == 0 else "dve", Xs[hh].t[cr_, :], pbx[hh].t[cr_, 0:64], [pbx[hh].b], [Xs[hh].b])
                                    yield
                                    pbu = []
                                    for hh in range(2):
                                        pb = PS()
                                        self.mm(pb.t[:, 0:64], Tinv[hh].t[:], Xs[hh].t[:], True, True, [Tinv[hh].b, Xs[hh].b], [pb.b])
                                        pbu.append(pb)
                                    for hh in range(2):
                                        hr_ = slice(64 * hh, 64 * hh + 64)
                                        e1 = "act" if hh == 0 else "dve"
                                        self.cp(e1, UT.t[cr_, hr_], pbu[hh].t[cr_, 0:64], [pbu[hh].b], [UT.b])
                                        self.cp(e1, UP[hh].t[cr_, hr_], pbu[hh].t[cr_, 0:64], [pbu[hh].b], [UP[hh].b])
                                    yield
                                    yp = PS()
                                    self.mm(yp.t[:, 0:64], ST.t[:], AR.t[:, pr, 1, cr_], True, False, [ST.b, AR.b], [yp.b])
                                    for hh in range(2):
                                        self.mm(yp.t[:, 0:64], UP[hh].t[cr_, :], N1[hh].t[cr_, 128 + 64 * cc:128 + 64 * cc + 64], False, False,
                                                [UP[hh].b, N1[hh].b], [yp.b])
                                        self.mm(yp.t[:, 0:64], VP[hh].t[cr_, pr, :], N2[hh].t[cr_, 128 + 64 * cc:128 + 64 * cc + 64], False, hh == 1,
                                                [VP[hh].b, N2[hh].b], [yp.b])
                                    sp = PS()
                                    self.mm(sp.t[:, 0:128], BC.t[cr_, pr, :], UT.t[cr_, :], True, False, [BC.b, UT.b], [sp.b])
                                    self.mm(sp.t[:, 0:128], KC.t[cr_, pr, :], VT.t[cr_, pr, :], False, True, [KC.b, VT.b], [sp.b])
                                    g0 = s0 + pr * 128 + 64 * cc
                                    if not rev:
                                        self.cp("act", ytile.t[:, g0:g0 + 64], yp.t[:, 0:64], [yp.b], [ytile.b])
                                    else:
                                        self.tt("dve", ytile.t[:, g0:g0 + 64], yp.t[:, 0:64], ytile.t[:, g0:g0 + 64], ALU.add, [yp.b, ytile.b], [ytile.b])
                                    for hh in range(2):
                                        hr_ = slice(64 * hh, 64 * hh + 64)
                                        self.stt(ST.t[hr_, hr_], ST.t[hr_, hr_], Gc.t[hr_, ck:ck + 1], sp.t[hr_, hr_], ALU.mult, ALU.add,
                                                 [ST.b, Gc.b, sp.b], [ST.b])
                                    yield

                        def drain(g):
                            for _ in g:
                                pass

                        def merge(gc, gi, ratio=4):
                            ci_done = False; ii_done = False
                            while not (ci_done and ii_done):
                                if not ci_done:
                                    try:
                                        next(gc)
                                    except StopIteration:
                                        ci_done = True
                                for _ in range(ratio):
                                    if ii_done:
                                        break
                                    try:
                                        next(gi)
                                    except StopIteration:
                                        ii_done = True

                        wp = PS()
                        wn = min(n, 512)
                        for _ in range(6):
                            self.mm(wp.t[:, 0:wn], AR.t[:, 0, 0, :], BH.t[:, 0:wn], True, True, [AR.b, BH.b, KH.b, BC.b, KC.b, VT.b], [wp.b])
                        results = [dict() for _ in batches]
                        drain(gen_inv(0, batches[0], results[0]))
                        for bi, batch in enumerate(batches):
                            gc = gen_chain(batch, results[bi])
                            if bi + 1 < len(batches):
                                merge(gc, gen_inv(bi + 1, batches[bi + 1], results[bi + 1]))
                            else:
                                drain(gc)
                    if kind == "P":
                        for hh in range(2):
                            self.dma(self.o_rwf[l, pi_, d_, 2 * hp + hh], ST.t[64 * hh:64 * hh + 64, 64 * hh:64 * hh + 64], reads=[ST.b])
                for c0 in range(0, T, 512):
                    m = min(512, T - c0)
                    g = off + c0; ci = g // 512; o_in = g % 512
                    yv = ytile.t[:, c0:c0 + m]
                    pb = PS()
                    self.mm(pb.t[:, 0:m], c2.t[:, 1, :], yv, True, True, [c2.b, ytile.b], [pb.b])
                    self.tt("dve", yc.t[:, 0:m], yv, pb.t[:, 0:m], ALU.subtract, [ytile.b, pb.b], [yc.b])
                    self.tt("pool", ysq.t[:, 0:m], yc.t[:, 0:m], yc.t[:, 0:m], ALU.mult, [yc.b], [ysq.b])
                    pb = PS()
                    self.mm(pb.t[:, 0:m], c2.t[:, 1, :], ysq.t[:, 0:m], True, True, [c2.b, ysq.b], [pb.b])
                    self.act(yrs.t[:, 0:m], pb.t[:, 0:m], AF.Sqrt, [pb.b], [yrs.b], bias=self.eps_t.t[:, 1:2], scale=1.0)
                    self.recip(yrs.t[:, 0:m], yrs.t[:, 0:m], [yrs.b], [yrs.b])
                    self.tt("dve", yc.t[:, 0:m], yc.t[:, 0:m], yrs.t[:, 0:m], ALU.mult, [yc.b, yrs.b], [yc.b])
                    self.act(yc.t[:, 0:m], yc.t[:, 0:m], AF.Identity, [yc.b, vec.b], [yc.b], bias=V(7), scale=V(6))
                    load_mixed(r_m, hp, V(0), seq, c0, m)
                    load_mixed(k_m, 4 + hp, V(1), seq, c0, m)
                    load_mixed(v_m, 8 + hp, V(2), seq, c0, m)
                    self.stt(ysq.t[:, 0:m], r_m.t[:, 0:m], V(5), k_m.t[:, 0:m], ALU.mult, ALU.mult, [r_m.b, k_m.b, vec.b], [ysq.b])
                    pb = PS()
                    self.mm(pb.t[:, 0:m], c2.t[:, 0, :], ysq.t[:, 0:m], True, True, [c2.b, ysq.b], [pb.b])
                    self.tt("dve", bon.t[:, 0:m], pb.t[:, 0:m], v_m.t[:, 0:m], ALU.mult, [pb.b, v_m.b], [bon.b])
                    self.tt("pool", yc.t[:, 0:m], yc.t[:, 0:m], bon.t[:, 0:m], ALU.add, [yc.b, bon.b], [yc.b])
                    load_mixed(hw_m, 14, mu3.t[:, 2:3], seq, c0, m)
                    self.act(gsg.t[:, 0:m], hw_m.t[:, 0:m], AF.Sigmoid, [hw_m.b], [gsg.b])
                    pb = PS()
                    self.mm(pb.t[:, 0:m], g2.t[:, hp * 128:(hp + 1) * 128], gsg.t[:, 0:m], True, True, [g2.b, gsg.b], [pb.b])
                    o_ = osb[pk["n"] % 2]
                    self.tt("dve", o_.t[:, 0:m], pb.t[:, 0:m], yc.t[:, 0:m], ALU.mult, [pb.b, yc.b], [o_.b])
                    self.dma(self.OUTs[2][hp, ci, :, o_in:o_in + m], o_.t[:, 0:m], reads=[o_.b])
        self.phase_end()

    def cumsum(self, dst, src, n, tmp):
        nc = self.nc
        self.P.add("dve", lambda: nc.vector.tensor_tensor_scan(out=dst.t[:, 0:n], data0=self._ones_n(n),
                                                               data1=src.t[:, 0:n], initial=0.0, op0=ALU.mult, op1=ALU.add),
                   reads=[src.b, self.onesL.b], writes=[dst.b])

    def _ones_n(self, n):
        return self.onesL.t[:, 0:n]

    def phase_attn(self, l):
        self.phase_begin()
        lam_init = 0.8 - 0.6 * math.exp(-0.3 * l)
        lt = self.sb("lamraw", [128, 256])
        self.dma(lt.t[:], self.da_lam[l].partition_broadcast(128), writes=[lt.b])
        lp = self.sb("lamprod", [128, 2, 64])
        self.tt("dve", lp.t[:, 0, :], lt.t[:, 0:64], lt.t[:, 64:128], ALU.mult, [lt.b], [lp.b])
        self.tt("dve", lp.t[:, 1, :], lt.t[:, 128:192], lt.t[:, 192:256], ALU.mult, [lt.b], [lp.b])
        ls = self.sb("lamsum", [128, 2])
        nc = self.nc
        self.P.add("dve", lambda: nc.vector.reduce_sum(out=ls.t[:], in_=lp.t[:], axis=AX.X), reads=[lp.b], writes=[ls.b])
        le = self.sb("lamexp", [128, 2])
        self.act(le.t[:], ls.t[:], AF.Exp, [ls.b], [le.b])
        neglam = self.sb("neglam", [128, 1])
        self.tt("dve", neglam.t[:], le.t[:, 1:2], le.t[:, 0:1], ALU.subtract, [le.b], [neglam.b])
        self.ts("dve", neglam.t[:], neglam.t[:], -lam_init, None, ALU.add, None, [neglam.b], [neglam.b])
        won = self.sb("won", [128, 1])
        self.dma(won.t[:], self.da_on[l], writes=[won.b])
        self.ts("dve", won.t[:], won.t[:], 1.0 - lam_init, None, ALU.mult, None, [won.b], [won.b])
        onesb = self.sb("onesb", [128, 128], BF16)
        self.memset("pool", onesb.t[:], 1.0, [onesb.b])
        maxk = max((T + (256 if kind == "S" else 0)) for kind, off, T in self.seqs)
        KT = self.sb("KT", [128, maxk], BF16)
        VA = self.sb("VA", [128, maxk // 128, 512], BF16)
        ckf = self.sb("ckf", [128, 256])
        cvf = self.sb("cvf", [128, 2, 512])
        Qt = [self.sb("Qt%d" % i, [128, 512], BF16) for i in range(2)]
        Et = [self.sb("Et%d" % i, [128, 512], BF16) for i in range(3)]
        rl = self.sb("rl", [128, 512])
        Esum = self.sb("Esum", [128, 512])
        tm = [self.sb("tm%d" % i, [128, 512]) for i in range(2)]
        of = self.sb("of", [128, 512])
        osq = self.sb("osq", [128, 512], BF16)
        ors = self.sb("ors", [128, 512])
        ob = [self.sb("ob%d" % i, [128, 512], BF16) for i in range(2)]
        Sps = [self.ps[0], self.ps[1]]
        Ops = [self.ps[2], self.ps[3]]
        Lps = [self.ps[4], self.ps[5]]
        Mps = self.ps[6]
        cnt = {"s": 0, "e": 0, "q": 0, "o": 0}
        for (kind, off, T) in self.seqs:
            nkc0 = 2 if kind == "S" else 0
            nkc = nkc0 + T // 128
            if kind == "S":
                self.dma(cvf.t[:], self.cacheV[l].rearrange("k p e -> p k e"), writes=[cvf.b])
                self.cp("pool", VA.t[:, 0:2, :], cvf.t[:], [cvf.b], [VA.b])
            for g0 in range(0, T // 128, 4):
                n = min(4, T // 128 - g0)
                self.dma(VA.t[:, nkc0 + g0:nkc0 + g0 + n, :],
                         self.Vs[off + g0 * 128:off + (g0 + n) * 128, :].rearrange("(k p) e -> p k e", p=128), writes=[VA.b])
            for hb in range(4):
                if kind == "S":
                    self.dma(ckf.t[:], self.cacheKT[l, hb], writes=[ckf.b])
                    self.cp("pool", KT.t[:, 0:256], ckf.t[:], [ckf.b], [KT.b])
                t = 0
                while t < T:
                    ci = (off + t) // 512
                    o_in = (off + t) % 512
                    n = min(512 - o_in, T - t)
                    self.dma(KT.t[:, nkc0 * 128 + t:nkc0 * 128 + t + n], self.Ks[hb, ci, :, o_in:o_in + n], writes=[KT.b])
                    t += n
                qlen = min(512, T)
                for q0 in range(0, T, qlen):
                    ci = (off + q0) // 512
                    o_in = (off + q0) % 512
                    Q = Qt[cnt["q"] % 2]; cnt["q"] += 1
                    self.dma(Q.t[:, 0:qlen], self.Qs[hb, ci, :, o_in:o_in + qlen], writes=[Q.b])
                    if q0 == 0:
                        for _ in range(20):
                            self.mm(self.ps[7].t[:, 0:qlen], KT.t[:, 0:128], Q.t[:, 0:qlen], True, True, [KT.b, Q.b], [self.ps[7].b])
                    for m in range(2):
                        Op = Ops[m]
                        Lp = Lps[m]
                        def s_stage(kc_):
                            Sp_ = Sps[cnt["s"] % 2]; cnt["s"] += 1
                            self.mm(Sp_.t[:, 0:qlen], KT.t[64 * m:64 * m + 64, kc_ * 128:(kc_ + 1) * 128],
                                    Q.t[64 * m:64 * m + 64, 0:qlen], True, True, [KT.b, Q.b], [Sp_.b])
                            E_ = Et[cnt["e"] % 3]; cnt["e"] += 1
                            self.act(E_.t[:, 0:qlen], Sp_.t[:, 0:qlen], AF.Exp, [Sp_.b], [E_.b], scale=0.125)
                            return E_

                        Ecur = s_stage(0)
                        for kc in range(nkc):
                            Enext = s_stage(kc + 1) if kc + 1 < nkc else None
                            E = Ecur
                            self.mm(Op.t[:, 0:qlen], VA.t[:, kc, hb * 128:(hb + 1) * 128], E.t[:, 0:qlen],
                                    kc == 0, kc == nkc - 1, [VA.b, E.b], [Op.b])
                            if kc == 0:
                                self.cp("dve", Esum.t[:, 0:qlen], E.t[:, 0:qlen], [E.b], [Esum.b])
                            else:
                                self.tt("dve", Esum.t[:, 0:qlen], Esum.t[:, 0:qlen], E.t[:, 0:qlen], ALU.add, [Esum.b, E.b], [Esum.b])
                            Ecur = Enext
                        self.mm(Lp.t[:, 0:qlen], self.onesf.t[:, 0:128], Esum.t[:, 0:qlen], True, True, [self.onesf.b, Esum.b], [Lp.b])
                        self.recip(rl.t[:, 0:qlen], Lp.t[:, 0:qlen], [Lp.b], [rl.b])
                        self.tt("dve", tm[m].t[:, 0:qlen], Op.t[:, 0:qlen], rl.t[:, 0:qlen], ALU.mult, [Op.b, rl.b], [tm[m].b])
                    self.stt(of.t[:, 0:qlen], tm[1].t[:, 0:qlen], neglam.t[:, 0:1], tm[0].t[:, 0:qlen], ALU.mult, ALU.add,
                             [tm[0].b, tm[1].b, neglam.b], [of.b])
                    self.act(osq.t[:, 0:qlen], of.t[:, 0:qlen], AF.Square, [of.b], [osq.b])
                    self.mm(Mps.t[:, 0:qlen], self.ones128, osq.t[:, 0:qlen], True, True, [osq.b, self.cstb.b], [Mps.b])
                    self.act(ors.t[:, 0:qlen], Mps.t[:, 0:qlen], AF.Sqrt, [Mps.b], [ors.b], bias=self.eps_t.t[:, 0:1], scale=1.0)
                    self.recip(ors.t[:, 0:qlen], ors.t[:, 0:qlen], [ors.b], [ors.b])
                    o_ = ob[cnt["o"] % 2]; cnt["o"] += 1
                    self.stt(o_.t[:, 0:qlen], of.t[:, 0:qlen], won.t[:, 0:1], ors.t[:, 0:qlen], ALU.mult, ALU.mult,
                             [of.b, won.b, ors.b], [o_.b])
                    self.dma(self.OUTs[0][hb, ci, :, o_in:o_in + qlen], o_.t[:, 0:qlen], reads=[o_.b])
        self.phase_end()

    def phase_C1(self, l, src_ap):
        self.phase_begin()
        xins = [self.sb("xin%d" % i, [128, NJ, 512]) for i in range(2)]
        sq = self.sb("sq", [128, NJ, 512], BF16)
        rstd = self.sb("rstd", [128, 512])
        h2 = self.sb("h2", [128, NJ, 512], BF16)
        oin = self.sb("oin", [128, 12, 512], BF16)
        mg = self.sb("mg", [128, NJ, 512], BF16)
        gts = [self.sb("gt%d" % i, [128, 3, 512], BF16) for i in range(2)]
        m1 = [self.sb("m1_%d" % i, [128, 512]) for i in range(3)]
        wu_f = [self.sb("wuf%d" % i, [128, 12, 128]) for i in range(2)]
        wu_b = [self.sb("wub%d" % i, [128, 12, 128], BF16) for i in range(2)]
        wo_f = [self.sb("wof%d" % i, [128, 16, 128]) for i in range(2)]
        wo_b = [self.sb("wob%d" % i, [128, 16, 128], BF16) for i in range(2)]
        ups = [self.ps[0], self.ps[1], self.ps[2], self.ps[3], self.ps[4], self.ps[5]]
        normps = self.ps[6]
        ops_ = [self.ps[7], self.ps[6]]
        k = 0
        for ci in range(self.NCH):
            row = self.chunk_row[ci]
            xin = xins[ci % 2]
            if ci == 0:
                self.dma(xin.t[:], src_ap[ci], writes=[xin.b])
                for n in range(3):
                    self.dma(oin.t[:, n * 4:(n + 1) * 4, :], self.OUTs[n][:, ci].rearrange("c p t -> p c t"), writes=[oin.b])
            for ob in range(16):
                wf = wu_f[ob % 2]; wb = wu_b[ob % 2]
                self.dma(wf.t[:], self.w_up[l, ob], writes=[wf.b])
                self.cp("dve", wb.t[:], wf.t[:], [wf.b], [wb.b])
                gt = gts[ob % 2]
                self.dma(gt.t[:], self.Gs[ob:48:16, ci].rearrange("n p t -> p n t"), writes=[gt.b])
                for n in range(3):
                    pb = ups[(ob % 2) * 3 + n]
                    for kj in range(4):
                        self.mm(pb.t[:], wb.t[:, n * 4 + kj, :], oin.t[:, n * 4 + kj, :], kj == 0, kj == 3, [wb.b, oin.b], [pb.b])
                    self.tt("dve", m1[n].t[:], pb.t[:], gt.t[:, n, :], ALU.mult, [pb.b, gt.b], [m1[n].b])
                self.tt("dve", m1[0].t[:], m1[0].t[:], m1[1].t[:], ALU.add, [m1[0].b, m1[1].b], [m1[0].b])
                self.tt("pool", mg.t[:, ob, :], m1[0].t[:], m1[2].t[:], ALU.add, [m1[0].b, m1[2].b], [mg.b])
            if ci + 1 < self.NCH:
                xn = xins[(ci + 1) % 2]
                self.dma(xn.t[:], src_ap[ci + 1], writes=[xn.b])
                for n in range(3):
                    self.dma(oin.t[:, n * 4:(n + 1) * 4, :], self.OUTs[n][:, ci + 1].rearrange("c p t -> p c t"), writes=[oin.b])
            for ob in range(16):
                wf = wo_f[ob % 2]; wb = wo_b[ob % 2]
                self.dma(wf.t[:], self.w_out[l, ob], writes=[wf.b])
                self.cp("act", wb.t[:], wf.t[:], [wf.b], [wb.b])
                pb = ops_[ob % 2]
                for kj in range(NJ):
                    self.mm(pb.t[:], wb.t[:, kj, :], mg.t[:, kj, :], kj == 0, kj == NJ - 1, [wb.b, mg.b], [pb.b])
                self.stt(xin.t[:, ob, :], pb.t[:], self.modT.t[:, 32 + ob, row:row + 1], xin.t[:, ob, :], ALU.mult, ALU.add,
                         [pb.b, self.modT.b, xin.b], [xin.b])
            self.dma(self.X1s[ci], xin.t[:], reads=[xin.b])
            self.norm_core(xin, h2, 0, row, 1, 48, sq, rstd, normps)
            self.dma(self.H2s[ci], h2.t[:], reads=[h2.b])
        self.phase_end()

    def phase_C2(self, l, dst_ap):
        self.phase_begin()
        SC = 2
        h2 = self.sb("h2", [128, NJ, SC * 512], BF16)
        act = self.sb("actT", [128, 44, SC * 512], BF16)
        wf = [self.sb("wf%d" % i, [128, 44, 128]) for i in range(2)]
        wb = [self.sb("wb%d" % i, [128, 44, 128], BF16) for i in range(2)]
        sg = [self.sb("sg%d" % i, [128, 512]) for i in range(2)]
        x1 = [self.sb("x1_%d" % i, [128, 512]) for i in range(3)]
        gps = [self.ps[0], self.ps[1]]
        ups = [self.ps[2], self.ps[3]]
        ops_ = [self.ps[4], self.ps[5], self.ps[6]]
        k = 0
        kx = 0
        for sc0 in range(0, self.NCH, SC):
            chunks = list(range(sc0, min(self.NCH, sc0 + SC)))
            for hs, ci in enumerate(chunks):
                self.dma(h2.t[:, :, hs * 512:(hs + 1) * 512], self.H2s[ci], writes=[h2.b])
            for fb in range(44):
                f = wf[fb % 2]; b = wb[fb % 2]
                self.dma(f.t[:, 0:32, :], self.w_ffn_in[l, fb], writes=[f.b])
                self.cp("dve" if fb % 2 == 0 else "act", b.t[:, 0:32, :], f.t[:, 0:32, :], [f.b], [b.b])
                for hs, ci in enumerate(chunks):
                    gp = gps[k % 2]; up = ups[k % 2]; s_ = sg[k % 2]; k += 1
                    for kj in range(NJ):
                        self.mm(gp.t[:], b.t[:, kj, :], h2.t[:, kj, hs * 512:(hs + 1) * 512], kj == 0, kj == NJ - 1, [b.b, h2.b], [gp.b])
                    for kj in range(NJ):
                        self.mm(up.t[:], b.t[:, 16 + kj, :], h2.t[:, kj, hs * 512:(hs + 1) * 512], kj == 0, kj == NJ - 1, [b.b, h2.b], [up.b])
                    self.act(s_.t[:], gp.t[:], AF.Silu, [gp.b], [s_.b])
                    self.tt("dve", act.t[:, fb, hs * 512:(hs + 1) * 512], up.t[:], s_.t[:], ALU.mult, [up.b, s_.b], [act.b])
            def fo_load(ob_):
                self.dma(wf[ob_ % 2].t[:], self.w_ffn_out[l, ob_], writes=[wf[ob_ % 2].b])

            def fo_conv(ob_):
                self.cp("dve" if ob_ % 2 == 0 else "act", wb[ob_ % 2].t[:], wf[ob_ % 2].t[:], [wf[ob_ % 2].b], [wb[ob_ % 2].b])

            fo_load(0)
            fo_conv(0)
            for ob in range(16):
                f = wf[ob % 2]; b = wb[ob % 2]
                if ob + 1 < 16:
                    fo_load(ob + 1)
                    fo_conv(ob + 1)
                for hs, ci in enumerate(chunks):
                    row = self.chunk_row[ci]
                    pb = ops_[kx % 3]; xt = x1[kx % 3]; kx += 1
                    self.dma(xt.t[:], self.X1s[ci, :, ob, :], writes=[xt.b])
                    for kj in range(44):
                        self.mm(pb.t[:], b.t[:, kj, :], act.t[:, kj, hs * 512:(hs + 1) * 512], kj == 0, kj == 43, [b.b, act.b], [pb.b])
                    self.stt(xt.t[:], pb.t[:], self.modT.t[:, 80 + ob, row:row + 1], xt.t[:], ALU.mult, ALU.add,
                             [pb.b, self.modT.b, xt.b], [xt.b])
                    self.dma(dst_ap[ci, :, ob, :], xt.t[:], reads=[xt.b])
        self.phase_end()

    def build(self):
        self.declare_io()
        self.load_consts()
        self.eps_t = self.sb("eps", [128, 2], persistent=True)
        self.memset("pool", self.eps_t.t[:, 0:1], RMS_EPS, [self.eps_t.b])
        self.memset("pool", self.eps_t.t[:, 1:2], GN_EPS, [self.eps_t.b])
        self.modT_l = [self.sb("modT%d" % l, [128, 96, 2], persistent=True) for l in range(self.L)]
        self.modA_l = [self.sb("modA%d" % l, [128, 2, NJ, 2], persistent=True) for l in range(self.L)]
        stop = self.cfg.get("stop")
        for l in range(self.L):
            self.phase_mod(l)
            if stop == "mod":
                o = self.dram_out("dbg_mod", [128, 192])
                self.dma(o[:, :], self.modT_l[l].t[:].rearrange("p c r -> p (c r)"), reads=[self.modT_l[l].b], writes=[self.dbufs["dbg_mod"]])
                o2 = self.dram_out("dbg_A", [128, 64])
                self.dma(o2[:, :], self.modA_l[l].t[:].rearrange("p a j r -> p (a j r)"), reads=[self.modA_l[l].b], writes=[self.dbufs["dbg_A"]])
                break
            src = self.xL if l == 0 else self.Xs[l - 1]
            dst = self.o_yL if l == self.L - 1 else self.Xs[l]
            self.phase_A(l, src)
            if stop == "A":
                break
            self.phase_attn(l)
            stub = self.cfg.get("stub", "none")
            if stub == "zero":
                self.phase_zero_branch(1)
            else:
                self.phase_s5(l)
            if stub in ("zero", "s5"):
                self.phase_zero_branch(2)
            else:
                self.phase_rwkv(l)
            self.phase_C1(l, src)
            self.phase_C2(l, dst)
        n = self.P.emit()
        return n


def _consts():
    c = np.zeros((128, 512), np.float32)
    c[:, 0:128] = np.eye(128, dtype=np.float32)
    bo = np.zeros((128, 128), np.float32)
    bo[0:64, 0:64] = 1.0 / 64
    bo[64:128, 64:128] = 1.0 / 64
    c[:, 128:256] = bo
    c[:, 256:384] = 1.0 / 128
    PT = np.zeros((128, 128), np.float32)
    for m in range(128):
        if m % 32 < 16:
            PT[m + 16, m] = -1.0
        else:
            PT[m - 16, m] = 1.0
    c[:, 384:512] = PT
    return c


def _rope_tables():
    t = np.arange(4096)
    row = (t // 64).astype(np.float32)
    col = (t % 64).astype(np.float32)
    n_freq = 16
    inv_freq = (10000.0 ** (-np.arange(n_freq, dtype=np.float32) / n_freq)).astype(np.float32)
    C = np.zeros((128, 4096), np.float32)
    S = np.zeros((128, 4096), np.float32)
    for p in range(128):
        d = p % 64
        pos = row if d < 32 else col
        ang = (pos * inv_freq[d % 16]).astype(np.float32)
        C[p] = np.cos(ang)
        S[p] = np.sin(ang)
    return C, S


def _st_layout(a):
    sh = a.shape[:-2]
    k = len(sh)
    return a.reshape(sh + (16, 2, 64)).transpose(tuple(range(k)) + (k + 1, k + 2, k)).reshape(sh + (128, 16))


def per_core_state(inp, b, m):
    f = np.float32
    ck = np.asarray(inp["cache_attn_kv"][b], dtype=f)
    m["cacheKT"] = np.ascontiguousarray(ck[:, :, 0].transpose(0, 2, 3, 1))
    m["cacheV"] = np.ascontiguousarray(ck[:, :, 1].reshape(DEPTH, 2, 128, 512))
    h0 = np.asarray(inp["state_s5"][b], dtype=f)
    m["s5h0"] = np.ascontiguousarray(_st_layout(h0).transpose(0, 3, 1, 2, 4))
    s0 = np.asarray(inp["state_rwkv"][b], dtype=f)
    m["rwS0"] = np.ascontiguousarray(s0.transpose(0, 1, 2, 4, 3))


PER_CORE_KEYS = ("xL", "cT", "cacheKT", "cacheV", "s5h0", "rwS0")


def make_in_map(inp, core, seqs_spec, shared=None):
    f = np.float32
    if shared is not None:
        m = {k: v for k, v in shared.items() if k not in PER_CORE_KEYS}
        xs = [np.asarray(inp["x_sample"][b] if kind == "S" else inp["x_prompt"][b]).T for kind, b in seqs_spec]
        xT = np.concatenate(xs, axis=1)
        nch = xT.shape[1] // 512
        m["xL"] = np.ascontiguousarray(xT.reshape(NJ, 128, nch, 512).transpose(2, 1, 0, 3), dtype=f)
        sb = [b for k, b in seqs_spec if k == "S"]
        cb = np.asarray(inp["c"][sb[0]] if sb else inp["c"][0])
        m["cT"] = np.ascontiguousarray(np.stack([cb, np.asarray(inp["c_ctx"])], axis=1).reshape(NJ, 128, 2).transpose(1, 0, 2), dtype=f)
        if sb:
            per_core_state(inp, sb[0], m)
        return m
    xs = []
    for kind, b in seqs_spec:
        xs.append(np.asarray(inp["x_sample"][b] if kind == "S" else inp["x_prompt"][b]).T)
    m = {}
    xT = np.concatenate(xs, axis=1)
    nch = xT.shape[1] // 512
    m["xL"] = np.ascontiguousarray(xT.reshape(NJ, 128, nch, 512).transpose(2, 1, 0, 3), dtype=f)
    sb = [b for k, b in seqs_spec if k == "S"]
    cb = np.asarray(inp["c"][sb[0]] if sb else inp["c"][0])
    m["cT"] = np.ascontiguousarray(np.stack([cb, np.asarray(inp["c_ctx"])], axis=1).reshape(NJ, 128, 2).transpose(1, 0, 2), dtype=f)
    m["w_mod"] = np.ascontiguousarray(
        np.asarray(inp["w_mod"], dtype=f).reshape(DEPTH, NJ, 128, 24, 512).transpose(0, 3, 2, 1, 4))
    m["bmodT"] = np.ascontiguousarray(np.asarray(inp["b_mod"]).reshape(DEPTH, 96, 128).transpose(0, 2, 1), dtype=f)
    m["nmixT"] = np.ascontiguousarray(np.asarray(inp["norm_mix"]).reshape(DEPTH, NJ, 128).transpose(0, 2, 1), dtype=f)
    m["nffnT"] = np.ascontiguousarray(np.asarray(inp["norm_ffn"]).reshape(DEPTH, NJ, 128).transpose(0, 2, 1), dtype=f)
    wi = np.zeros((DEPTH, D_MODEL, 40 * 256), f)
    wi[:, :, :N_IN] = np.asarray(inp["w_in"], dtype=f)
    m["w_in"] = np.ascontiguousarray(wi.reshape(DEPTH, NJ, 128, 40, 256).transpose(0, 3, 2, 1, 4))
    qw = np.tile(np.asarray(inp["da_q_norm"]), (1, 2))
    kw = np.tile(np.asarray(inp["da_k_norm"]), (1, 2))
    m["qkw"] = np.ascontiguousarray(np.stack([qw, kw], axis=2), dtype=f)
    wu = np.asarray(inp["w_up"], dtype=f)
    m["w_up"] = np.ascontiguousarray(wu.reshape(DEPTH, 3, 4, 128, 16, 128).transpose(0, 4, 3, 1, 2, 5).reshape(DEPTH, 16, 128, 12, 128))
    wo = np.asarray(inp["w_out"], dtype=f)
    m["w_out"] = np.ascontiguousarray(wo.reshape(DEPTH, 16, 128, 16, 128).transpose(0, 3, 2, 1, 4))
    wfi = np.asarray(inp["w_ffn_in"], dtype=f)
    m["w_ffn_in"] = np.ascontiguousarray(
        wfi.reshape(DEPTH, 16, 128, 2, 44, 128).transpose(0, 4, 2, 3, 1, 5).reshape(DEPTH, 44, 128, 32, 128))
    wfo = np.asarray(inp["w_ffn_out"], dtype=f)
    m["w_ffn_out"] = np.ascontiguousarray(wfo.reshape(DEPTH, 44, 128, 16, 128).transpose(0, 3, 2, 1, 4))
    m["da_lam"] = np.ascontiguousarray(np.asarray(inp["da_lambda"], dtype=f).reshape(DEPTH, 1, 256))
    m["da_on"] = np.ascontiguousarray(np.asarray(inp["da_out_norm"], dtype=f).reshape(DEPTH, 128, 1))
    if sb:
        per_core_state(inp, sb[0], m)
    def st_layout(a):
        sh = a.shape[:-2]
        return a.reshape(sh + (16, 2, 64)).transpose(tuple(range(len(sh))) + (len(sh) + 1, len(sh) + 2, len(sh))).reshape(sh + (128, 16))
    lre = st_layout(np.asarray(inp["s5_lam_re"], dtype=f))
    lim = st_layout(np.asarray(inp["s5_lam_im"], dtype=f))
    lst = st_layout(np.broadcast_to(np.asarray(inp["s5_log_step"], dtype=f)[..., None], (DEPTH, 2, 32, 64)))
    m["s5p"] = np.ascontiguousarray(np.stack([lre, lim, lst], axis=3))
    def bc_layout(a):
        return a.reshape(DEPTH, 2, 16, 2, 64, 16).transpose(0, 1, 3, 4, 2, 5).reshape(DEPTH, 2, 128, 16, 16)
    bre = bc_layout(np.asarray(inp["s5_b_re"], dtype=f)); bim = bc_layout(np.asarray(inp["s5_b_im"], dtype=f))
    m["s5b"] = np.ascontiguousarray(np.stack([bre, bim], axis=3))
    cre = bc_layout(np.asarray(inp["s5_c_re"], dtype=f).transpose(0, 1, 2, 4, 3))
    cim = bc_layout(np.asarray(inp["s5_c_im"], dtype=f).transpose(0, 1, 2, 4, 3))
    m["s5c"] = np.ascontiguousarray(np.stack([cre, cim], axis=3))
    m["s5d"] = np.ascontiguousarray(np.asarray(inp["s5_d"], dtype=f).reshape(DEPTH, 4, 128).transpose(0, 2, 1))
    m["s5glu"] = np.ascontiguousarray(np.asarray(inp["s5_w_glu"], dtype=f).reshape(DEPTH, 4, 128, 512).transpose(0, 2, 1, 3))
    def hp_layout(a):
        return np.asarray(a, dtype=f).reshape(DEPTH, 4, 128).transpose(0, 2, 1)
    mu = np.asarray(inp["rw_mu"], dtype=f)
    vec = np.stack([hp_layout(mu[:, 0:512]), hp_layout(mu[:, 512:1024]), hp_layout(mu[:, 1024:1536]),
                    hp_layout(inp["rw_k_k"]), hp_layout(inp["rw_k_a"]), hp_layout(np.asarray(inp["rw_r_k"]).reshape(DEPTH, 512)),
                    hp_layout(inp["rw_ln_w"]), hp_layout(inp["rw_ln_b"]),
                    hp_layout(inp["rw_ln_b"]), hp_layout(inp["rw_ln_b"])], axis=3)
    m["rw_vec"] = np.ascontiguousarray(vec)
    w0 = np.asarray(inp["rw_w0"], dtype=f).reshape(DEPTH, 2, 4, 128).transpose(0, 3, 2, 1)
    a0 = np.asarray(inp["rw_a0"], dtype=f).reshape(DEPTH, 2, 4, 128).transpose(0, 3, 2, 1)
    m["rw_wa"] = np.ascontiguousarray(np.stack([w0, a0], axis=4))
    m["rw_mu3"] = np.ascontiguousarray(mu[:, 1536:1920].reshape(DEPTH, 3, 128).transpose(0, 2, 1))
    m["rw_w2"] = np.ascontiguousarray(np.asarray(inp["rw_w2"], dtype=f).reshape(DEPTH, 128, 512))
    m["rw_a2"] = np.ascontiguousarray(np.asarray(inp["rw_a2"], dtype=f).reshape(DEPTH, 128, 512))
    m["rw_g2"] = np.ascontiguousarray(np.asarray(inp["rw_g2"], dtype=f))
    ii = np.arange(128)[:, None]; tt_ = np.arange(128)[None, :]
    same = (ii // 64) == (tt_ // 64)
    mk = np.stack([(ii < tt_) & same, (ii <= tt_) & same, (ii > tt_) & same, (ii >= tt_) & same], axis=1).astype(f)
    m["rw_masks"] = np.ascontiguousarray(mk)
    m["rw_c2"] = np.ascontiguousarray(np.stack([same.astype(f), same.astype(f) / 64.0], axis=1))
    m["consts"] = _consts()
    C, S = _rope_tables()
    m["ropeC"] = C
    m["ropeS"] = S
    return m


FULL_SEQS = [("S", 0, 4096), ("P", 4096, 256), ("P", 4352, 256)]


def assemble(results, n_cores=8):
    f = np.float32
    y_prompt = np.zeros((16, 256, D_MODEL), f)
    y_sample = np.zeros((8, 4096, D_MODEL), f)
    new_kv = np.zeros((16, DEPTH, 256, 2, 4, 128), f)
    new_s5 = np.zeros((16, DEPTH, 2, 2, 32, 64), f)
    new_rw = np.zeros((16, DEPTH, 2, 8, 64, 64), f)
    for c in range(n_cores):
        r = results[c]
        yL = np.asarray(r["yL"])
        y = yL.transpose(0, 3, 2, 1).reshape(-1, D_MODEL)
        y_sample[c] = y[:4096]
        kT = np.asarray(r["kT"]); vt = np.asarray(r["vtok"])
        s5f = np.asarray(r["s5f"]); rwf = np.asarray(r["rwf"])
        for i in range(2):
            b = 2 * c + i
            y_prompt[b] = y[4096 + i * 256:4096 + (i + 1) * 256]
            for l in range(DEPTH):
                new_kv[b, l, :, 0] = kT[l, :, i * 256:(i + 1) * 256].T.reshape(256, 4, 128)
                new_kv[b, l, :, 1] = vt[l, i * 256:(i + 1) * 256].reshape(256, 4, 128)
                g = s5f[l][:, i].reshape(2, 64, 2, 2, 16)
                new_s5[b, l] = g.transpose(2, 3, 4, 0, 1).reshape(2, 2, 32, 64)
                new_rw[b, l] = rwf[l, i].transpose(0, 1, 3, 2)
    return (y_prompt, y_sample, new_kv, new_s5, new_rw)


_SHARED = {}


def kernel(**inputs):
    inp = {k: np.asarray(v) for k, v in inputs.items()}
    cfg = {"seqs": FULL_SEQS, "layers": DEPTH}
    B = Builder(cfg)
    B.build()
    in_maps = []
    shared = None
    for c in range(8):
        m = make_in_map(inp, c, [("S", c), ("P", 2 * c), ("P", 2 * c + 1)], shared)
        if shared is None:
            shared = m
        in_maps.append(m)
    res = run_bass_kernel_spmd(B.nc, in_maps, core_ids=list(range(8)))
    return assemble(res.results)
```

```python
import math
from contextlib import ExitStack
from functools import partial

import numpy as np
import concourse.bass as bass
import concourse.mybir as mybir
from concourse.bass_utils import run_bass_kernel_spmd

F32 = mybir.dt.float32
BF16 = mybir.dt.bfloat16
AF = mybir.ActivationFunctionType
ALU = mybir.AluOpType
AX = mybir.AxisListType

D_MODEL = 2048
DEPTH = 2
BW = 512
N_IN = 10112
D_FF = 5632
OFF_S5 = 1536
OFF_RW = 2048
OFF_GATE = 3968
RW_COLS = 1920
RMS_EPS = 1e-6
GN_EPS = 64e-5
L2_EPS = 1e-12
NJ = D_MODEL // 128
SAME_ENGINE_SYNC = True


class Buf:
    __slots__ = ("name", "last_w", "readers", "excl")

    def __init__(self, name, excl=False):
        self.name = name
        self.last_w = None
        self.readers = []
        self.excl = excl


class Op:
    __slots__ = ("eng", "fn", "deps", "need_inc", "semkey", "val", "dma", "idx", "barrier")


class Prog:
    ENGS = ("pe", "act", "dve", "pool", "sp")

    def __init__(self, nc, n_dma_sems=8, same_engine_sync=True):
        self.nc = nc
        self.ops = []
        self.same = same_engine_sync
        self.n_dma_sems = n_dma_sems
        self.eng_obj = {"pe": nc.tensor, "act": nc.scalar, "dve": nc.vector,
                        "pool": nc.gpsimd, "sp": nc.sync}
        self.last_op = {}

    def add(self, eng, fn, reads=(), writes=(), dma=False):
        op = Op()
        op.eng = eng
        op.fn = fn
        op.dma = dma
        op.need_inc = False
        op.val = None
        op.semkey = None
        op.barrier = False
        op.idx = len(self.ops)
        ex = [b for b in reads if b.excl]
        if ex:
            reads = [b for b in reads if not b.excl]
            writes = list(writes) + ex
        deps = {}
        for b in reads:
            if b.last_w is not None:
                deps[b.last_w.idx] = b.last_w
        for b in writes:
            if b.last_w is not None:
                deps[b.last_w.idx] = b.last_w
            for r in b.readers:
                deps[r.idx] = r
        dl = []
        for d in deps.values():
            if d.eng == eng and not d.dma:
                if eng == "pe" or not self.same:
                    continue
            d.need_inc = True
            dl.append(d)
        op.deps = dl
        for b in reads:
            b.readers.append(op)
        for b in writes:
            b.last_w = op
            b.readers = []
        self.ops.append(op)
        if not dma:
            self.last_op[eng] = op
        return op

    def barrier(self):
        for e, o in self.last_op.items():
            o.need_inc = True
        op = Op()
        op.barrier = True
        op.idx = len(self.ops)
        op.deps = []
        op.dma = False
        op.eng = None
        op.need_inc = False
        op.val = None
        op.semkey = None
        self.ops.append(op)

    def emit(self):
        nc = self.nc
        sems = {}

        def getsem(key):
            if key not in sems:
                c = nc.semaphore("s_" + "_".join(str(k) for k in key))
                sems[key] = c.__enter__()
            return sems[key]

        cnt = {}
        seen = {e: {} for e in self.ENGS}
        dma_n = {e: 0 for e in self.ENGS}
        dma_cnt = {}
        for op in self.ops:
            if op.barrier:
                tgt = {}
                for k, v in cnt.items():
                    tgt[k] = v
                for k, n in dma_cnt.items():
                    tgt[k] = 16 * n
                for e in self.ENGS:
                    eo = self.eng_obj[e]
                    for k, v in tgt.items():
                        if k == ("e", e):
                            continue
                        if seen[e].get(k, 0) >= v:
                            continue
                        eo.wait_ge(getsem(k), v)
                        seen[e][k] = v
                continue
            e = op.eng
            eo = self.eng_obj[e]
            need = {}
            for d in op.deps:
                assert d.val is not None, "dep not emitted"
                if need.get(d.semkey, 0) < d.val:
                    need[d.semkey] = d.val
            if op.dma:
                slot = dma_n[e] % self.n_dma_sems
                dma_n[e] += 1
                dkey = ("d", e, slot)
                prev = dma_cnt.get(dkey, 0)
                if prev > 0 and need.get(dkey, 0) < 16 * prev:
                    need[dkey] = 16 * prev
            for key, v in need.items():
                if seen[e].get(key, 0) >= v:
                    continue
                eo.wait_ge(getsem(key), v)
                seen[e][key] = v
            ins = op.fn()
            if op.dma:
                dma_cnt[dkey] = prev + 1
                op.semkey = dkey
                op.val = 16 * (prev + 1)
                ins.then_inc(getsem(dkey), 16)
            elif op.need_inc:
                key = ("e", e)
                cnt[key] = cnt.get(key, 0) + 1
                op.semkey = key
                op.val = cnt[key]
                ins.then_inc(getsem(key), 1)
        eo = self.eng_obj["sp"]
        for key, n in dma_cnt.items():
            eo.wait_ge(getsem(key), 16 * n)
        return len(self.ops)


class Tl:
    __slots__ = ("t", "b")

    def __init__(self, t, name, excl=False):
        self.t = t
        self.b = Buf(name, excl)


class Builder:
    def __init__(self, cfg):
        self.cfg = cfg
        self.nc = bass.Bass("TRN2", target_bir_lowering=False)
        self.P = Prog(self.nc, same_engine_sync=cfg.get("same", SAME_ENGINE_SYNC))
        self.es = ExitStack()
        self.pes = None
        self.uid = 0
        self.dbufs = {}
        self.outs = []
        nc = self.nc
        self.seqs = cfg["seqs"]
        self.NT = sum(s[2] for s in self.seqs)
        assert self.NT % 512 == 0
        self.NCH = self.NT // 512
        self.chunk_row = []
        self.chunk_rope = []
        for ci in range(self.NCH):
            t0 = ci * 512
            for (kind, off, T) in self.seqs:
                if off <= t0 < off + T:
                    self.chunk_row.append(0 if kind == "S" else 1)
                    self.chunk_rope.append((t0 - off) if kind == "S" else None)
        self.L = cfg["layers"]
        self.ps = [Tl(self.es.enter_context(nc.psum_tensor("ps%d" % i, [128, 512], F32)), "ps%d" % i, True)
                   for i in range(8)]

    def dram_in(self, name, shape, dt=F32):
        t = self.nc.dram_tensor(name, list(shape), dt, kind="ExternalInput")
        self.dbufs[name] = Buf(name)
        return t.ap()

    def dram_out(self, name, shape, dt=F32):
        t = self.nc.dram_tensor(name, list(shape), dt, kind="ExternalOutput")
        self.dbufs[name] = Buf(name)
        self.outs.append(name)
        return t.ap()

    def dram_tmp(self, name, shape, dt=F32):
        t = self.nc.dram_tensor(name, list(shape), dt, kind="Internal")
        return t.ap()

    def sb(self, name, shape, dt=F32, persistent=False):
        self.uid += 1
        nm = "%s_%d" % (name, self.uid)
        st = self.es if persistent else self.pes
        t = st.enter_context(self.nc.sbuf_tensor(nm, list(shape), dt))
        return Tl(t, nm)

    def phase_begin(self):
        self.pes = ExitStack()

    def phase_end(self):
        self.P.barrier()
        self.pes.close()
        self.pes = None

    def dma(self, out_ap, in_ap, reads=(), writes=(), q="sp", slow=False):
        nc = self.nc
        eng = {"sp": nc.sync, "pool": nc.gpsimd, "act": nc.scalar}[q]
        if slow:
            self.P.add(q, lambda: eng.dma_start(out=out_ap, in_=in_ap, allow_slow_non_contiguous=True),
                       reads=reads, writes=writes, dma=True)
        else:
            self.P.add(q, lambda: eng.dma_start(out=out_ap, in_=in_ap), reads=reads, writes=writes, dma=True)

    def mm(self, out_ap, lhsT, rhs, start, stop, reads, writes):
        nc = self.nc
        self.P.add("pe", lambda: nc.tensor.matmul(out_ap, lhsT=lhsT, rhs=rhs, start=start, stop=stop),
                   reads=reads, writes=writes)

    def tr(self, out_ap, in_ap, ident, reads, writes):
        nc = self.nc
        self.P.add("pe", lambda: nc.tensor.transpose(out=out_ap, in_=in_ap, identity=ident),
                   reads=reads, writes=writes)

    def act(self, out_ap, in_ap, func, reads, writes, bias=None, scale=None):
        nc = self.nc
        kw = {}
        if bias is not None:
            kw["bias"] = bias
        if scale is not None:
            kw["scale"] = scale
        self.P.add("act", lambda: nc.scalar.activation(out=out_ap, in_=in_ap, func=func, **kw),
                   reads=reads, writes=writes)

    def _veng(self, e):
        return self.nc.vector if e == "dve" else self.nc.gpsimd

    def tt(self, e, out_ap, in0, in1, op, reads, writes):
        en = self._veng(e)
        self.P.add(e, lambda: en.tensor_tensor(out=out_ap, in0=in0, in1=in1, op=op), reads=reads, writes=writes)

    def ts(self, e, out_ap, in0, s1, s2, op0, op1, reads, writes):
        en = self._veng(e)
        if op1 is None:
            self.P.add(e, lambda: en.tensor_scalar(out=out_ap, in0=in0, scalar1=s1, scalar2=None, op0=op0),
                       reads=reads, writes=writes)
        else:
            self.P.add(e, lambda: en.tensor_scalar(out=out_ap, in0=in0, scalar1=s1, scalar2=s2, op0=op0, op1=op1),
                       reads=reads, writes=writes)

    def stt(self, out_ap, in0, scalar, in1, op0, op1, reads, writes):
        nc = self.nc
        self.P.add("dve", lambda: nc.vector.scalar_tensor_tensor(out=out_ap, in0=in0, scalar=scalar, in1=in1,
                                                                 op0=op0, op1=op1), reads=reads, writes=writes)

    def cp(self, e, out_ap, in_ap, reads, writes):
        nc = self.nc
        if e == "act":
            self.P.add("act", lambda: nc.scalar.activation(out=out_ap, in_=in_ap, func=AF.Identity), reads=reads, writes=writes)
        else:
            en = self._veng(e)
            self.P.add(e, lambda: en.tensor_copy(out=out_ap, in_=in_ap), reads=reads, writes=writes)

    def recip(self, out_ap, in_ap, reads, writes):
        nc = self.nc
        self.P.add("dve", lambda: nc.vector.reciprocal(out=out_ap, in_=in_ap), reads=reads, writes=writes)

    def memset(self, e, ap, val, writes):
        en = self._veng(e)
        self.P.add(e, lambda: en.memset(ap, val), writes=writes)

    def declare_io(self):
        NT = self.NT
        L = DEPTH
        d = self.dram_in
        self.xL = d("xL", [self.NCH, 128, NJ, 512])
        self.cT = d("cT", [128, NJ, 2])
        self.w_mod = d("w_mod", [L, 24, 128, NJ, 512])
        self.bmodT = d("bmodT", [L, 128, 96])
        self.nmixT = d("nmixT", [L, 128, NJ])
        self.nffnT = d("nffnT", [L, 128, NJ])
        self.w_in = d("w_in", [L, 40, 128, NJ, 256])
        self.qkw = d("qkw", [L, 128, 2])
        self.consts = d("consts", [128, 4 * 128])
        self.ropeC = d("ropeC", [128, 4096])
        self.ropeS = d("ropeS", [128, 4096])
        self.n_prompt = sum(1 for s in self.seqs if s[0] == "P")
        npt = max(256, sum(s_[2] for s_ in self.seqs if s_[0] == "P"))
        self.o_yL = self.dram_out("yL", [self.NCH, 128, NJ, 512])
        self.o_kT = self.dram_out("kT", [L, BW, npt])
        self.o_v = self.dram_out("vtok", [L, npt, BW])
        t = self.dram_tmp
        NCH = self.NCH
        self.Qs = t("Qs", [4, NCH, 128, 512], BF16)
        self.Ks = t("Ks", [4, NCH, 128, 512], BF16)
        self.Vs = t("Vs", [NT, BW], BF16)
        self.Us = t("Us", [4, NCH, 128, 512], F32)
        self.RWs = t("RWs", [15, NCH, 128, 512], F32)
        self.Gs = t("Gs", [48, NCH, 128, 512], BF16)
        for n in ("Qs", "Ks", "Vs", "Us", "RWs", "Gs"):
            self.dbufs[n] = Buf(n)
        self.OUTs = [t("ODA", [4, NCH, 128, 512], BF16), t("OS5", [4, NCH, 128, 512], BF16),
                     t("ORW", [4, NCH, 128, 512], BF16)]
        self.X1s = t("X1s", [NCH, 128, NJ, 512], F32)
        self.H2s = t("H2s", [NCH, 128, NJ, 512], BF16)
        self.Xs = [t("Xs%d" % i, [NCH, 128, NJ, 512], F32) for i in range(max(1, self.L - 1))]
        self.w_up = d("w_up", [L, 16, 128, 12, 128])
        self.w_out = d("w_out", [L, 16, 128, 16, 128])
        self.w_ffn_in = d("w_ffn_in", [L, 44, 128, 32, 128])
        self.w_ffn_out = d("w_ffn_out", [L, 16, 128, 44, 128])
        self.da_lam = d("da_lam", [L, 1, 256])
        self.da_on = d("da_on", [L, 128, 1])
        self.s5p = d("s5p", [L, 2, 128, 3, 16])
        self.s5b = d("s5b", [L, 2, 128, 2, 16, 16])
        self.s5c = d("s5c", [L, 2, 128, 2, 16, 16])
        self.s5d = d("s5d", [L, 128, 4])
        self.s5glu = d("s5glu", [L, 128, 4, 512])
        self.o_s5f = self.dram_out("s5f", [L, 128, max(1, self.n_prompt), 2, 2, 16])
        self.o_rwf = self.dram_out("rwf", [L, max(1, self.n_prompt), 2, 8, 64, 64])
        self.rw_vec = d("rw_vec", [L, 128, 4, 10])
        self.rw_wa = d("rw_wa", [L, 128, 4, 2, 2])
        self.rw_mu3 = d("rw_mu3", [L, 128, 3])
        self.rw_w2 = d("rw_w2", [L, 128, 512])
        self.rw_a2 = d("rw_a2", [L, 128, 512])
        self.rw_g2 = d("rw_g2", [L, 128, 512])
        self.rw_masks = d("rw_masks", [128, 4, 128])
        self.rw_c2 = d("rw_c2", [128, 2, 128])
        self.has_sample = any(s_[0] == "S" for s_ in self.seqs)
        if self.has_sample:
            self.cacheKT = d("cacheKT", [L, 4, 128, 256])
            self.cacheV = d("cacheV", [L, 2, 128, 512])
            self.s5h0 = d("s5h0", [L, 128, 2, 2, 16])
            self.rwS0 = d("rwS0", [L, 2, 8, 64, 64])

    def load_consts(self):
        cst = self.sb("consts", [128, 512], F32, persistent=True)
        self.dma(cst.t[:], self.consts[:, :], writes=[cst.b])
        self.cst = cst
        self.ident = cst.t[:, 0:128]
        self.ropePT = cst.t[:, 384:512]
        cb = self.sb("constsb", [128, 256], BF16, persistent=True)
        self.cp("dve", cb.t[:], cst.t[:, 128:384], [cst.b], [cb.b])
        self.cstb = cb
        self.bones64 = cb.t[:, 0:128]
        self.ones128 = cb.t[:, 128:256]
        self.onesf = self.sb("onesf", [128, 512], F32, persistent=True)
        self.memset("pool", self.onesf.t[:], 1.0, [self.onesf.b])
        self.onesL = self.sb("onesL", [128, 1024], F32, persistent=True)
        self.memset("pool", self.onesL.t[:], 1.0, [self.onesL.b])

    def phase_mod(self, l):
        self.phase_begin()
        nc = self.nc
        cs = self.sb("cs", [128, NJ, 2])
        self.dma(cs.t[:], self.cT, writes=[cs.b])
        sc = self.sb("silu_c", [128, NJ, 2])
        self.act(sc.t[:], cs.t[:], AF.Silu, [cs.b], [sc.b])
        slabs = [self.sb("wmod_slab%d" % i, [128, NJ, 512]) for i in range(2)]
        mp = self.ps[0]
        for s in range(24):
            sl = slabs[s % 2]
            self.dma(sl.t[:], self.w_mod[l, s], writes=[sl.b])
            for cb in range(4):
                jc = s * 4 + cb
                for j in range(NJ):
                    self.mm(mp.t[:, 2 * jc:2 * jc + 2], sl.t[:, j, cb * 128:(cb + 1) * 128], sc.t[:, j, :],
                            j == 0, j == NJ - 1, [sl.b, sc.b], [mp.b])
        bm = self.sb("bmod", [128, 96])
        self.dma(bm.t[:], self.bmodT[l], writes=[bm.b])
        nm = self.sb("nmix", [128, NJ])
        self.dma(nm.t[:], self.nmixT[l], writes=[nm.b])
        nf = self.sb("nffn", [128, NJ])
        self.dma(nf.t[:], self.nffnT[l], writes=[nf.b])
        modT = self.modT_l[l]
        self.tt("dve", modT.t[:], mp.t[:, 0:192].rearrange("p (c r) -> p c r", r=2),
                bm.t[:].unsqueeze(2).to_broadcast([128, 96, 2]), ALU.add, [mp.b, bm.b], [modT.b])
        A = self.modA_l[l]
        self.stt(A.t[:, 0], modT.t[:, 16:32, :], 1.0, nm.t[:].unsqueeze(2).to_broadcast([128, NJ, 2]),
                 ALU.add, ALU.mult, [modT.b, nm.b], [A.b])
        self.stt(A.t[:, 1], modT.t[:, 64:80, :], 1.0, nf.t[:].unsqueeze(2).to_broadcast([128, NJ, 2]),
                 ALU.add, ALU.mult, [modT.b, nf.b], [A.b])
        self.modT = modT
        self.modA = A
        self.phase_end()

    def norm_chunk(self, src_ap, xin, hT, hslot, row, Aidx, sh_base, sq, rstd, pbank):
        self.dma(xin.t[:], src_ap, writes=[xin.b])
        self.norm_core(xin, hT, hslot, row, Aidx, sh_base, sq, rstd, pbank)

    def norm_core(self, xin, hT, hslot, row, Aidx, sh_base, sq, rstd, pbank):
        self.act(sq.t[:], xin.t[:], AF.Square, [xin.b], [sq.b])
        for j in range(NJ):
            self.mm(pbank.t[:], self.ones128, sq.t[:, j, :], j == 0, j == NJ - 1, [sq.b, self.cstb.b], [pbank.b])
        self.act(rstd.t[:], pbank.t[:], AF.Sqrt, [pbank.b], [rstd.b], bias=self.eps_t.t[:, 0:1], scale=1.0 / 16.0)
        self.recip(rstd.t[:], rstd.t[:], [rstd.b], [rstd.b])
        for j in range(NJ):
            e = "dve" if j % 2 == 0 else "pool"
            self.tt(e, xin.t[:, j, :], xin.t[:, j, :], rstd.t[:], ALU.mult, [xin.b, rstd.b], [xin.b])
            self.act(hT.t[:, j, hslot * 512:(hslot + 1) * 512], xin.t[:, j, :], AF.Identity, [xin.b, self.modA.b, self.modT.b], [hT.b],
                     bias=self.modT.t[:, sh_base + j, row:row + 1], scale=self.modA.t[:, Aidx, j, row:row + 1])

    def phase_A(self, l, src_ap):
        self.phase_begin()
        nc = self.nc
        SC = 3
        hT = self.sb("hT", [128, NJ, SC * 512], BF16)
        xins = [self.sb("xin%d" % i, [128, NJ, 512]) for i in range(1)]
        sq = self.sb("sq", [128, NJ, 512], BF16)
        rstd = self.sb("rstd", [128, 512])
        wst = [self.sb("wst%d" % i, [128, NJ, 256]) for i in range(2)]
        wbf = [self.sb("wbf%d" % i, [128, NJ, 256], BF16) for i in range(2)]
        wv = self.sb("wv", [128, NJ, 512], BF16)
        qkw = self.sb("qkw", [128, 2])
        self.dma(qkw.t[:], self.qkw[l], writes=[qkw.b])
        stg_f = [self.sb("stgf%d" % i, [128, 512]) for i in range(4)]
        stg_b = [self.sb("stgb%d" % i, [128, 512], BF16) for i in range(4)]
        qf = self.sb("qf", [128, 512])
        qn = self.sb("qn", [128, 512])
        qsq = self.sb("qsq", [128, 512], BF16)
        qrs = self.sb("qrs", [128, 512])
        qt1 = self.sb("qt1", [128, 512])
        qt2 = self.sb("qt2", [128, 512])
        rc = self.sb("ropec", [128, 512])
        rs = self.sb("ropes", [128, 512])
        cnt = {"f": 0, "b": 0, "ps": 0}
        mainps = [self.ps[i] for i in (0, 1, 2, 3)]
        auxps = [self.ps[i] for i in (4, 5)]
        normps = self.ps[6]
        vps = [self.ps[4], self.ps[5]]
        nsc = (self.NCH + SC - 1) // SC
        Qb, Kb, Vb, Ub, RWb, Gb = (self.dbufs[n] for n in ("Qs", "Ks", "Vs", "Us", "RWs", "Gs"))
        pr_tok0 = min([s[1] for s in self.seqs if s[0] == "P"] + [1 << 30])
        for sci in range(nsc):
            chunks = list(range(sci * SC, min(self.NCH, (sci + 1) * SC)))
            for hs, ci in enumerate(chunks):
                self.norm_chunk(src_ap[ci], xins[0], hT, hs, self.chunk_row[ci], 0, 0, sq, rstd, normps)
            if self.cfg.get("A_stop") == "norm":
                o = self.dram_out("dbg_h", [128, NJ * 512], BF16)
                self.dma(o[:, :].rearrange("p (j t) -> p j t", j=NJ), hT.t[:, :, 0:512], reads=[hT.b], writes=[self.dbufs["dbg_h"]])
                break
            for half in range(2):
                w = wst[half]
                self.dma(w.t[:], self.w_in[l, 4 + half], writes=[w.b])
                self.cp("dve", wv.t[:, :, half * 256:(half + 1) * 256], w.t[:], [w.b], [wv.b])
            if self.cfg.get("A_stop") == "v0":
                o = self.dram_out("dbg_wv", [128, NJ * 512], BF16)
                self.dma(o[:, :], wv.t[:].rearrange("p j c -> p (j c)"), reads=[wv.b], writes=[self.dbufs["dbg_wv"]])
                break
            for hs, ci in enumerate(chunks):
                for tt_ in range(4):
                    pb = vps[tt_ % 2]
                    for j in range(NJ):
                        self.mm(pb.t[:], hT.t[:, j, hs * 512 + tt_ * 128:hs * 512 + (tt_ + 1) * 128], wv.t[:, j, :],
                                j == 0, j == NJ - 1, [hT.b, wv.b], [pb.b])
                    tok0 = ci * 512 + tt_ * 128
                    sb_ = stg_b[cnt["b"] % 4]; cnt["b"] += 1
                    if self.chunk_row[ci] == 1:
                        sf = stg_f[cnt["f"] % 4]; cnt["f"] += 1
                        self.cp("act", sf.t[:], pb.t[:], [pb.b], [sf.b])
                        self.cp("pool", sb_.t[:], sf.t[:], [sf.b], [sb_.b])
                        p0 = tok0 - pr_tok0
                        self.dma(self.o_v[l, p0:p0 + 128, :], sf.t[:], reads=[sf.b])
                    else:
                        self.cp("act", sb_.t[:], pb.t[:], [pb.b], [sb_.b])
                    self.dma(self.Vs[tok0:tok0 + 128, :], sb_.t[:], reads=[sb_.b])
            if self.cfg.get("A_stop") in ("v", "v1"):
                break
            slabs = [[2 * i, 2 * i + 1] for i in range(39) if i not in (4, 5)] + [[78]]
            def slab_load(si_):
                self.dma(wst[si_ % 2].t[:], self.w_in[l, slabs[si_][0] // 2], writes=[wst[si_ % 2].b])

            def slab_conv(si_):
                nb_ = len(slabs[si_])
                self.cp("act" if si_ % 2 == 0 else "dve", wbf[si_ % 2].t[:, :, 0:nb_ * 128], wst[si_ % 2].t[:, :, 0:nb_ * 128],
                        [wst[si_ % 2].b], [wbf[si_ % 2].b])

            slab_load(0)
            slab_conv(0)
            for si, slab in enumerate(slabs):
                w = wst[si % 2]
                wb = wbf[si % 2]
                nb = len(slab)
                c0 = slab[0] * 128
                if si + 1 < len(slabs):
                    slab_load(si + 1)
                for hs, ci in enumerate(chunks):
                    tsl = slice(ci * 512, (ci + 1) * 512)
                    for bi, blk in enumerate(slab):
                        pb = mainps[cnt["ps"] % 4]; cnt["ps"] += 1
                        for j in range(NJ):
                            self.mm(pb.t[:], wb.t[:, j, bi * 128:(bi + 1) * 128], hT.t[:, j, hs * 512:(hs + 1) * 512],
                                    j == 0, j == NJ - 1, [wb.b, hT.b], [pb.b])
                        col = blk * 128
                        if col < 1024:
                            isk = col >= 512
                            hb = (col % 512) // 128
                            ap_ = auxps[0]
                            self.cp("act", qf.t[:], pb.t[:], [pb.b], [qf.b])
                            self.act(qsq.t[:], pb.t[:], AF.Square, [pb.b], [qsq.b])
                            self.mm(ap_.t[:], self.bones64, qsq.t[:], True, True, [qsq.b, self.cstb.b], [ap_.b])
                            self.act(qrs.t[:], ap_.t[:], AF.Sqrt, [ap_.b], [qrs.b], bias=self.eps_t.t[:, 0:1], scale=1.0)
                            self.recip(qrs.t[:], qrs.t[:], [qrs.b], [qrs.b])
                            self.stt(qn.t[:], qf.t[:], qkw.t[:, (1 if isk else 0):(2 if isk else 1)], qrs.t[:],
                                     ALU.mult, ALU.mult, [qf.b, qkw.b, qrs.b], [qn.b])
                            sb_ = stg_b[cnt["b"] % 4]; cnt["b"] += 1
                            ro = self.chunk_rope[ci]
                            if ro is not None:
                                if bi == 0:
                                    self.dma(rc.t[:], self.ropeC[:, ro:ro + 512], writes=[rc.b])
                                    self.dma(rs.t[:], self.ropeS[:, ro:ro + 512], writes=[rs.b])
                                ap2 = auxps[1]
                                self.mm(ap2.t[:], self.ropePT, qn.t[:], True, True, [qn.b, self.cst.b], [ap2.b])
                                self.tt("dve", qt1.t[:], ap2.t[:], rs.t[:], ALU.mult, [ap2.b, rs.b], [qt1.b])
                                self.tt("pool", qt2.t[:], qn.t[:], rc.t[:], ALU.mult, [qn.b, rc.b], [qt2.b])
                                self.tt("pool", sb_.t[:], qt1.t[:], qt2.t[:], ALU.add, [qt1.b, qt2.b], [sb_.b])
                            else:
                                self.cp("pool", sb_.t[:], qn.t[:], [qn.b], [sb_.b])
                                if isk:
                                    p0 = ci * 512 - pr_tok0
                                    self.dma(self.o_kT[l, hb * 128:(hb + 1) * 128, p0:p0 + 512], qn.t[:], reads=[qn.b])
                            dst = self.Ks if isk else self.Qs
                            self.dma(dst[hb, ci], sb_.t[:], reads=[sb_.b])
                        elif col < OFF_GATE:
                            sf = stg_f[cnt["f"] % 4]; cnt["f"] += 1
                            self.cp("act" if cnt["f"] % 2 else "dve", sf.t[:], pb.t[:], [pb.b], [sf.b])
                            if col < OFF_RW:
                                self.dma(self.Us[(col - OFF_S5) // 128, ci], sf.t[:], reads=[sf.b])
                            else:
                                self.dma(self.RWs[(col - OFF_RW) // 128, ci], sf.t[:], reads=[sf.b])
                        else:
                            sb_ = stg_b[cnt["b"] % 4]; cnt["b"] += 1
                            self.act(sb_.t[:], pb.t[:], AF.Sigmoid, [pb.b], [sb_.b])
                            self.dma(self.Gs[(col - OFF_GATE) // 128, ci], sb_.t[:], reads=[sb_.b])
                if si + 1 < len(slabs):
                    slab_conv(si + 1)
        self.phase_end()

    def phase_zero_branch(self, which):
        self.phase_begin()
        z = self.sb("zb", [128, 512], BF16)
        self.memset("pool", z.t[:], 0.0, [z.b])
        for cb in range(4):
            for ci in range(self.NCH):
                self.dma(self.OUTs[which][cb, ci], z.t[:], reads=[z.b])
        self.phase_end()

    def phase_s5(self, l):
        self.phase_begin()
        nc = self.nc
        PI = math.pi
        Tmax = max(T for _, _, T in self.seqs)
        prm = self.sb("s5prm", [128, 2, 3, 16])
        self.dma(prm.t[:], self.s5p[l].rearrange("d p k s -> p d k s"), writes=[prm.b])
        bt = self.sb("s5bt", [128, 2, 2, 16, 16])
        self.dma(bt.t[:], self.s5b[l].rearrange("d p r s c -> p d r s c"), writes=[bt.b])
        ct = self.sb("s5ct", [128, 2, 2, 16, 16])
        self.dma(ct.t[:], self.s5c[l].rearrange("d p r s c -> p d r s c"), writes=[ct.b])
        dsk = self.sb("s5dsk", [128, 4])
        self.dma(dsk.t[:], self.s5d[l], writes=[dsk.b])
        io_i = self.sb("io_i", [128, 512], mybir.dt.int32)
        self.P.add("pool", lambda: nc.gpsimd.iota(io_i.t[:], pattern=[[1, 512]], base=1, channel_multiplier=0), writes=[io_i.b])
        io_f = self.sb("io_f", [128, 512])
        self.cp("dve", io_f.t[:], io_i.t[:], [io_i.b], [io_f.b])
        w = self.sb("s5w", [128, 2, 12, 16])
        W = lambda k: w.t[:, :, k, :]
        lr = prm.t[:, :, 0, :]; li = prm.t[:, :, 1, :]
        self.act(W(0), prm.t[:, :, 2, :], AF.Exp, [prm.b], [w.b])
        self.tt("dve", W(1), lr, W(0), ALU.mult, [prm.b, w.b], [w.b])
        self.act(W(1), W(1), AF.Exp, [w.b], [w.b])
        self.tt("dve", W(2), li, W(0), ALU.mult, [prm.b, w.b], [w.b])
        qi = self.sb("s5qi", [128, 2, 16], mybir.dt.int32)
        qf = self.sb("s5qf", [128, 2, 16])

        def sincos(dst_sin, dst_cos, ang_ap, shape_tile_i, shape_tile_f, tmp_ap, rd, wr):
            for dst, shift in ((dst_sin, 0.0), (dst_cos, PI / 2)):
                self.ts("dve", tmp_ap, ang_ap, shift, None, ALU.add, None, rd, wr)
                self.ts("dve", shape_tile_i, tmp_ap, 1.0 / (2 * PI), None, ALU.mult, None, wr, wr)
                self.cp("dve", shape_tile_f, shape_tile_i, wr, wr)
                self.stt(tmp_ap, shape_tile_f, -2 * PI, tmp_ap, ALU.mult, ALU.add, wr, wr)
                self.act(dst, tmp_ap, AF.Sin, wr, wr)

        sincos(W(3), W(4), W(2), qi.t[:], qf.t[:], W(5), [w.b], [w.b, qi.b, qf.b])
        self.tt("dve", W(6), W(1), W(4), ALU.mult, [w.b], [w.b])
        self.tt("dve", W(7), W(1), W(3), ALU.mult, [w.b], [w.b])
        self.tt("dve", W(8), lr, lr, ALU.mult, [prm.b], [w.b])
        self.tt("dve", W(9), li, li, ALU.mult, [prm.b], [w.b])
        self.tt("dve", W(8), W(8), W(9), ALU.add, [w.b], [w.b])
        self.recip(W(8), W(8), [w.b], [w.b])
        self.ts("dve", W(9), W(6), -1.0, None, ALU.add, None, [w.b], [w.b])
        self.tt("dve", W(10), W(9), lr, ALU.mult, [w.b, prm.b], [w.b])
        self.tt("dve", W(11), W(7), li, ALU.mult, [w.b, prm.b], [w.b])
        self.tt("dve", W(10), W(10), W(11), ALU.add, [w.b], [w.b])
        self.tt("dve", W(10), W(10), W(8), ALU.mult, [w.b], [w.b])
        self.tt("dve", W(11), W(7), lr, ALU.mult, [w.b, prm.b], [w.b])
        self.tt("dve", W(9), W(9), li, ALU.mult, [w.b, prm.b], [w.b])
        self.tt("dve", W(11), W(11), W(9), ALU.subtract, [w.b], [w.b])
        self.tt("dve", W(11), W(11), W(8), ALU.mult, [w.b], [w.b])
        Z = self.sb("s5Z", [128, 2, 16, 32])
        t1 = self.sb("s5t1", [128, 16, 16])
        t2 = self.sb("s5t2", [128, 16, 16])
        BB = self.sb("s5BB", [32, 2, 16, 128])
        CC = self.sb("s5CC", [128, 2, 16, 128])
        self.memset("pool", Z.t[:], 0.0, [Z.b])
        self.memset("pool", CC.t[:], 0.0, [CC.b])
        kk_ = [0]

        def build_dir(d_):
            cr = W(10)[:, d_, :].unsqueeze(2).to_broadcast([128, 16, 16])
            cim = W(11)[:, d_, :].unsqueeze(2).to_broadcast([128, 16, 16])
            br = bt.t[:, d_, 0]; bi = bt.t[:, d_, 1]
            for (g2, lo) in ((0, 0), (1, 64)):
                ps_ = slice(lo, lo + 64)
                cs = slice(g2 * 16, g2 * 16 + 16)
                self.tt("dve", t1.t[ps_], br[ps_], cr[ps_], ALU.mult, [bt.b, w.b], [t1.b])
                self.tt("dve", t2.t[ps_], bi[ps_], cim[ps_], ALU.mult, [bt.b, w.b], [t2.b])
                self.tt("dve", Z.t[ps_, 0, :, cs], t1.t[ps_], t2.t[ps_], ALU.subtract, [t1.b, t2.b], [Z.b])
                self.tt("dve", t1.t[ps_], bi[ps_], cr[ps_], ALU.mult, [bt.b, w.b], [t1.b])
                self.tt("dve", t2.t[ps_], br[ps_], cim[ps_], ALU.mult, [bt.b, w.b], [t2.b])
                self.tt("dve", Z.t[ps_, 1, :, cs], t1.t[ps_], t2.t[ps_], ALU.add, [t1.b, t2.b], [Z.b])
            for ri in range(2):
                for st in range(16):
                    pb = self.ps[kk_[0] % 2]; kk_[0] += 1
                    self.tr(pb.t[0:32, 0:128], Z.t[:, ri, st, :], self.ident, [Z.b, self.cst.b], [pb.b])
                    self.cp("act" if kk_[0] % 2 else "dve", BB.t[:, ri, st, :], pb.t[0:32, 0:128], [pb.b], [BB.b])
            for st in range(16):
                for (g2, lo) in ((0, 0), (1, 64)):
                    c0 = (st % 4) * 32 + g2 * 16
                    self.cp("pool", CC.t[lo:lo + 64, 0, st, c0:c0 + 16], ct.t[lo:lo + 64, d_, 0, st, :], [ct.b], [CC.b])
                    self.ts("dve", CC.t[lo:lo + 64, 1, st, c0:c0 + 16], ct.t[lo:lo + 64, d_, 1, st, :], -1.0, None, ALU.mult, None,
                            [ct.b], [CC.b])
        ytile = self.sb("s5y", [128, 4, Tmax], BF16)
        hst = self.sb("s5hst", [128, 4, 2])
        fin = self.sb("s5fin", [128, max(1, self.n_prompt), 2, 2, 16])
        h0t = self.sb("s5h0", [128, 2, 2, 16])
        if self.has_sample:
            self.dma(h0t.t[:], self.s5h0[l], writes=[h0t.b])
        cosT = [self.sb("s5cos%d" % i, [128, 512]) for i in range(4)]
        sinT = [self.sb("s5sin%d" % i, [128, 512]) for i in range(4)]
        magT = [self.sb("s5mag%d" % i, [128, 512]) for i in range(4)]
        angi = self.sb("s5angi", [128, 512], mybir.dt.int32)
        angf = self.sb("s5angf", [128, 512])
        angt = self.sb("s5angt", [128, 512])
        ang0 = self.sb("s5ang0", [128, 512])
        Ut = [self.sb("s5U%d" % i, [32, 512]) for i in range(4)]
        D2 = lambda nm: [self.sb("%s%d" % (nm, i), [128, 512]) for i in range(2)]
        burs = D2("s5bur"); buis = D2("s5bui"); xrs = D2("s5xr"); xis = D2("s5xi")
        grs = D2("s5gr"); gis = D2("s5gi"); hrs = D2("s5hr"); his = D2("s5hi")
        xa = self.sb("s5xa", [128, 512]); xb = self.sb("s5xb", [128, 512])
        xa2 = self.sb("s5xa2", [128, 512]); xb2 = self.sb("s5xb2", [128, 512])
        ubuf = self.sb("s5u", [128, 512])
        ygf = self.sb("s5ygf", [128, 4, 512])
        self.dma(ygf.t[:], self.s5glu[l], writes=[ygf.b])
        glub = self.sb("glub", [128, 4, 512], BF16)
        self.cp("dve", glub.t[:], ygf.t[:], [ygf.b], [glub.b])
        ygb = self.sb("s5ygb", [128, 4, 512], BF16)
        zs = self.sb("s5zs", [128, 512])
        osb = [self.sb("s5o%d" % i, [128, 512], BF16) for i in range(2)]
        bups = [self.ps[2], self.ps[3], self.ps[4], self.ps[5]]
        yps = [self.ps[6], self.ps[7]]
        gps = [self.ps[0], self.ps[1]]
        pi_ = -1
        ku = 0
        ky = 0
        for (kind, off, T) in self.seqs:
            if kind == "P":
                pi_ += 1
            CS = min(512, T)
            nch = T // CS
            sl = slice(0, CS)
            for d_ in range(2):
                rev = (d_ == 1)
                order = list(range(nch))[::-1] if rev else list(range(nch))
                build_dir(d_)
                for cb in range(4):
                    for sti in range(4):
                        st = cb * 4 + sti
                        cT = cosT[sti]; sT = sinT[sti]; mT = magT[sti]
                        self.ts("dve", ang0.t[:], io_f.t[:], W(2)[:, d_, st:st + 1], None, ALU.mult, None, [io_f.b, w.b], [ang0.b])
                        sincos(sT.t[:], cT.t[:], ang0.t[:], angi.t[:], angf.t[:], angt.t[:], [ang0.b], [angt.b, angi.b, angf.b, sT.b, cT.b])
                        self.ts("pool", mT.t[:], self.onesf.t[:], W(1)[:, d_, st:st + 1], None, ALU.mult, None, [self.onesf.b, w.b], [mT.b])
                        if kind == "S":
                            self.cp("pool", hst.t[:, sti, 0:1], h0t.t[:, d_, 0, st:st + 1], [h0t.b], [hst.b])
                            self.cp("pool", hst.t[:, sti, 1:2], h0t.t[:, d_, 1, st:st + 1], [h0t.b], [hst.b])
                    if kind != "S":
                        self.memset("pool", hst.t[:], 0.0, [hst.b])
                    for ch in order:
                        c0 = ch * CS
                        ci = (off + c0) // 512; o_in = (off + c0) % 512
                        yp = yps[ky % 2]; ky += 1
                        for sti in range(4):
                            st = cb * 4 + sti
                            cT = cosT[sti]; sT = sinT[sti]; mT = magT[sti]
                            c_, s_ = cT.t[:, sl], sT.t[:, sl]
                            k2 = ku % 2
                            U = Ut[ku % 4]; ku += 1
                            pr = bups[2 * k2]; pim = bups[2 * k2 + 1]
                            bur = burs[k2]; bui = buis[k2]; xr = xrs[k2]; xi = xis[k2]
                            gr = grs[k2]; gi = gis[k2]; HR = hrs[k2]; HI = his[k2]
                            self.dma(U.t[:, 0:CS], self.Us[cb, ci, sti * 32:(sti + 1) * 32, o_in:o_in + CS], writes=[U.b])
                            self.mm(pr.t[:, 0:CS], BB.t[:, 0, st, :], U.t[:, 0:CS], True, True, [BB.b, U.b], [pr.b])
                            self.mm(pim.t[:, 0:CS], BB.t[:, 1, st, :], U.t[:, 0:CS], True, True, [BB.b, U.b], [pim.b])
                            self.cp("act", bur.t[:, 0:CS], pr.t[:, 0:CS], [pr.b], [bur.b])
                            self.cp("act", bui.t[:, 0:CS], pim.t[:, 0:CS], [pim.b], [bui.b])
                            BR = self.revap(bur, CS, rev); BI = self.revap(bui, CS, rev)
                            self.tt("dve", xa.t[:, sl], BR, c_, ALU.mult, [bur.b, cT.b], [xa.b])
                            self.tt("dve", xb.t[:, sl], BI, s_, ALU.mult, [bui.b, sT.b], [xb.b])
                            self.tt("dve", xr.t[:, sl], xa.t[:, sl], xb.t[:, sl], ALU.add, [xa.b, xb.b], [xr.b])
                            self.tt("dve", xa2.t[:, sl], BI, c_, ALU.mult, [bui.b, cT.b], [xa2.b])
                            self.tt("dve", xb2.t[:, sl], BR, s_, ALU.mult, [bur.b, sT.b], [xb2.b])
                            self.tt("dve", xi.t[:, sl], xa2.t[:, sl], xb2.t[:, sl], ALU.subtract, [xa2.b, xb2.b], [xi.b])
                            self.P.add("dve", (lambda a=gr.t[:, sl], b=mT.t[:, sl], c=xr.t[:, sl], i=hst.t[:, sti, 0:1]:
                                               nc.vector.tensor_tensor_scan(out=a, data0=b, data1=c, initial=i, op0=ALU.mult, op1=ALU.add)),
                                       reads=[mT.b, xr.b, hst.b], writes=[gr.b])
                            self.P.add("dve", (lambda a=gi.t[:, sl], b=mT.t[:, sl], c=xi.t[:, sl], i=hst.t[:, sti, 1:2]:
                                               nc.vector.tensor_tensor_scan(out=a, data0=b, data1=c, initial=i, op0=ALU.mult, op1=ALU.add)),
                                       reads=[mT.b, xi.b, hst.b], writes=[gi.b])
                            self.tt("dve", xa.t[:, sl], gr.t[:, sl], c_, ALU.mult, [gr.b, cT.b], [xa.b])
                            self.tt("dve", xb.t[:, sl], gi.t[:, sl], s_, ALU.mult, [gi.b, sT.b], [xb.b])
                            self.tt("dve", HR.t[:, sl], xa.t[:, sl], xb.t[:, sl], ALU.subtract, [xa.b, xb.b], [HR.b])
                            self.tt("dve", xa2.t[:, sl], gr.t[:, sl], s_, ALU.mult, [gr.b, sT.b], [xa2.b])
                            self.tt("dve", xb2.t[:, sl], gi.t[:, sl], c_, ALU.mult, [gi.b, cT.b], [xb2.b])
                            self.tt("dve", HI.t[:, sl], xa2.t[:, sl], xb2.t[:, sl], ALU.add, [xa2.b, xb2.b], [HI.b])
                            self.cp("dve", hst.t[:, sti, 0:1], HR.t[:, CS - 1:CS], [HR.b], [hst.b])
                            self.cp("dve", hst.t[:, sti, 1:2], HI.t[:, CS - 1:CS], [HI.b], [hst.b])
                            self.mm(yp.t[:, 0:CS], CC.t[:, 0, st, :], HR.t[:, sl], sti == 0, False, [CC.b, HR.b], [yp.b])
                            self.mm(yp.t[:, 0:CS], CC.t[:, 1, st, :], HI.t[:, sl], False, sti == 3, [CC.b, HI.b], [yp.b])
                        yv = self.revap2(ytile, cb, c0, CS, rev)
                        if d_ == 0:
                            self.cp("act", yv, yp.t[:, 0:CS], [yp.b], [ytile.b])
                        else:
                            self.tt("dve", yv, yp.t[:, 0:CS], yv, ALU.add, [yp.b, ytile.b], [ytile.b])
                    if kind == "P":
                        for sti in range(4):
                            self.cp("pool", fin.t[:, pi_, d_, :, cb * 4 + sti], hst.t[:, sti, :], [hst.b], [fin.b])
            for ch in range(nch):
                c0 = ch * CS
                ci = (off + c0) // 512; o_in = (off + c0) % 512
                for cb in range(4):
                    self.dma(ubuf.t[:, 0:CS], self.Us[cb, ci, :, o_in:o_in + CS], writes=[ubuf.b])
                    self.stt(ygf.t[:, cb, 0:CS], ubuf.t[:, 0:CS], dsk.t[:, cb:cb + 1], ytile.t[:, cb, c0:c0 + CS], ALU.mult, ALU.add,
                             [ubuf.b, dsk.b, ytile.b], [ygf.b])
                    yv_ = ygf.t[:, cb, 0:CS]
                    self.tt("pool", zs.t[:, 0:CS], yv_, yv_, ALU.mult, [ygf.b], [zs.b])
                    self.ts("pool", zs.t[:, 0:CS], zs.t[:, 0:CS], 0.044715, 1.0, ALU.mult, ALU.add, [zs.b], [zs.b])
                    self.tt("pool", zs.t[:, 0:CS], zs.t[:, 0:CS], yv_, ALU.mult, [zs.b, ygf.b], [zs.b])
                    self.act(zs.t[:, 0:CS], zs.t[:, 0:CS], AF.Tanh, [zs.b], [zs.b], scale=math.sqrt(2.0 / math.pi))
                    self.stt(yv_, zs.t[:, 0:CS], 1.0, yv_, ALU.add, ALU.mult, [zs.b, ygf.b], [ygf.b])
                    self.ts("pool", yv_, yv_, 0.5, None, ALU.mult, None, [ygf.b], [ygf.b])
                    self.cp("pool", ygb.t[:, cb, 0:CS], ygf.t[:, cb, 0:CS], [ygf.b], [ygb.b])
                for c2 in range(4):
                    gp = gps[c2 % 2]
                    for c1 in range(4):
                        self.mm(gp.t[:, 0:CS], glub.t[:, c1, c2 * 128:(c2 + 1) * 128], ygb.t[:, c1, 0:CS], c1 == 0, c1 == 3,
                                [glub.b, ygb.b], [gp.b])
                    self.act(zs.t[:, 0:CS], gp.t[:, 0:CS], AF.Sigmoid, [gp.b], [zs.b])
                    o_ = osb[c2 % 2]
                    self.tt("dve", o_.t[:, 0:CS], zs.t[:, 0:CS], ygf.t[:, c2, 0:CS], ALU.mult, [zs.b, ygf.b], [o_.b])
                    self.dma(self.OUTs[1][c2, ci, :, o_in:o_in + CS], o_.t[:, 0:CS], reads=[o_.b])
        if self.n_prompt:
            self.dma(self.o_s5f[l], fin.t[:], reads=[fin.b])
        self.phase_end()

    def revap(self, tl, CS, rev):
        return tl.t[:, CS - 1::-1] if rev else tl.t[:, 0:CS]

    def revap2(self, tl, cb, c0, CS, rev):
        if not rev:
            return tl.t[:, cb, c0:c0 + CS]
        if c0 == 0:
            return tl.t[:, cb, CS - 1::-1]
        return tl.t[:, cb, c0 + CS - 1:c0 - 1:-1]

    def phase_rwkv(self, l):
        self.phase_begin()
        nc = self.nc
        SL = 1024 if any(T >= 1024 for _, _, T in self.seqs) else 256
        NCK = SL // 64
        Tmax = max(T for _, _, T in self.seqs)
        vec = self.sb("rwvec", [128, 4, 10]); self.dma(vec.t[:], self.rw_vec[l], writes=[vec.b])
        wa = self.sb("rwwa", [128, 4, 2, 2]); self.dma(wa.t[:], self.rw_wa[l], writes=[wa.b])
        mu3 = self.sb("rwmu3", [128, 3]); self.dma(mu3.t[:], self.rw_mu3[l], writes=[mu3.b])
        w2 = self.sb("rww2", [128, 512]); self.dma(w2.t[:], self.rw_w2[l], writes=[w2.b])
        a2 = self.sb("rwa2", [128, 512]); self.dma(a2.t[:], self.rw_a2[l], writes=[a2.b])
        g2 = self.sb("rwg2", [128, 512]); self.dma(g2.t[:], self.rw_g2[l], writes=[g2.b])
        msk = self.sb("rwmsk", [128, 4, 128]); self.dma(msk.t[:], self.rw_masks, writes=[msk.b])
        c2 = self.sb("rwc2", [128, 2, 128]); self.dma(c2.t[:], self.rw_c2, writes=[c2.b])
        eps2 = self.sb("rweps", [128, 1]); self.memset("pool", eps2.t[:], L2_EPS, [eps2.b])
        def tile(nm, w=SL, dt=F32):
            return self.sb(nm, [128, w], dt)
        zb = [self.sb("rwzb%d" % i, [128, SL + 2]) for i in range(2)]
        sh = tile("rwsh")
        r_m = tile("rw_r"); k_m = tile("rw_k"); v_m = tile("rw_v"); hw_m = tile("rw_hw"); ha_m = tile("rw_ha")
        kk = tile("rw_kk"); lw = tile("rw_lw"); Lc = tile("rw_Lc"); av = tile("rw_a"); lg = tile("rw_lg"); ex = tile("rw_ex")
        t1 = tile("rw_t1"); t2 = tile("rw_t2")
        AR = self.sb("rw_AR", [128, SL // 128, 2, 128])
        BH = tile("rw_BH"); KH = tile("rw_KH")
        BC = self.sb("rw_BC", [128, SL // 128, 128]); KC = self.sb("rw_KC", [128, SL // 128, 128])
        VT = self.sb("rw_VT", [128, SL // 128, 128])
        VP = [self.sb("rw_VP%d" % i, [128, SL // 128, 128]) for i in range(2)]
        Lst = self.sb("rw_Lst", [128, NCK]); Gc = self.sb("rw_Gc", [128, NCK])
        ST = self.sb("rw_ST", [128, 128])
        ytile = self.sb("rw_y", [128, Tmax])
        UT = self.sb("rw_UT", [128, 128]); UP = [self.sb("rw_UP%d" % i, [128, 128]) for i in range(2)]
        Xs = [self.sb("rw_Xs%d" % i, [128, 64]) for i in range(2)]
        NU = 8
        NM1 = [self.sb("rw_NM1_%d" % i, [128, 256]) for i in range(NU)]
        NM2 = [self.sb("rw_NM2_%d" % i, [128, 256]) for i in range(NU)]
        NT_ = [self.sb("rw_NT_%d" % i, [128, 128]) for i in range(NU)]
        XA = [[self.sb("rw_XA%d_%d" % (h, i), [128, 128]) for i in range(2)] for h in range(NU)]
        XB = [[self.sb("rw_XB%d_%d" % (h, i), [128, 128]) for i in range(2)] for h in range(NU)]
        PT = [[self.sb("rw_P%d_%d" % (h, i), [128, 128]) for i in range(2)] for h in range(NU)]
        ysq = tile("rw_ysq", 512); yc = tile("rw_yc", 512); yrs = tile("rw_yrs", 512); gsg = tile("rw_gsg", 512)
        bon = tile("rw_bon", 512); osb = [self.sb("rw_o%d" % i, [128, 512], BF16) for i in range(2)]
        for t_ in (UP[0], UP[1], UT, Xs[0], Xs[1], VP[0], VP[1]):
            self.memset("pool", t_.t[:], 0.0, [t_.b])
        s0f = self.sb("rw_s0", [128, 64])
        pk = {"n": 0}

        def PS():
            pk["n"] += 1
            return self.ps[pk["n"] % 8]

        def load_mixed(dst, blk, mu_ap, seq, s0, n):
            kind, off, T = seq
            z = zb[pk["n"] % 2]; pk["n"] += 1
            t = 0
            while t < n:
                g = off + s0 + t
                ci = g // 512; o_in = g % 512; m = min(512 - o_in, n - t)
                self.dma(z.t[:, 1 + t:1 + t + m], self.RWs[blk, ci, :, o_in:o_in + m], writes=[z.b])
                t += m
            for (pos, col) in ((s0 - 1, 0), (s0 + n, n + 1)):
                if pos < 0 or pos >= T:
                    self.memset("pool", z.t[:, col:col + 1], 0.0, [z.b])
                else:
                    g = off + pos
                    self.dma(z.t[:, col:col + 1], self.RWs[blk, g // 512, :, (g % 512):(g % 512) + 1], writes=[z.b], slow=True)
            self.tt("dve", sh.t[:, 0:n], z.t[:, 0:n], z.t[:, 2:n + 2], ALU.add, [z.b], [sh.b])
            self.stt(sh.t[:, 0:n], sh.t[:, 0:n], 0.5, z.t[:, 1:n + 1], ALU.mult, ALU.subtract, [sh.b, z.b], [sh.b])
            self.stt(dst.t[:, 0:n], sh.t[:, 0:n], mu_ap, z.t[:, 1:n + 1], ALU.mult, ALU.add, [sh.b, z.b, vec.b, mu3.b], [dst.b])

        pi_ = -1
        for seq in self.seqs:
            kind, off, T = seq
            if kind == "P":
                pi_ += 1
            sl_ = min(SL, T)
            nseg = T // sl_
            nck = sl_ // 64
            for hp in range(4):
                V = lambda k_: vec.t[:, hp, k_:k_ + 1]
                for d_ in range(2):
                    rev = d_ == 1
                    self.memset("pool", ST.t[:], 0.0, [ST.b])
                    if kind == "S":
                        for hh in range(2):
                            self.dma(s0f.t[64 * hh:64 * hh + 64, :], self.rwS0[l, d_, 2 * hp + hh], writes=[s0f.b])
                        for hh in range(2):
                            self.cp("pool", ST.t[64 * hh:64 * hh + 64, 64 * hh:64 * hh + 64], s0f.t[64 * hh:64 * hh + 64, :], [s0f.b], [ST.b])
                    mS, mI = (2, 3) if rev else (0, 1)
                    mST = 0 if rev else 2
                    segs = list(range(nseg))[::-1] if rev else list(range(nseg))
                    for sg in segs:
                        s0 = sg * sl_
                        n = sl_
                        load_mixed(r_m, hp, V(0), seq, s0, n)
                        load_mixed(k_m, 4 + hp, V(1), seq, s0, n)
                        load_mixed(v_m, 8 + hp, V(2), seq, s0, n)
                        load_mixed(hw_m, 12, mu3.t[:, 0:1], seq, s0, n)
                        load_mixed(ha_m, 13, mu3.t[:, 1:2], seq, s0, n)
                        dr = slice(64 * d_, 64 * d_ + 64)
                        self.act(hw_m.t[dr, 0:n], hw_m.t[dr, 0:n], AF.Tanh, [hw_m.b], [hw_m.b])
                        for c0 in range(0, n, 512):
                            m = min(512, n - c0)
                            pb = PS()
                            self.mm(pb.t[:, 0:m], w2.t[dr, hp * 128:(hp + 1) * 128], hw_m.t[dr, c0:c0 + m], True, True, [w2.b, hw_m.b], [pb.b])
                            self.act(lw.t[:, c0:c0 + m], pb.t[:, 0:m], AF.Sigmoid, [pb.b, wa.b], [lw.b], bias=wa.t[:, hp, d_, 0:1], scale=1.0)
                            pb = PS()
                            self.mm(pb.t[:, 0:m], a2.t[dr, hp * 128:(hp + 1) * 128], ha_m.t[dr, c0:c0 + m], True, True, [a2.b, ha_m.b], [pb.b])
                            self.act(av.t[:, c0:c0 + m], pb.t[:, 0:m], AF.Sigmoid, [pb.b, wa.b], [av.b], bias=wa.t[:, hp, d_, 1:2], scale=1.0)
                        self.ts("dve", lw.t[:, 0:n], lw.t[:, 0:n], -math.exp(-0.5), None, ALU.mult, None, [lw.b], [lw.b])
                        self.ts("dve", t1.t[:, 0:n], k_m.t[:, 0:n], V(3), None, ALU.mult, None, [k_m.b, vec.b], [t1.b])
                        self.tt("dve", t2.t[:, 0:n], t1.t[:, 0:n], t1.t[:, 0:n], ALU.mult, [t1.b], [t2.b])
                        for c0 in range(0, n, 512):
                            m = min(512, n - c0)
                            pb = PS()
                            self.mm(pb.t[:, 0:m], c2.t[:, 0, :], t2.t[:, c0:c0 + m], True, True, [c2.b, t2.b], [pb.b])
                            self.act(kk.t[:, c0:c0 + m], pb.t[:, 0:m], AF.Sqrt, [pb.b, eps2.b], [kk.b], bias=eps2.t[:, 0:1], scale=1.0)
                        self.recip(kk.t[:, 0:n], kk.t[:, 0:n], [kk.b], [kk.b])
                        self.tt("dve", kk.t[:, 0:n], kk.t[:, 0:n], t1.t[:, 0:n], ALU.mult, [kk.b, t1.b], [kk.b])
                        self.cumsum(Lc, lw, n, t2)
                        L3 = Lc.t[:, 0:n].rearrange("p (c j) -> p c j", j=64)
                        self.memset("pool", Lst.t[:, 0:1], 0.0, [Lst.b])
                        if nck > 1:
                            self.cp("pool", Lst.t[:, 1:nck], L3[:, 0:nck - 1, 63], [Lc.b], [Lst.b])
                        lg3 = lg.t[:, 0:n].rearrange("p (c j) -> p c j", j=64)
                        if not rev:
                            self.tt("dve", lg3, L3, Lst.t[:, 0:nck].unsqueeze(2).to_broadcast([128, nck, 64]), ALU.subtract, [Lc.b, Lst.b], [lg.b])
                        else:
                            self.tt("dve", t2.t[:, 0:n], Lc.t[:, 0:n], lw.t[:, 0:n], ALU.subtract, [Lc.b, lw.b], [t2.b])
                            self.tt("dve", lg3, L3[:, :, 63:64].to_broadcast([128, nck, 64]),
                                    t2.t[:, 0:n].rearrange("p (c j) -> p c j", j=64), ALU.subtract, [Lc.b, t2.b], [lg.b])
                        self.tt("dve", Gc.t[:, 0:nck], L3[:, :, 63], Lst.t[:, 0:nck], ALU.subtract, [Lc.b, Lst.b], [Gc.b])
                        self.act(Gc.t[:, 0:nck], Gc.t[:, 0:nck], AF.Exp, [Gc.b], [Gc.b])
                        AR4 = AR.t[:, 0:n // 128]
                        self.act(ex.t[:, 0:n], lg.t[:, 0:n], AF.Exp, [lg.b], [ex.b])
                        self.tt("dve", AR4[:, :, 1, :], r_m.t[:, 0:n].rearrange("p (c j) -> p c j", j=128),
                                ex.t[:, 0:n].rearrange("p (c j) -> p c j", j=128), ALU.mult, [r_m.b, ex.b], [AR.b])
                        self.act(ex.t[:, 0:n], lg.t[:, 0:n], AF.Exp, [lg.b], [ex.b], scale=-1.0)
                        self.tt("dve", t1.t[:, 0:n], kk.t[:, 0:n], av.t[:, 0:n], ALU.mult, [kk.b, av.b], [t1.b])
                        self.tt("dve", BH.t[:, 0:n], t1.t[:, 0:n], ex.t[:, 0:n], ALU.mult, [t1.b, ex.b], [BH.b])
                        self.ts("pool", t1.t[:, 0:n], av.t[:, 0:n], -1.0, V(4), ALU.add, ALU.mult, [av.b, vec.b], [t1.b])
                        self.ts("dve", t1.t[:, 0:n], t1.t[:, 0:n], 1.0, None, ALU.add, None, [t1.b], [t1.b])
                        self.tt("dve", t1.t[:, 0:n], t1.t[:, 0:n], k_m.t[:, 0:n], ALU.mult, [t1.b, k_m.b], [t1.b])
                        self.tt("dve", KH.t[:, 0:n], t1.t[:, 0:n], ex.t[:, 0:n], ALU.mult, [t1.b, ex.b], [KH.b])
                        self.tt("pool", t1.t[:, 0:n], lg.t[:, 0:n], lw.t[:, 0:n], ALU.subtract, [lg.b, lw.b], [t1.b])
                        self.act(ex.t[:, 0:n], t1.t[:, 0:n], AF.Exp, [t1.b], [ex.b])
                        self.stt(AR4[:, :, 0, :], kk.t[:, 0:n].rearrange("p (c j) -> p c j", j=128), -1.0,
                                 ex.t[:, 0:n].rearrange("p (c j) -> p c j", j=128), ALU.mult, ALU.mult, [kk.b, ex.b], [AR.b])
                        Gb = Gc.t[:, 0:nck].unsqueeze(2).to_broadcast([128, nck, 64])
                        self.tt("dve", t1.t[:, 0:n].rearrange("p (c j) -> p c j", j=64), BH.t[:, 0:n].rearrange("p (c j) -> p c j", j=64), Gb,
                                ALU.mult, [BH.b, Gc.b], [t1.b])
                        self.tt("dve", t2.t[:, 0:n].rearrange("p (c j) -> p c j", j=64), KH.t[:, 0:n].rearrange("p (c j) -> p c j", j=64), Gb,
                                ALU.mult, [KH.b, Gc.b], [t2.b])
                        for tl in range(n // 128):
                            for (src, dst, eng) in ((t1, BC, "act"), (t2, KC, "dve"), (v_m, VT, "act")):
                                pb = PS()
                                self.tr(pb.t[:, 0:128], src.t[:, tl * 128:(tl + 1) * 128], self.ident, [src.b, self.cst.b], [pb.b])
                                self.cp(eng, dst.t[:, tl, :], pb.t[:, 0:128], [pb.b], [dst.b])
                            for hh in range(2):
                                self.cp("pool", VP[hh].t[:, tl, 64 * hh:64 * hh + 64], VT.t[:, tl, 64 * hh:64 * hh + 64], [VT.b], [VP[hh].b])
                        pairs = list(range(n // 128))[::-1] if rev else list(range(n // 128))
                        GB = 2
                        NUB = 2 * GB
                        batches = [pairs[i:i + GB] for i in range(0, len(pairs), GB)]

                        def gen_inv(bi, batch, res):
                            base = (bi % 2) * NUB
                            units = [(pr, hh) for pr in batch for hh in range(2)]
                            for u, (pr, hh) in enumerate(units):
                                ub = base + u
                                tk = slice(pr * 128, (pr + 1) * 128)
                                ARp = AR.t[:, pr].rearrange("p a j -> p (a j)")
                                hr_ = slice(64 * hh, 64 * hh + 64)
                                n1 = NM1[ub]; n2 = NM2[ub]; nt = NT_[ub]
                                pb = PS()
                                self.mm(pb.t[:, 0:256], BH.t[hr_, tk], ARp[hr_, :], True, True, [BH.b, AR.b], [pb.b])
                                self.tt("dve", n1.t[:].rearrange("p (a j) -> p a j", j=128), pb.t[:, 0:256].rearrange("p (a j) -> p a j", j=128),
                                        msk.t[:, mS:mS + 2, :], ALU.mult, [pb.b, msk.b], [n1.b])
                                yield
                                pb = PS()
                                self.mm(pb.t[:, 0:256], KH.t[hr_, tk], ARp[hr_, :], True, True, [KH.b, AR.b], [pb.b])
                                self.tt("dve", n2.t[:].rearrange("p (a j) -> p a j", j=128), pb.t[:, 0:256].rearrange("p (a j) -> p a j", j=128),
                                        msk.t[:, mS:mS + 2, :], ALU.mult, [pb.b, msk.b], [n2.b])
                                yield
                                pb = PS()
                                self.mm(pb.t[:, 0:128], AR.t[hr_, pr, 0, :], BH.t[hr_, tk], True, True, [BH.b, AR.b], [pb.b])
                                self.tt("dve", nt.t[:], pb.t[:, 0:128], msk.t[:, mST, :], ALU.mult, [pb.b, msk.b], [nt.b])
                                self.tt("pool", PT[ub][0].t[:], self.ident, n1.t[:, 0:128], ALU.add, [self.cst.b, n1.b], [PT[ub][0].b])
                                yield
                            Xc = [(NM1[base + u].t[:, 0:128], NM1[base + u].b, NT_[base + u].t[:], NT_[base + u].b) for u in range(len(units))]
                            Pc = [PT[base + u][0] for u in range(len(units))]
                            for j in range(1, 6):
                                banks = []
                                for u in range(len(units)):
                                    X, Xb, XTt, XTb = Xc[u]
                                    pb = PS()
                                    self.mm(pb.t[:, 0:128], X, XTt, True, True, [Xb, XTb], [pb.b])
                                    if j < 5:
                                        self.mm(pb.t[:, 128:256], XTt, X, True, True, [Xb, XTb], [pb.b])
                                    banks.append(pb)
                                    yield
                                for u in range(len(units)):
                                    pb = banks[u]
                                    xa = XA[base + u][j % 2]; xb_ = XB[base + u][j % 2]
                                    self.cp("act", xb_.t[:], pb.t[:, 0:128], [pb.b], [xb_.b])
                                    if j < 5:
                                        self.cp("act", xa.t[:], pb.t[:, 128:256], [pb.b], [xa.b])
                                    Xc[u] = (xa.t[:], xa.b, xb_.t[:], xb_.b)
                                yield
                                banks = []
                                for u in range(len(units)):
                                    pb = PS()
                                    self.mm(pb.t[:, 0:128], Xc[u][2], Pc[u].t[:], True, True, [Xc[u][3], Pc[u].b], [pb.b])
                                    banks.append(pb)
                                    yield
                                for u in range(len(units)):
                                    Pn = PT[base + u][j % 2]
                                    self.tt("dve", Pn.t[:], banks[u].t[:, 0:128], Pc[u].t[:], ALU.add, [banks[u].b, Pc[u].b], [Pn.b])
                                    Pc[u] = Pn
                                yield
                            res["units"] = units
                            res["Pc"] = Pc
                            res["base"] = base

                        def gen_chain(batch, res):
                            units = res["units"]; Pc = res["Pc"]; base = res["base"]
                            for pr in batch:
                                u0 = units.index((pr, 0))
                                Tinv = [Pc[u0], Pc[u0 + 1]]
                                N1 = [NM1[base + u0], NM1[base + u0 + 1]]
                                N2 = [NM2[base + u0], NM2[base + u0 + 1]]
                                chs = (1, 0) if rev else (0, 1)
                                for cc in chs:
                                    ck = (pr * 2 + cc)
                                    cr_ = slice(64 * cc, 64 * cc + 64)
                                    pbx = []
                                    for hh in range(2):
                                        hr_ = slice(64 * hh, 64 * hh + 64)
                                        pb = PS()
                                        self.mm(pb.t[:, 0:64], AR.t[hr_, pr, 0, :], ST.t[hr_, hr_], True, False, [AR.b, ST.b], [pb.b])
                                        self.mm(pb.t[:, 0:64], N2[hh].t[:, 0:128], VT.t[:, pr, hr_], False, True, [N2[hh].b, VT.b], [pb.b])
                                        pbx.append(pb)
                                    for hh in range(2):
                                        self.cp("act" if hh == 0 else "dve", Xs[hh].t[cr_, :], pbx[hh].t[cr_, 0:64], [pbx[hh].b], [Xs[hh].b])
                                    yield
                                    pbu = []
                                    for hh in range(2):
                                        pb = PS()
                                        self.mm(pb.t[:, 0:64], Tinv[hh].t[:], Xs[hh].t[:], True, True, [Tinv[hh].b, Xs[hh].b], [pb.b])
                                        pbu.append(pb)
                                    for hh in range(2):
                                        hr_ = slice(64 * hh, 64 * hh + 64)
                                        e1 = "act" if hh == 0 else "dve"
                                        self.cp(e1, UT.t[cr_, hr_], pbu[hh].t[cr_, 0:64], [pbu[hh].b], [UT.b])
                                        self.cp(e1, UP[hh].t[cr_, hr_], pbu[hh].t[cr_, 0:64], [pbu[hh].b], [UP[hh].b])
                                    yield
                                    yp = PS()
                                    self.mm(yp.t[:, 0:64], ST.t[:], AR.t[:, pr, 1, cr_], True, False, [ST.b, AR.b], [yp.b])
                                    for hh in range(2):
                                        self.mm(yp.t[:, 0:64], UP[hh].t[cr_, :], N1[hh].t[cr_, 128 + 64 * cc:128 + 64 * cc + 64], False, False,
                                                [UP[hh].b, N1[hh].b], [yp.b])
                                        self.mm(yp.t[:, 0:64], VP[hh].t[cr_, pr, :], N2[hh].t[cr_, 128 + 64 * cc:128 + 64 * cc + 64], False, hh == 1,
                                                [VP[hh].b, N2[hh].b], [yp.b])
                                    sp = PS()
                                    self.mm(sp.t[:, 0:128], BC.t[cr_, pr, :], UT.t[cr_, :], True, False, [BC.b, UT.b], [sp.b])
                                    self.mm(sp.t[:, 0:128], KC.t[cr_, pr, :], VT.t[cr_, pr, :], False, True, [KC.b, VT.b], [sp.b])
                                    g0 = s0 + pr * 128 + 64 * cc
                                    if not rev:
                                        self.cp("act", ytile.t[:, g0:g0 + 64], yp.t[:, 0:64], [yp.b], [ytile.b])
                                    else:
                                        self.tt("dve", ytile.t[:, g0:g0 + 64], yp.t[:, 0:64], ytile.t[:, g0:g0 + 64], ALU.add, [yp.b, ytile.b], [ytile.b])
                                    for hh in range(2):
                                        hr_ = slice(64 * hh, 64 * hh + 64)
                                        self.stt(ST.t[hr_, hr_], ST.t[hr_, hr_], Gc.t[hr_, ck:ck + 1], sp.t[hr_, hr_], ALU.mult, ALU.add,
                                                 [ST.b, Gc.b, sp.b], [ST.b])
                                    yield

                        def drain(g):
                            for _ in g:
                                pass

                        def merge(gc, gi, ratio=4):
                            ci_done = False; ii_done = False
                            while not (ci_done and ii_done):
                                if not ci_done:
                                    try:
                                        next(gc)
                                    except StopIteration:
                                        ci_done = True
                                for _ in range(ratio):
                                    if ii_done:
                                        break
                                    try:
                                        next(gi)
                                    except StopIteration:
                                        ii_done = True

                        results = [dict() for _ in batches]
                        drain(gen_inv(0, batches[0], results[0]))
                        for bi, batch in enumerate(batches):
                            gc = gen_chain(batch, results[bi])
                            if bi + 1 < len(batches):
                                merge(gc, gen_inv(bi + 1, batches[bi + 1], results[bi + 1]))
                            else:
                                drain(gc)
                    if kind == "P":
                        for hh in range(2):
                            self.dma(self.o_rwf[l, pi_, d_, 2 * hp + hh], ST.t[64 * hh:64 * hh + 64, 64 * hh:64 * hh + 64], reads=[ST.b])
                for c0 in range(0, T, 512):
                    m = min(512, T - c0)
                    g = off + c0; ci = g // 512; o_in = g % 512
                    yv = ytile.t[:, c0:c0 + m]
                    pb = PS()
                    self.mm(pb.t[:, 0:m], c2.t[:, 1, :], yv, True, True, [c2.b, ytile.b], [pb.b])
                    self.tt("dve", yc.t[:, 0:m], yv, pb.t[:, 0:m], ALU.subtract, [ytile.b, pb.b], [yc.b])
                    self.tt("pool", ysq.t[:, 0:m], yc.t[:, 0:m], yc.t[:, 0:m], ALU.mult, [yc.b], [ysq.b])
                    pb = PS()
                    self.mm(pb.t[:, 0:m], c2.t[:, 1, :], ysq.t[:, 0:m], True, True, [c2.b, ysq.b], [pb.b])
                    self.act(yrs.t[:, 0:m], pb.t[:, 0:m], AF.Sqrt, [pb.b], [yrs.b], bias=self.eps_t.t[:, 1:2], scale=1.0)
                    self.recip(yrs.t[:, 0:m], yrs.t[:, 0:m], [yrs.b], [yrs.b])
                    self.tt("dve", yc.t[:, 0:m], yc.t[:, 0:m], yrs.t[:, 0:m], ALU.mult, [yc.b, yrs.b], [yc.b])
                    self.act(yc.t[:, 0:m], yc.t[:, 0:m], AF.Identity, [yc.b, vec.b], [yc.b], bias=V(7), scale=V(6))
                    load_mixed(r_m, hp, V(0), seq, c0, m)
                    load_mixed(k_m, 4 + hp, V(1), seq, c0, m)
                    load_mixed(v_m, 8 + hp, V(2), seq, c0, m)
                    self.stt(ysq.t[:, 0:m], r_m.t[:, 0:m], V(5), k_m.t[:, 0:m], ALU.mult, ALU.mult, [r_m.b, k_m.b, vec.b], [ysq.b])
                    pb = PS()
                    self.mm(pb.t[:, 0:m], c2.t[:, 0, :], ysq.t[:, 0:m], True, True, [c2.b, ysq.b], [pb.b])
                    self.tt("dve", bon.t[:, 0:m], pb.t[:, 0:m], v_m.t[:, 0:m], ALU.mult, [pb.b, v_m.b], [bon.b])
                    self.tt("pool", yc.t[:, 0:m], yc.t[:, 0:m], bon.t[:, 0:m], ALU.add, [yc.b, bon.b], [yc.b])
                    load_mixed(hw_m, 14, mu3.t[:, 2:3], seq, c0, m)
                    self.act(gsg.t[:, 0:m], hw_m.t[:, 0:m], AF.Sigmoid, [hw_m.b], [gsg.b])
                    pb = PS()
                    self.mm(pb.t[:, 0:m], g2.t[:, hp * 128:(hp + 1) * 128], gsg.t[:, 0:m], True, True, [g2.b, gsg.b], [pb.b])
                    o_ = osb[pk["n"] % 2]
                    self.tt("dve", o_.t[:, 0:m], pb.t[:, 0:m], yc.t[:, 0:m], ALU.mult, [pb.b, yc.b], [o_.b])
                    self.dma(self.OUTs[2][hp, ci, :, o_in:o_in + m], o_.t[:, 0:m], reads=[o_.b])
        self.phase_end()

    def cumsum(self, dst, src, n, tmp):
        nc = self.nc
        self.P.add("dve", lambda: nc.vector.tensor_tensor_scan(out=dst.t[:, 0:n], data0=self._ones_n(n),
                                                               data1=src.t[:, 0:n], initial=0.0, op0=ALU.mult, op1=ALU.add),
                   reads=[src.b, self.onesL.b], writes=[dst.b])

    def _ones_n(self, n):
        return self.onesL.t[:, 0:n]

    def phase_attn(self, l):
        self.phase_begin()
        lam_init = 0.8 - 0.6 * math.exp(-0.3 * l)
        lt = self.sb("lamraw", [128, 256])
        self.dma(lt.t[:], self.da_lam[l].partition_broadcast(128), writes=[lt.b])
        lp = self.sb("lamprod", [128, 2, 64])
        self.tt("dve", lp.t[:, 0, :], lt.t[:, 0:64], lt.t[:, 64:128], ALU.mult, [lt.b], [lp.b])
        self.tt("dve", lp.t[:, 1, :], lt.t[:, 128:192], lt.t[:, 192:256], ALU.mult, [lt.b], [lp.b])
        ls = self.sb("lamsum", [128, 2])
        nc = self.nc
        self.P.add("dve", lambda: nc.vector.reduce_sum(out=ls.t[:], in_=lp.t[:], axis=AX.X), reads=[lp.b], writes=[ls.b])
        le = self.sb("lamexp", [128, 2])
        self.act(le.t[:], ls.t[:], AF.Exp, [ls.b], [le.b])
        neglam = self.sb("neglam", [128, 1])
        self.tt("dve", neglam.t[:], le.t[:, 1:2], le.t[:, 0:1], ALU.subtract, [le.b], [neglam.b])
        self.ts("dve", neglam.t[:], neglam.t[:], -lam_init, None, ALU.add, None, [neglam.b], [neglam.b])
        won = self.sb("won", [128, 1])
        self.dma(won.t[:], self.da_on[l], writes=[won.b])
        self.ts("dve", won.t[:], won.t[:], 1.0 - lam_init, None, ALU.mult, None, [won.b], [won.b])
        onesb = self.sb("onesb", [128, 128], BF16)
        self.memset("pool", onesb.t[:], 1.0, [onesb.b])
        maxk = max((T + (256 if kind == "S" else 0)) for kind, off, T in self.seqs)
        KT = self.sb("KT", [128, maxk], BF16)
        VA = self.sb("VA", [128, maxk // 128, 512], BF16)
        ckf = self.sb("ckf", [128, 256])
        cvf = self.sb("cvf", [128, 2, 512])
        Qt = [self.sb("Qt%d" % i, [128, 512], BF16) for i in range(2)]
        Et = [self.sb("Et%d" % i, [128, 512], BF16) for i in range(4)]
        rl = self.sb("rl", [128, 512])
        Esum = self.sb("Esum", [128, 512])
        tm = [self.sb("tm%d" % i, [128, 512]) for i in range(2)]
        of = self.sb("of", [128, 512])
        osq = self.sb("osq", [128, 512], BF16)
        ors = self.sb("ors", [128, 512])
        ob = [self.sb("ob%d" % i, [128, 512], BF16) for i in range(2)]
        Sps = [self.ps[0], self.ps[1], self.ps[7]]
        Ops = [self.ps[2], self.ps[3]]
        Lps = [self.ps[4], self.ps[5]]
        Mps = self.ps[6]
        cnt = {"s": 0, "e": 0, "q": 0, "o": 0}
        for (kind, off, T) in self.seqs:
            nkc0 = 2 if kind == "S" else 0
            nkc = nkc0 + T // 128
            if kind == "S":
                self.dma(cvf.t[:], self.cacheV[l].rearrange("k p e -> p k e"), writes=[cvf.b])
                self.cp("pool", VA.t[:, 0:2, :], cvf.t[:], [cvf.b], [VA.b])
            for g0 in range(0, T // 128, 4):
                n = min(4, T // 128 - g0)
                self.dma(VA.t[:, nkc0 + g0:nkc0 + g0 + n, :],
                         self.Vs[off + g0 * 128:off + (g0 + n) * 128, :].rearrange("(k p) e -> p k e", p=128), writes=[VA.b])
            for hb in range(4):
                if kind == "S":
                    self.dma(ckf.t[:], self.cacheKT[l, hb], writes=[ckf.b])
                    self.cp("pool", KT.t[:, 0:256], ckf.t[:], [ckf.b], [KT.b])
                t = 0
                while t < T:
                    ci = (off + t) // 512
                    o_in = (off + t) % 512
                    n = min(512 - o_in, T - t)
                    self.dma(KT.t[:, nkc0 * 128 + t:nkc0 * 128 + t + n], self.Ks[hb, ci, :, o_in:o_in + n], writes=[KT.b])
                    t += n
                qlen = min(512, T)
                for q0 in range(0, T, qlen):
                    ci = (off + q0) // 512
                    o_in = (off + q0) % 512
                    Q = Qt[cnt["q"] % 2]; cnt["q"] += 1
                    self.dma(Q.t[:, 0:qlen], self.Qs[hb, ci, :, o_in:o_in + qlen], writes=[Q.b])
                    for m in range(2):
                        Op = Ops[m]
                        Lp = Lps[m]
                        def s_stage(kc_):
                            Sp_ = Sps[cnt["s"] % 3]; cnt["s"] += 1
                            self.mm(Sp_.t[:, 0:qlen], KT.t[64 * m:64 * m + 64, kc_ * 128:(kc_ + 1) * 128],
                                    Q.t[64 * m:64 * m + 64, 0:qlen], True, True, [KT.b, Q.b], [Sp_.b])
                            E_ = Et[cnt["e"] % 4]; cnt["e"] += 1
                            self.act(E_.t[:, 0:qlen], Sp_.t[:, 0:qlen], AF.Exp, [Sp_.b], [E_.b], scale=0.125)
                            return E_

                        Eq = [s_stage(0)]
                        if nkc > 1:
                            Eq.append(s_stage(1))
                        for kc in range(nkc):
                            if kc + 2 < nkc:
                                Eq.append(s_stage(kc + 2))
                            E = Eq.pop(0)
                            self.mm(Op.t[:, 0:qlen], VA.t[:, kc, hb * 128:(hb + 1) * 128], E.t[:, 0:qlen],
                                    kc == 0, kc == nkc - 1, [VA.b, E.b], [Op.b])
                            if kc == 0:
                                self.cp("dve", Esum.t[:, 0:qlen], E.t[:, 0:qlen], [E.b], [Esum.b])
                            else:
                                self.tt("dve", Esum.t[:, 0:qlen], Esum.t[:, 0:qlen], E.t[:, 0:qlen], ALU.add, [Esum.b, E.b], [Esum.b])
                        self.mm(Lp.t[:, 0:qlen], self.onesf.t[:, 0:128], Esum.t[:, 0:qlen], True, True, [self.onesf.b, Esum.b], [Lp.b])
                        self.recip(rl.t[:, 0:qlen], Lp.t[:, 0:qlen], [Lp.b], [rl.b])
                        self.tt("dve", tm[m].t[:, 0:qlen], Op.t[:, 0:qlen], rl.t[:, 0:qlen], ALU.mult, [Op.b, rl.b], [tm[m].b])
                    self.stt(of.t[:, 0:qlen], tm[1].t[:, 0:qlen], neglam.t[:, 0:1], tm[0].t[:, 0:qlen], ALU.mult, ALU.add,
                             [tm[0].b, tm[1].b, neglam.b], [of.b])
                    self.act(osq.t[:, 0:qlen], of.t[:, 0:qlen], AF.Square, [of.b], [osq.b])
                    self.mm(Mps.t[:, 0:qlen], self.ones128, osq.t[:, 0:qlen], True, True, [osq.b, self.cstb.b], [Mps.b])
                    self.act(ors.t[:, 0:qlen], Mps.t[:, 0:qlen], AF.Sqrt, [Mps.b], [ors.b], bias=self.eps_t.t[:, 0:1], scale=1.0)
                    self.recip(ors.t[:, 0:qlen], ors.t[:, 0:qlen], [ors.b], [ors.b])
                    o_ = ob[cnt["o"] % 2]; cnt["o"] += 1
                    self.stt(o_.t[:, 0:qlen], of.t[:, 0:qlen], won.t[:, 0:1], ors.t[:, 0:qlen], ALU.mult, ALU.mult,
                             [of.b, won.b, ors.b], [o_.b])
                    self.dma(self.OUTs[0][hb, ci, :, o_in:o_in + qlen], o_.t[:, 0:qlen], reads=[o_.b])
        self.phase_end()

    def phase_C1(self, l, src_ap):
        self.phase_begin()
        xins = [self.sb("xin%d" % i, [128, NJ, 512]) for i in range(2)]
        sq = self.sb("sq", [128, NJ, 512], BF16)
        rstd = self.sb("rstd", [128, 512])
        h2 = self.sb("h2", [128, NJ, 512], BF16)
        oin = self.sb("oin", [128, 12, 512], BF16)
        mg = self.sb("mg", [128, NJ, 512], BF16)
        gts = [self.sb("gt%d" % i, [128, 3, 512], BF16) for i in range(2)]
        m1 = [self.sb("m1_%d" % i, [128, 512]) for i in range(3)]
        wu_f = [self.sb("wuf%d" % i, [128, 12, 128]) for i in range(2)]
        wu_b = [self.sb("wub%d" % i, [128, 12, 128], BF16) for i in range(2)]
        wo_f = [self.sb("wof%d" % i, [128, 16, 128]) for i in range(2)]
        wo_b = [self.sb("wob%d" % i, [128, 16, 128], BF16) for i in range(2)]
        ups = [self.ps[0], self.ps[1], self.ps[2], self.ps[3], self.ps[4], self.ps[5]]
        normps = self.ps[6]
        ops_ = [self.ps[7], self.ps[6]]
        k = 0
        for ci in range(self.NCH):
            row = self.chunk_row[ci]
            xin = xins[ci % 2]
            if ci == 0:
                self.dma(xin.t[:], src_ap[ci], writes=[xin.b])
                for n in range(3):
                    self.dma(oin.t[:, n * 4:(n + 1) * 4, :], self.OUTs[n][:, ci].rearrange("c p t -> p c t"), writes=[oin.b])
            for ob in range(16):
                wf = wu_f[ob % 2]; wb = wu_b[ob % 2]
                self.dma(wf.t[:], self.w_up[l, ob], writes=[wf.b])
                self.cp("dve", wb.t[:], wf.t[:], [wf.b], [wb.b])
                gt = gts[ob % 2]
                self.dma(gt.t[:], self.Gs[ob:48:16, ci].rearrange("n p t -> p n t"), writes=[gt.b])
                for n in range(3):
                    pb = ups[(ob % 2) * 3 + n]
                    for kj in range(4):
                        self.mm(pb.t[:], wb.t[:, n * 4 + kj, :], oin.t[:, n * 4 + kj, :], kj == 0, kj == 3, [wb.b, oin.b], [pb.b])
                    self.tt("dve", m1[n].t[:], pb.t[:], gt.t[:, n, :], ALU.mult, [pb.b, gt.b], [m1[n].b])
                self.tt("dve", m1[0].t[:], m1[0].t[:], m1[1].t[:], ALU.add, [m1[0].b, m1[1].b], [m1[0].b])
                self.tt("pool", mg.t[:, ob, :], m1[0].t[:], m1[2].t[:], ALU.add, [m1[0].b, m1[2].b], [mg.b])
            if ci + 1 < self.NCH:
                xn = xins[(ci + 1) % 2]
                self.dma(xn.t[:], src_ap[ci + 1], writes=[xn.b])
                for n in range(3):
                    self.dma(oin.t[:, n * 4:(n + 1) * 4, :], self.OUTs[n][:, ci + 1].rearrange("c p t -> p c t"), writes=[oin.b])
            for ob in range(16):
                wf = wo_f[ob % 2]; wb = wo_b[ob % 2]
                self.dma(wf.t[:], self.w_out[l, ob], writes=[wf.b])
                self.cp("act", wb.t[:], wf.t[:], [wf.b], [wb.b])
                pb = ops_[ob % 2]
                for kj in range(NJ):
                    self.mm(pb.t[:], wb.t[:, kj, :], mg.t[:, kj, :], kj == 0, kj == NJ - 1, [wb.b, mg.b], [pb.b])
                self.stt(xin.t[:, ob, :], pb.t[:], self.modT.t[:, 32 + ob, row:row + 1], xin.t[:, ob, :], ALU.mult, ALU.add,
                         [pb.b, self.modT.b, xin.b], [xin.b])
            self.dma(self.X1s[ci], xin.t[:], reads=[xin.b])
            self.norm_core(xin, h2, 0, row, 1, 48, sq, rstd, normps)
            self.dma(self.H2s[ci], h2.t[:], reads=[h2.b])
        self.phase_end()

    def phase_C2(self, l, dst_ap):
        self.phase_begin()
        SC = 2
        h2 = self.sb("h2", [128, NJ, SC * 512], BF16)
        act = self.sb("actT", [128, 44, SC * 512], BF16)
        wf = [self.sb("wf%d" % i, [128, 44, 128]) for i in range(2)]
        wb = [self.sb("wb%d" % i, [128, 44, 128], BF16) for i in range(2)]
        sg = [self.sb("sg%d" % i, [128, 512]) for i in range(2)]
        x1 = [self.sb("x1_%d" % i, [128, 512]) for i in range(3)]
        gps = [self.ps[0], self.ps[1]]
        ups = [self.ps[2], self.ps[3]]
        ops_ = [self.ps[4], self.ps[5], self.ps[6]]
        k = 0
        kx = 0
        for sc0 in range(0, self.NCH, SC):
            chunks = list(range(sc0, min(self.NCH, sc0 + SC)))
            for hs, ci in enumerate(chunks):
                self.dma(h2.t[:, :, hs * 512:(hs + 1) * 512], self.H2s[ci], writes=[h2.b])
            for fb in range(44):
                f = wf[fb % 2]; b = wb[fb % 2]
                self.dma(f.t[:, 0:32, :], self.w_ffn_in[l, fb], writes=[f.b])
                self.cp("dve" if fb % 2 == 0 else "act", b.t[:, 0:32, :], f.t[:, 0:32, :], [f.b], [b.b])
                for hs, ci in enumerate(chunks):
                    gp = gps[k % 2]; up = ups[k % 2]; s_ = sg[k % 2]; k += 1
                    for kj in range(NJ):
                        self.mm(gp.t[:], b.t[:, kj, :], h2.t[:, kj, hs * 512:(hs + 1) * 512], kj == 0, kj == NJ - 1, [b.b, h2.b], [gp.b])
                    for kj in range(NJ):
                        self.mm(up.t[:], b.t[:, 16 + kj, :], h2.t[:, kj, hs * 512:(hs + 1) * 512], kj == 0, kj == NJ - 1, [b.b, h2.b], [up.b])
                    self.act(s_.t[:], gp.t[:], AF.Silu, [gp.b], [s_.b])
                    self.tt("dve", act.t[:, fb, hs * 512:(hs + 1) * 512], up.t[:], s_.t[:], ALU.mult, [up.b, s_.b], [act.b])
            def fo_load(ob_):
                self.dma(wf[ob_ % 2].t[:], self.w_ffn_out[l, ob_], writes=[wf[ob_ % 2].b])

            def fo_conv(ob_):
                self.cp("dve" if ob_ % 2 == 0 else "act", wb[ob_ % 2].t[:], wf[ob_ % 2].t[:], [wf[ob_ % 2].b], [wb[ob_ % 2].b])

            fo_load(0)
            fo_conv(0)
            for ob in range(16):
                f = wf[ob % 2]; b = wb[ob % 2]
                if ob + 1 < 16:
                    fo_load(ob + 1)
                    fo_conv(ob + 1)
                for hs, ci in enumerate(chunks):
                    row = self.chunk_row[ci]
                    pb = ops_[kx % 3]; xt = x1[kx % 3]; kx += 1
                    self.dma(xt.t[:], self.X1s[ci, :, ob, :], writes=[xt.b])
                    for kj in range(44):
                        self.mm(pb.t[:], b.t[:, kj, :], act.t[:, kj, hs * 512:(hs + 1) * 512], kj == 0, kj == 43, [b.b, act.b], [pb.b])
                    self.stt(xt.t[:], pb.t[:], self.modT.t[:, 80 + ob, row:row + 1], xt.t[:], ALU.mult, ALU.add,
                             [pb.b, self.modT.b, xt.b], [xt.b])
                    self.dma(dst_ap[ci, :, ob, :], xt.t[:], reads=[xt.b])
        self.phase_end()

    def build(self):
        self.declare_io()
        self.load_consts()
        self.eps_t = self.sb("eps", [128, 2], persistent=True)
        self.memset("pool", self.eps_t.t[:, 0:1], RMS_EPS, [self.eps_t.b])
        self.memset("pool", self.eps_t.t[:, 1:2], GN_EPS, [self.eps_t.b])
        self.modT_l = [self.sb("modT%d" % l, [128, 96, 2], persistent=True) for l in range(self.L)]
        self.modA_l = [self.sb("modA%d" % l, [128, 2, NJ, 2], persistent=True) for l in range(self.L)]
        stop = self.cfg.get("stop")
        for l in range(self.L):
            self.phase_mod(l)
            if stop == "mod":
                o = self.dram_out("dbg_mod", [128, 192])
                self.dma(o[:, :], self.modT_l[l].t[:].rearrange("p c r -> p (c r)"), reads=[self.modT_l[l].b], writes=[self.dbufs["dbg_mod"]])
                o2 = self.dram_out("dbg_A", [128, 64])
                self.dma(o2[:, :], self.modA_l[l].t[:].rearrange("p a j r -> p (a j r)"), reads=[self.modA_l[l].b], writes=[self.dbufs["dbg_A"]])
                break
            src = self.xL if l == 0 else self.Xs[l - 1]
            dst = self.o_yL if l == self.L - 1 else self.Xs[l]
            self.phase_A(l, src)
            if stop == "A":
                break
            self.phase_attn(l)
            stub = self.cfg.get("stub", "none")
            if stub == "zero":
                self.phase_zero_branch(1)
            else:
                self.phase_s5(l)
            if stub in ("zero", "s5"):
                self.phase_zero_branch(2)
            else:
                self.phase_rwkv(l)
            self.phase_C1(l, src)
            self.phase_C2(l, dst)
        n = self.P.emit()
        return n


def _consts():
    c = np.zeros((128, 512), np.float32)
    c[:, 0:128] = np.eye(128, dtype=np.float32)
    bo = np.zeros((128, 128), np.float32)
    bo[0:64, 0:64] = 1.0 / 64
    bo[64:128, 64:128] = 1.0 / 64
    c[:, 128:256] = bo
    c[:, 256:384] = 1.0 / 128
    PT = np.zeros((128, 128), np.float32)
    for m in range(128):
        if m % 32 < 16:
            PT[m + 16, m] = -1.0
        else:
            PT[m - 16, m] = 1.0
    c[:, 384:512] = PT
    return c


def _rope_tables():
    t = np.arange(4096)
    row = (t // 64).astype(np.float32)
    col = (t % 64).astype(np.float32)
    n_freq = 16
    inv_freq = (10000.0 ** (-np.arange(n_freq, dtype=np.float32) / n_freq)).astype(np.float32)
    C = np.zeros((128, 4096), np.float32)
    S = np.zeros((128, 4096), np.float32)
    for p in range(128):
        d = p % 64
        pos = row if d < 32 else col
        ang = (pos * inv_freq[d % 16]).astype(np.float32)
        C[p] = np.cos(ang)
        S[p] = np.sin(ang)
    return C, S


def _st_layout(a):
    sh = a.shape[:-2]
    k = len(sh)
    return a.reshape(sh + (16, 2, 64)).transpose(tuple(range(k)) + (k + 1, k + 2, k)).reshape(sh + (128, 16))


def per_core_state(inp, b, m):
    f = np.float32
    ck = np.asarray(inp["cache_attn_kv"][b], dtype=f)
    m["cacheKT"] = np.ascontiguousarray(ck[:, :, 0].transpose(0, 2, 3, 1))
    m["cacheV"] = np.ascontiguousarray(ck[:, :, 1].reshape(DEPTH, 2, 128, 512))
    h0 = np.asarray(inp["state_s5"][b], dtype=f)
    m["s5h0"] = np.ascontiguousarray(_st_layout(h0).transpose(0, 3, 1, 2, 4))
    s0 = np.asarray(inp["state_rwkv"][b], dtype=f)
    m["rwS0"] = np.ascontiguousarray(s0.transpose(0, 1, 2, 4, 3))


PER_CORE_KEYS = ("xL", "cT", "cacheKT", "cacheV", "s5h0", "rwS0")


def make_in_map(inp, core, seqs_spec, shared=None):
    f = np.float32
    if shared is not None:
        m = {k: v for k, v in shared.items() if k not in PER_CORE_KEYS}
        xs = [np.asarray(inp["x_sample"][b] if kind == "S" else inp["x_prompt"][b]).T for kind, b in seqs_spec]
        xT = np.concatenate(xs, axis=1)
        nch = xT.shape[1] // 512
        m["xL"] = np.ascontiguousarray(xT.reshape(NJ, 128, nch, 512).transpose(2, 1, 0, 3), dtype=f)
        sb = [b for k, b in seqs_spec if k == "S"]
        cb = np.asarray(inp["c"][sb[0]] if sb else inp["c"][0])
        m["cT"] = np.ascontiguousarray(np.stack([cb, np.asarray(inp["c_ctx"])], axis=1).reshape(NJ, 128, 2).transpose(1, 0, 2), dtype=f)
        if sb:
            per_core_state(inp, sb[0], m)
        return m
    xs = []
    for kind, b in seqs_spec:
        xs.append(np.asarray(inp["x_sample"][b] if kind == "S" else inp["x_prompt"][b]).T)
    m = {}
    xT = np.concatenate(xs, axis=1)
    nch = xT.shape[1] // 512
    m["xL"] = np.ascontiguousarray(xT.reshape(NJ, 128, nch, 512).transpose(2, 1, 0, 3), dtype=f)
    sb = [b for k, b in seqs_spec if k == "S"]
    cb = np.asarray(inp["c"][sb[0]] if sb else inp["c"][0])
    m["cT"] = np.ascontiguousarray(np.stack([cb, np.asarray(inp["c_ctx"])], axis=1).reshape(NJ, 128, 2).transpose(1, 0, 2), dtype=f)
    m["w_mod"] = np.ascontiguousarray(
        np.asarray(inp["w_mod"], dtype=f).reshape(DEPTH, NJ, 128, 24, 512).transpose(0, 3, 2, 1, 4))
    m["bmodT"] = np.ascontiguousarray(np.asarray(inp["b_mod"]).reshape(DEPTH, 96, 128).transpose(0, 2, 1), dtype=f)
    m["nmixT"] = np.ascontiguousarray(np.asarray(inp["norm_mix"]).reshape(DEPTH, NJ, 128).transpose(0, 2, 1), dtype=f)
    m["nffnT"] = np.ascontiguousarray(np.asarray(inp["norm_ffn"]).reshape(DEPTH, NJ, 128).transpose(0, 2, 1), dtype=f)
    wi = np.zeros((DEPTH, D_MODEL, 40 * 256), f)
    wi[:, :, :N_IN] = np.asarray(inp["w_in"], dtype=f)
    m["w_in"] = np.ascontiguousarray(wi.reshape(DEPTH, NJ, 128, 40, 256).transpose(0, 3, 2, 1, 4))
    qw = np.tile(np.asarray(inp["da_q_norm"]), (1, 2))
    kw = np.tile(np.asarray(inp["da_k_norm"]), (1, 2))
    m["qkw"] = np.ascontiguousarray(np.stack([qw, kw], axis=2), dtype=f)
    wu = np.asarray(inp["w_up"], dtype=f)
    m["w_up"] = np.ascontiguousarray(wu.reshape(DEPTH, 3, 4, 128, 16, 128).transpose(0, 4, 3, 1, 2, 5).reshape(DEPTH, 16, 128, 12, 128))
    wo = np.asarray(inp["w_out"], dtype=f)
    m["w_out"] = np.ascontiguousarray(wo.reshape(DEPTH, 16, 128, 16, 128).transpose(0, 3, 2, 1, 4))
    wfi = np.asarray(inp["w_ffn_in"], dtype=f)
    m["w_ffn_in"] = np.ascontiguousarray(
        wfi.reshape(DEPTH, 16, 128, 2, 44, 128).transpose(0, 4, 2, 3, 1, 5).reshape(DEPTH, 44, 128, 32, 128))
    wfo = np.asarray(inp["w_ffn_out"], dtype=f)
    m["w_ffn_out"] = np.ascontiguousarray(wfo.reshape(DEPTH, 44, 128, 16, 128).transpose(0, 3, 2, 1, 4))
    m["da_lam"] = np.ascontiguousarray(np.asarray(inp["da_lambda"], dtype=f).reshape(DEPTH, 1, 256))
    m["da_on"] = np.ascontiguousarray(np.asarray(inp["da_out_norm"], dtype=f).reshape(DEPTH, 128, 1))
    if sb:
        per_core_state(inp, sb[0], m)
    def st_layout(a):
        sh = a.shape[:-2]
        return a.reshape(sh + (16, 2, 64)).transpose(tuple(range(len(sh))) + (len(sh) + 1, len(sh) + 2, len(sh))).reshape(sh + (128, 16))
    lre = st_layout(np.asarray(inp["s5_lam_re"], dtype=f))
    lim = st_layout(np.asarray(inp["s5_lam_im"], dtype=f))
    lst = st_layout(np.broadcast_to(np.asarray(inp["s5_log_step"], dtype=f)[..., None], (DEPTH, 2, 32, 64)))
    m["s5p"] = np.ascontiguousarray(np.stack([lre, lim, lst], axis=3))
    def bc_layout(a):
        return a.reshape(DEPTH, 2, 16, 2, 64, 16).transpose(0, 1, 3, 4, 2, 5).reshape(DEPTH, 2, 128, 16, 16)
    bre = bc_layout(np.asarray(inp["s5_b_re"], dtype=f)); bim = bc_layout(np.asarray(inp["s5_b_im"], dtype=f))
    m["s5b"] = np.ascontiguousarray(np.stack([bre, bim], axis=3))
    cre = bc_layout(np.asarray(inp["s5_c_re"], dtype=f).transpose(0, 1, 2, 4, 3))
    cim = bc_layout(np.asarray(inp["s5_c_im"], dtype=f).transpose(0, 1, 2, 4, 3))
    m["s5c"] = np.ascontiguousarray(np.stack([cre, cim], axis=3))
    m["s5d"] = np.ascontiguousarray(np.asarray(inp["s5_d"], dtype=f).reshape(DEPTH, 4, 128).transpose(0, 2, 1))
    m["s5glu"] = np.ascontiguousarray(np.asarray(inp["s5_w_glu"], dtype=f).reshape(DEPTH, 4, 128, 512).transpose(0, 2, 1, 3))
    def hp_layout(a):
        return np.asarray(a, dtype=f).reshape(DEPTH, 4, 128).transpose(0, 2, 1)
    mu = np.asarray(inp["rw_mu"], dtype=f)
    vec = np.stack([hp_layout(mu[:, 0:512]), hp_layout(mu[:, 512:1024]), hp_layout(mu[:, 1024:1536]),
                    hp_layout(inp["rw_k_k"]), hp_layout(inp["rw_k_a"]), hp_layout(np.asarray(inp["rw_r_k"]).reshape(DEPTH, 512)),
                    hp_layout(inp["rw_ln_w"]), hp_layout(inp["rw_ln_b"]),
                    hp_layout(inp["rw_ln_b"]), hp_layout(inp["rw_ln_b"])], axis=3)
    m["rw_vec"] = np.ascontiguousarray(vec)
    w0 = np.asarray(inp["rw_w0"], dtype=f).reshape(DEPTH, 2, 4, 128).transpose(0, 3, 2, 1)
    a0 = np.asarray(inp["rw_a0"], dtype=f).reshape(DEPTH, 2, 4, 128).transpose(0, 3, 2, 1)
    m["rw_wa"] = np.ascontiguousarray(np.stack([w0, a0], axis=4))
    m["rw_mu3"] = np.ascontiguousarray(mu[:, 1536:1920].reshape(DEPTH, 3, 128).transpose(0, 2, 1))
    m["rw_w2"] = np.ascontiguousarray(np.asarray(inp["rw_w2"], dtype=f).reshape(DEPTH, 128, 512))
    m["rw_a2"] = np.ascontiguousarray(np.asarray(inp["rw_a2"], dtype=f).reshape(DEPTH, 128, 512))
    m["rw_g2"] = np.ascontiguousarray(np.asarray(inp["rw_g2"], dtype=f))
    ii = np.arange(128)[:, None]; tt_ = np.arange(128)[None, :]
    same = (ii // 64) == (tt_ // 64)
    mk = np.stack([(ii < tt_) & same, (ii <= tt_) & same, (ii > tt_) & same, (ii >= tt_) & same], axis=1).astype(f)
    m["rw_masks"] = np.ascontiguousarray(mk)
    m["rw_c2"] = np.ascontiguousarray(np.stack([same.astype(f), same.astype(f) / 64.0], axis=1))
    m["consts"] = _consts()
    C, S = _rope_tables()
    m["ropeC"] = C
    m["ropeS"] = S
    return m


FULL_SEQS = [("S", 0, 4096), ("P", 4096, 256), ("P", 4352, 256)]


def assemble(results, n_cores=8):
    f = np.float32
    y_prompt = np.zeros((16, 256, D_MODEL), f)
    y_sample = np.zeros((8, 4096, D_MODEL), f)
    new_kv = np.zeros((16, DEPTH, 256, 2, 4, 128), f)
    new_s5 = np.zeros((16, DEPTH, 2, 2, 32, 64), f)
    new_rw = np.zeros((16, DEPTH, 2, 8, 64, 64), f)
    for c in range(n_cores):
        r = results[c]
        yL = np.asarray(r["yL"])
        y = yL.transpose(0, 3, 2, 1).reshape(-1, D_MODEL)
        y_sample[c] = y[:4096]
        kT = np.asarray(r["kT"]); vt = np.asarray(r["vtok"])
        s5f = np.asarray(r["s5f"]); rwf = np.asarray(r["rwf"])
        for i in range(2):
            b = 2 * c + i
            y_prompt[b] = y[4096 + i * 256:4096 + (i + 1) * 256]
            for l in range(DEPTH):
                new_kv[b, l, :, 0] = kT[l, :, i * 256:(i + 1) * 256].T.reshape(256, 4, 128)
                new_kv[b, l, :, 1] = vt[l, i * 256:(i + 1) * 256].reshape(256, 4, 128)
                g = s5f[l][:, i].reshape(2, 64, 2, 2, 16)
                new_s5[b, l] = g.transpose(2, 3, 4, 0, 1).reshape(2, 2, 32, 64)
                new_rw[b, l] = rwf[l, i].transpose(0, 1, 3, 2)
    return (y_prompt, y_sample, new_kv, new_s5, new_rw)


_SHARED = {}


def kernel(**inputs):
    inp = {k: np.asarray(v) for k, v in inputs.items()}
    cfg = {"seqs": FULL_SEQS, "layers": DEPTH}
    B = Builder(cfg)
    B.build()
    in_maps = []
    shared = None
    for c in range(8):
        m = make_in_map(inp, c, [("S", c), ("P", 2 * c), ("P", 2 * c + 1)], shared)
        if shared is None:
            shared = m
        in_maps.append(m)
    res = run_bass_kernel_spmd(B.nc, in_maps, core_ids=list(range(8)))
    return assemble(res.results)
```
